# Optimizing a Trainium2 kernel written in Bass

```python
import math
import jax, jax.numpy as jnp
from jax import lax
import numpy as np

D_MODEL = 1024
BATCH = 8
SEQ = 4096
DEPTH = 4

GRID_W = 64
NUM_MIXERS = 2
HEAD_DIM = 64
MIX_WIDTH = D_MODEL
A_HEADS = MIX_WIDTH // HEAD_DIM
A_WIN_ROWS = 8
A_WIN_COLS = 16
B_Q_HEADS = MIX_WIDTH // HEAD_DIM
B_KV_HEADS = 4
B_QBLK = 128
ROPE_THETA = 10000.0
ROPE_SECTION = HEAD_DIM // 2
PLE_DIM = 256
NORM_EPS = 1e-6

kernel_name = "hybrid_natten_gqa_axial_rope_encoder"


def rms_norm(x, g):
    xf = x.astype(jnp.float32)
    y = xf * lax.rsqrt(jnp.mean(xf * xf, axis=-1, keepdims=True) + NORM_EPS)
    return (y * g.astype(jnp.float32)).astype(x.dtype)


def _na_indices(rows):
    win_r = min(A_WIN_ROWS, rows)
    win_c = A_WIN_COLS
    r = np.arange(rows)
    c = np.arange(GRID_W)
    rs = np.clip(r - win_r // 2, 0, rows - win_r)
    cs = np.clip(c - win_c // 2, 0, GRID_W - win_c)
    key_r = rs[:, None] + np.arange(win_r)[None, :]
    key_c = cs[:, None] + np.arange(win_c)[None, :]
    idx = key_r[:, None, :, None] * GRID_W + key_c[None, :, None, :]
    idx = idx.reshape(rows, GRID_W, win_r * win_c).astype(np.int32)
    dr = (key_r - r[:, None] + (A_WIN_ROWS - 1)).astype(np.int32)
    dc = (key_c - c[:, None] + (A_WIN_COLS - 1)).astype(np.int32)
    return idx, dr, dc


def mixer_a(h, w_in, rpb, w_out):
    b, s, _ = h.shape
    rows = s // GRID_W
    idx_np, dr, dc = _na_indices(rows)
    nk = idx_np.shape[-1]
    proj = h @ w_in
    q, k, v, gate = jnp.split(proj, 4, axis=-1)
    to_heads = lambda t: t.reshape(b, s, A_HEADS, HEAD_DIM).transpose(0, 2, 1, 3)
    q, k, v = to_heads(q), to_heads(k), to_heads(v)
    bias = rpb[:, dr[:, None, :, None], dc[None, :, None, :]]
    bias = bias.reshape(A_HEADS, rows, GRID_W, nk).transpose(1, 0, 2, 3)
    q_blk = q.reshape(b, A_HEADS, rows, GRID_W, HEAD_DIM).transpose(2, 0, 1, 3, 4)
    idx = jnp.asarray(idx_np)
    scale = 1.0 / math.sqrt(HEAD_DIM)

    def row_block(args):
        qb, ib, bb = args
        flat = ib.reshape(-1)
        kg = jnp.take(k, flat, axis=2).reshape(b, A_HEADS, GRID_W, nk, HEAD_DIM)
        vg = jnp.take(v, flat, axis=2).reshape(b, A_HEADS, GRID_W, nk, HEAD_DIM)
        sc = jnp.einsum('bhqd,bhqnd->bhqn', qb, kg).astype(jnp.float32) * scale
        sc = sc + bb.astype(jnp.float32)[None]
        pr = jax.nn.softmax(sc, axis=-1).astype(vg.dtype)
        return jnp.einsum('bhqn,bhqnd->bhqd', pr, vg)

    o = lax.map(row_block, (q_blk, idx, bias))
    o = o.transpose(1, 0, 3, 2, 4).reshape(b, s, MIX_WIDTH)
    return (o * jax.nn.silu(gate)) @ w_out


def _axial_rope(seq):
    t = jnp.arange(seq)
    row = (t // GRID_W).astype(jnp.float32)
    col = (t % GRID_W).astype(jnp.float32)
    n_freq = ROPE_SECTION // 2
    inv = jnp.power(ROPE_THETA, -jnp.arange(n_freq, dtype=jnp.float32) * 2.0 / ROPE_SECTION)
    ang_r = row[:, None] * inv[None, :]
    ang_c = col[:, None] * inv[None, :]
    return jnp.cos(ang_r), jnp.sin(ang_r), jnp.cos(ang_c), jnp.sin(ang_c)


def _rotate(x, cos, sin):
    x1, x2 = jnp.split(x, 2, axis=-1)
    return jnp.concatenate([x1 * cos - x2 * sin, x2 * cos + x1 * sin], axis=-1)


def _apply_axial_rope(x, tabs):
    cos_r, sin_r, cos_c, sin_c = tabs
    xf = x.astype(jnp.float32)
    xr = _rotate(xf[..., :ROPE_SECTION], cos_r, sin_r)
    xc = _rotate(xf[..., ROPE_SECTION:], cos_c, sin_c)
    return jnp.concatenate([xr, xc], axis=-1).astype(x.dtype)


def mixer_b(h, w_in, q_norm_g, k_norm_g, w_out):
    b, s, _ = h.shape
    group = B_Q_HEADS // B_KV_HEADS
    kv_w = B_KV_HEADS * HEAD_DIM
    proj = h @ w_in
    q, k, v, gate = jnp.split(proj, [MIX_WIDTH, MIX_WIDTH + kv_w, MIX_WIDTH + 2 * kv_w], axis=-1)
    q = rms_norm(q.reshape(b, s, B_Q_HEADS, HEAD_DIM), q_norm_g).transpose(0, 2, 1, 3)
    k = rms_norm(k.reshape(b, s, B_KV_HEADS, HEAD_DIM), k_norm_g).transpose(0, 2, 1, 3)
    v = v.reshape(b, s, B_KV_HEADS, HEAD_DIM).transpose(0, 2, 1, 3)
    tabs = _axial_rope(s)
    q = _apply_axial_rope(q, tabs)
    k = _apply_axial_rope(k, tabs)
    nb = s // B_QBLK
    q_blk = q.reshape(b, B_KV_HEADS, group, nb, B_QBLK, HEAD_DIM).transpose(3, 0, 1, 2, 4, 5)
    scale = 1.0 / math.sqrt(HEAD_DIM)

    def q_block(qb):
        sc = jnp.einsum('bkgqd,bksd->bkgqs', qb, k).astype(jnp.float32) * scale
        pr = jax.nn.softmax(sc, axis=-1).astype(v.dtype)
        return jnp.einsum('bkgqs,bksd->bkgqd', pr, v)

    o = lax.map(q_block, q_blk)
    o = o.transpose(1, 0, 4, 2, 3, 5).reshape(b, s, MIX_WIDTH)
    return (o * jax.nn.silu(gate)) @ w_out


def setup_inputs(seed: int = 0) -> dict:
    key = jax.random.key(seed)
    ks = jax.random.split(key, 16)
    n_a = (DEPTH + NUM_MIXERS - 1) // NUM_MIXERS
    n_b = DEPTH // NUM_MIXERS
    kv_w = B_KV_HEADS * HEAD_DIM
    b_in_w = 2 * MIX_WIDTH + 2 * kv_w
    nrm = lambda k, shape, fan: jax.random.normal(k, shape, jnp.float32) * (fan ** -0.5)
    gain = lambda k, shape: 1.0 + 0.05 * jax.random.normal(k, shape, jnp.float32)
    return {
        "x": jax.random.normal(ks[0], (BATCH, SEQ, D_MODEL), jnp.float32),
        "p": jax.random.normal(ks[1], (DEPTH, BATCH, SEQ, PLE_DIM), jnp.float32),
        "norm_g": gain(ks[2], (DEPTH, D_MODEL)),
        "a_w_in": nrm(ks[3], (n_a, D_MODEL, 4 * MIX_WIDTH), D_MODEL),
        "a_rpb": 0.1 * jax.random.normal(ks[4], (n_a, A_HEADS, 2 * A_WIN_ROWS - 1, 2 * A_WIN_COLS - 1), jnp.float32),
        "a_w_out": nrm(ks[5], (n_a, MIX_WIDTH, D_MODEL), MIX_WIDTH),
        "b_w_in": nrm(ks[6], (n_b, D_MODEL, b_in_w), D_MODEL),
        "b_q_norm": gain(ks[7], (n_b, HEAD_DIM)),
        "b_k_norm": gain(ks[8], (n_b, HEAD_DIM)),
        "b_w_out": nrm(ks[9], (n_b, MIX_WIDTH, D_MODEL), MIX_WIDTH),
        "ple_norm_g": gain(ks[10], (DEPTH, D_MODEL)),
        "ple_w_gate": nrm(ks[11], (DEPTH, D_MODEL, D_MODEL), D_MODEL),
        "ple_w_proj": nrm(ks[12], (DEPTH, PLE_DIM, D_MODEL), PLE_DIM),
        "final_norm_g": gain(ks[13], (D_MODEL,)),
    }


def reference(x, p, norm_g, a_w_in, a_rpb, a_w_out, b_w_in, b_q_norm, b_k_norm,
              b_w_out, ple_norm_g, ple_w_gate, ple_w_proj, final_norm_g):
    for i in range(DEPTH):
        h = rms_norm(x, norm_g[i])
        j = i // NUM_MIXERS
        if i % NUM_MIXERS == 0:
            y = mixer_a(h, a_w_in[j], a_rpb[j], a_w_out[j])
        else:
            y = mixer_b(h, b_w_in[j], b_q_norm[j], b_k_norm[j], b_w_out[j])
        x = x + y
        g = jax.nn.sigmoid(rms_norm(x, ple_norm_g[i]) @ ple_w_gate[i])
        x = x + g * (p[i] @ ple_w_proj[i])
    return rms_norm(x, final_norm_g)
```

```python
import contextlib
import numpy as np
import ml_dtypes
import concourse.bass as bass
import concourse.mybir as mybir
from concourse.bass_utils import run_bass_kernel_spmd

F32 = mybir.dt.float32
BF16 = mybir.dt.bfloat16
AF = mybir.ActivationFunctionType
ALU = mybir.AluOpType

S = 4096
D = 1024
NT = 8
TT = 512
DEPTH = 4
EPS = 1e-6
NE = 22
NFILL = 1
ENGS = ("pe", "act", "dve", "pool", "sp")


class Res:
    __slots__ = ("name", "w", "r", "sem", "semval")

    def __init__(self, name):
        self.name = name
        self.w = None
        self.r = []
        self.sem = None
        self.semval = 0


class Sched:
    def __init__(self, nc):
        self.nc = nc
        self.ops = {e: [] for e in ENGS}
        self.known = {e: {} for e in ENGS}
        self.dma_sems = []

    def _need(self, eng, tok, waits):
        if tok is None:
            return
        if tok[0] == "e":
            _, f, idx = tok
            if f == eng and eng == "pe":
                return
            key = ("e", f)
            val = idx
        else:
            key = ("s", tok[1])
            val = tok[2]
        kn = self.known[eng]
        if kn.get(key, -1) >= val:
            return
        kn[key] = val
        waits.append(tok)
        if tok[0] == "e":
            self.ops[tok[1]][tok[2]][2] = True

    def _deps(self, eng, reads, writes):
        waits = []
        for r in reads:
            self._need(eng, r.w, waits)
        for w in writes:
            self._need(eng, w.w, waits)
            for t in w.r:
                self._need(eng, t, waits)
        return waits

    def _commit(self, tok, reads, writes):
        for r in reads:
            r.r.append(tok)
        for w in writes:
            w.w = tok
            w.r = []

    def op(self, eng, fn, reads=(), writes=()):
        waits = self._deps(eng, reads, writes)
        tok = ("e", eng, len(self.ops[eng]))
        self.ops[eng].append([fn, waits, False, None])
        self._commit(tok, reads, writes)
        return tok

    def dma(self, eng, out, in_, reads=(), writes=(), sem_res=None):
        waits = self._deps(eng, reads, writes)
        if sem_res is None:
            sem_res = writes[0]
        if sem_res.sem is None:
            sem_res.sem = len(self.dma_sems)
            self.dma_sems.append(sem_res)
        sem_res.semval += 16
        tok = ("s", sem_res.sem, sem_res.semval)

        def fn(e, out=out, in_=in_):
            return e.dma_start(out=out, in_=in_)

        self.ops[eng].append([fn, waits, False, sem_res.sem])
        self._commit(tok, reads, writes)
        return tok

    def barrier(self):
        last = {}
        for f in ENGS:
            for i in range(len(self.ops[f]) - 1, -1, -1):
                o = self.ops[f][i]
                if o[0] is not None and o[3] is None:
                    last[f] = ("e", f, i)
                    break
        for e in ENGS:
            waits = []
            for f in ENGS:
                if f != e and f in last:
                    self._need(e, last[f], waits)
            for r in self.dma_sems:
                self._need(e, ("s", r.sem, r.semval), waits)
            self.ops[e].append([None, waits, False, None])

    def emit(self):
        nc = self.nc
        val = {}
        for e in ENGS:
            c = 0
            for i, o in enumerate(self.ops[e]):
                if o[2]:
                    c += 1
                    val[(e, i)] = c
        with contextlib.ExitStack() as st:
            esem = {e: st.enter_context(nc.semaphore("prog_" + e)) for e in ENGS}
            dsem = [st.enter_context(nc.semaphore("dma_%d" % i)) for i in range(len(self.dma_sems))]
            block = st.enter_context(nc.Block())
            ops = self.ops

            def body(e, h):
                for fn, waits, needed, dinc in ops[e]:
                    for t in waits:
                        if t[0] == "e":
                            h.wait_ge(esem[t[1]], val[(t[1], t[2])])
                        else:
                            h.wait_ge(dsem[t[1]], t[2])
                    if fn is None:
                        continue
                    ins = fn(h)
                    if dinc is not None:
                        ins.then_inc(dsem[dinc], 16)
                    elif needed:
                        ins.then_inc(esem[e], 1)

            @block.tensor
            def _(h):
                body("pe", h)

            @block.scalar
            def _(h):
                body("act", h)

            @block.vector
            def _(h):
                body("dve", h)

            @block.gpsimd
            def _(h):
                body("pool", h)

            @block.sync
            def _(h):
                body("sp", h)


def _rope_tables():
    t = np.arange(S)
    row = (t // 64).astype(np.float32)
    col = (t % 64).astype(np.float32)
    inv = np.power(np.float32(10000.0), -np.arange(16, dtype=np.float32) * np.float32(2.0) / np.float32(32.0)).astype(np.float32)
    cosT = np.zeros((128, S), np.float32)
    sinT = np.zeros((128, S), np.float32)
    perm = np.zeros((128, 128), np.float32)
    for p in range(128):
        d = p % 64
        sec = d // 32
        dd = d % 32
        f = dd % 16
        pos = row if sec == 0 else col
        ang = (pos * inv[f]).astype(np.float32)
        cosT[p] = np.cos(ang)
        sn = np.sin(ang)
        if dd < 16:
            sinT[p] = -sn
            src = p + 16
        else:
            sinT[p] = sn
            src = p - 16
        perm[src, p] = 1.0
    return cosT, sinT, perm


def _a_tables(rpb):
    NEG = np.float32(-30000.0)
    kc = np.arange(64)[:, None]
    c = np.arange(64)[None, :]
    cs = np.clip(c - 8, 0, 48)
    colvalid = (kc >= cs) & (kc < cs + 16)
    dc = np.clip(kc - c + 15, 0, 30)
    out = np.full((16, 2, 64, 2, NE, 64), NEG, np.float32)
    for a in range(2):
        for e in range(NE):
            dr = 17 + a - e
            if 0 <= dr <= 14:
                blk = np.where(colvalid[None], rpb[:, dr][:, dc], NEG)
                out[:, a, :, 1, e, :] = blk
                if 3 <= dr <= 10:
                    out[:, a, :, 0, e, :] = blk
    return np.ascontiguousarray(out.reshape(16, 128, 2 * NE * 64))


def _pc(w):
    K = w.shape[0] // 128
    return np.ascontiguousarray(w.reshape(K, 128, w.shape[1]).transpose(1, 0, 2).reshape(128, K * w.shape[1]))


def _vec_pc(g):
    return np.ascontiguousarray(g.reshape(-1, 128).T)


def a_tiles():
    res = []
    for m in range(8):
        if m == 0:
            js = list(range(0, 6))
        elif m == 7:
            js = list(range(26, 32))
        else:
            js = list(range(4 * m - 2, 4 * m + 6))
        res.append((m, js))
    return res


def a_segments(m, j):
    if m == 0 and j <= 3:
        return [(0, 5, 1), (5, 8, 0)]
    if m == 7 and j >= 28:
        return [(0, 5, 0), (5, 8, 1)]
    return [(0, 8, 0)]


def build(n_layers=DEPTH, dbg=None):
    nc = bass.Bass("TRN2", target_bir_lowering=False)
    s = Sched(nc)

    def din(name, shape, dt=F32, pad=False):
        if pad:
            t = nc.dram_tensor(name, [shape[0] + 1] + list(shape[1:]), dt, kind="ExternalInput").ap()
            return t[0:shape[0]]
        return nc.dram_tensor(name, list(shape), dt, kind="ExternalInput").ap()

    x_in = din("x", [S, D])
    p_in = din("p", [DEPTH, S, 256])
    gv_in = din("gv", [128, 80])
    ident_in = din("ident", [128, 128])
    perm_in = din("perm", [128, 128])
    cos_in = din("cosT", [128, S], pad=True)
    sin_in = din("sinT", [128, S], pad=True)
    a_in = [din("a_in%d" % j, [8, 128, 8 * 512], pad=True) for j in range(2)]
    b_in = [din("b_in%d" % j, [4, 128, 8 * 640], pad=True) for j in range(2)]
    wo_in = [din("wo%d" % i, [128, 8 * 1024], pad=True) for i in range(DEPTH)]
    wg_in = [din("wg%d" % i, [128, 8 * 1024], pad=True) for i in range(DEPTH)]
    wp_in = [din("wp%d" % i, [128, 2 * 1024], pad=True) for i in range(DEPTH)]
    tb_in = [din("tb%d" % j, [16, 128, 2 * NE * 64], pad=True) for j in range(2)]
    out = nc.dram_tensor("out", [S, D], F32, kind="ExternalOutput").ap()
    x_scr = nc.dram_tensor("x_scr", [128, 8, S], F32, kind="Internal").ap()
    o_scr = nc.dram_tensor("o_scr", [128, 8, S], BF16, kind="Internal").ap()
    R_xscr = [Res("xscr%d" % t) for t in range(NT)]
    R_oscr = [Res("oscr%d" % c) for c in range(8)]
    R_out = [Res("out%d" % t) for t in range(NT)]
    dbg_out = None
    if dbg:
        dbg_out = nc.dram_tensor("dbg", [128, 8, S], F32 if dbg[0] == "x" else BF16, kind="ExternalOutput").ap()

    hT = nc.alloc_sbuf_tensor("sb_hT", [128, 8, S], BF16)
    R_h = [[Res("h%d_%d" % (t, c)) for c in range(8)] for t in range(NT)]
    ident = nc.alloc_sbuf_tensor("sb_ident", [128, 128], F32)
    perm = nc.alloc_sbuf_tensor("sb_perm", [128, 128], F32)
    gv = nc.alloc_sbuf_tensor("sb_gv", [128, 80], F32)
    ones_bf = nc.alloc_sbuf_tensor("sb_ones_bf", [128, 128], BF16)
    ones_f = nc.alloc_sbuf_tensor("sb_ones_f", [128, 128], F32)
    bones = nc.alloc_sbuf_tensor("sb_bones", [128, 128], BF16)
    epsc = nc.alloc_sbuf_tensor("sb_epsc", [128, 1], F32)
    R_c = Res("consts")
    ps = nc.alloc_psum_tensor("ps", [128, 8 * 512], F32)
    R_b = [Res("bank%d" % b) for b in range(8)]

    def bank(b, lo=0, hi=512):
        return ps[:, b * 512 + lo:b * 512 + hi]

    s.dma("sp", ident[:, :], ident_in[:, :], writes=[R_c])
    s.dma("sp", perm[:, :], perm_in[:, :], writes=[R_c])
    s.dma("sp", gv[:, :], gv_in[:, :], writes=[R_c])
    s.op("pool", lambda e: e.memset(ones_bf[:, :], 1.0), writes=[R_c])
    s.op("pool", lambda e: e.memset(ones_f[:, :], 1.0), writes=[R_c])
    s.op("pool", lambda e: e.memset(bones[:, :], 0.0), writes=[R_c])
    s.op("pool", lambda e: e.memset(bones[0:64, 0:64], 1.0), writes=[R_c])
    s.op("pool", lambda e: e.memset(bones[64:128, 64:128], 1.0), writes=[R_c])
    s.op("pool", lambda e: e.memset(epsc[:, :], EPS), writes=[R_c])
    s.barrier()

    def g_norm(i):
        return 16 * i

    def g_ple(i):
        return 16 * i + 8

    G_FINAL = 64
    G_BQ = 72
    G_BK = 74

    bank_rr = [0]

    def next_bank():
        b = bank_rr[0]
        bank_rr[0] = (b + 1) % 8
        return b

    def emit_rstd(xt, R_xt, sq, R_sq, sd, R_sd, width_scale):
        s.op("act", lambda e: e.activation(out=sq[:, :, :], in_=xt[:, :, :], func=AF.Square),
             reads=R_xt, writes=[R_sq])
        b = next_bank()

        def mm(e):
            for c in range(8):
                ins = e.matmul(bank(b), ones_bf[:, :], sq[:, c, :], start=(c == 0), stop=(c == 7))
            return ins
        s.op("pe", mm, reads=[R_sq, R_c], writes=[R_b[b]])
        s.op("act", lambda e: e.activation(out=sd[:, :], in_=bank(b), func=AF.Sqrt, scale=width_scale, bias=epsc[:, 0:1]),
             reads=[R_b[b], R_c], writes=[R_sd])
        s.op("dve", lambda e: e.reciprocal(out=sd[:, :], in_=sd[:, :]), reads=[R_sd], writes=[R_sd])

    def emit_h(xt, R_xt, sq, R_sq, sd, R_sd, tt, gcol):
        emit_rstd(xt, R_xt, sq, R_sq, sd, R_sd, 1.0 / D)
        for c in range(8):
            eng = "dve"
            s.op(eng, lambda e, c=c: e.scalar_tensor_tensor(
                out=hT[:, c, tt * TT:(tt + 1) * TT], in0=xt[:, c, :], scalar=gv[:, gcol + c:gcol + c + 1],
                in1=sd[:, :], op0=ALU.mult, op1=ALU.mult),
                reads=[R_xt[c], R_sd, R_c], writes=[R_h[tt][c]])

    with contextlib.ExitStack() as st:
        xin = [st.enter_context(nc.sbuf_tensor("sT_xin%d" % i, [128, 4, D], F32)) for i in range(2)]
        R_xin = [Res("xin%d" % i) for i in range(2)]
        xt = [st.enter_context(nc.sbuf_tensor("xtT%d" % i, [128, 8, TT], F32)) for i in range(2)]
        R_xt = [[Res("xtT%d_%d" % (i, c)) for c in range(8)] for i in range(2)]
        sq = st.enter_context(nc.sbuf_tensor("sqT", [128, 8, TT], BF16))
        R_sq = Res("sqT")
        sd = st.enter_context(nc.sbuf_tensor("sdT", [128, TT], F32))
        R_sd = Res("sdT")
        xv = x_in.rearrange("(t b p) d -> t p b d", b=4, p=128)
        for tt in range(NT):
            k = tt % 2
            s.dma("sp", xin[k][:, :, :], xv[tt], writes=[R_xin[k]])
            for c in range(8):
                b = next_bank()

                def tr(e, c=c, b=b, k=k):
                    for tb in range(4):
                        ins = e.transpose(out=bank(b, tb * 128, (tb + 1) * 128), in_=xin[k][:, tb, c * 128:(c + 1) * 128],
                                          identity=ident[:, :])
                    return ins
                s.op("pe", tr, reads=[R_xin[k], R_c], writes=[R_b[b]])
                if c % 2 == 0:
                    s.op("dve", lambda e, c=c, b=b, k=k: e.tensor_copy(out=xt[k][:, c, :], in_=bank(b)),
                         reads=[R_b[b]], writes=[R_xt[k][c]])
                else:
                    s.op("act", lambda e, c=c, b=b, k=k: e.activation(out=xt[k][:, c, :], in_=bank(b), func=AF.Copy),
                         reads=[R_b[b]], writes=[R_xt[k][c]])
            s.dma("sp", x_scr[:, :, tt * TT:(tt + 1) * TT], xt[k][:, :, :], reads=R_xt[k], writes=[R_xscr[tt]])
            emit_h(xt[k], R_xt[k], sq, R_sq, sd, R_sd, tt, g_norm(0))
        s.barrier()

    def finalize_a(a, ob, rrow, R_rrow):
        dp = 64 if a == 0 else 0
        s.op("dve", lambda e: e.reciprocal(out=rrow[dp:dp + 1, :], in_=bank(ob)[dp:dp + 1, :]),
             reads=[R_b[ob]], writes=[R_rrow])

    def finalize_b(a, ob, obc, sg, R_sg_t, qt, rrow, R_rrow, tmp, R_tmp, ochunk, R_ochunk):
        dp = 64 if a == 0 else 0
        lo = 64 * a
        s.op("pe", lambda e: e.matmul(bank(obc), ones_f[dp:dp + 1, :], rrow[dp:dp + 1, :], start=True, stop=True),
             reads=[R_rrow, R_c], writes=[R_b[obc]])
        s.op("dve", lambda e: e.tensor_tensor(out=tmp[lo:lo + 64, :], in0=bank(ob)[lo:lo + 64, :],
                                              in1=sg[lo:lo + 64, qt * TT:(qt + 1) * TT], op=ALU.mult),
             reads=[R_b[ob], R_sg_t], writes=[R_tmp])
        s.op("dve", lambda e: e.tensor_tensor(out=ochunk[lo:lo + 64, qt * TT:(qt + 1) * TT], in0=tmp[lo:lo + 64, :],
                                              in1=bank(obc)[lo:lo + 64, :], op=ALU.mult),
             reads=[R_tmp, R_b[obc]], writes=[R_ochunk])

    def phase_ga(i):
        j_l = i // 2
        with contextlib.ExitStack() as st:
            al = lambda n, sh, dt: st.enter_context(nc.sbuf_tensor("s%d_%s" % (i, n), sh, dt))
            wst = al("wst", [128, 8, 512], F32)
            R_wst = Res("wst")
            wbf = [al("wbf%d" % k, [128, 8, 512], BF16) for k in range(2)]
            R_wbf = [Res("wbf%d" % k) for k in range(2)]
            qT = al("qT", [128, S], BF16)
            kz = [al("kz%d" % a, [128, S], BF16) for a in range(2)]
            R_kz = Res("kzinit")
            s.op("pool", lambda e: e.memset(kz[0][64:128, :], 0.0), writes=[R_kz])
            s.op("pool", lambda e: e.memset(kz[1][0:64, :], 0.0), writes=[R_kz])
            sg = al("sg", [128, S], BF16)
            R_q = [Res("q%d" % t) for t in range(NT)]
            R_k = [Res("k%d" % t) for t in range(NT)]
            R_sg = [Res("sg%d" % t) for t in range(NT)]
            vaug = [al("vaug%d" % a, [128, 32, 128], BF16) for a in range(2)]
            R_v = [Res("v%d" % t) for t in range(NT)]
            tst = al("tst", [128, 2 * NE * 64], F32)
            R_tst = Res("tst")
            tbf = [al("tbf%d" % k, [128, 2, NE, 64], BF16) for k in range(2)]
            R_tbf = [Res("tbf%d" % k) for k in range(2)]
            exs = [al("exs%d" % k, [128, 1024], BF16) for k in range(2)]
            R_exs = [Res("exs%d" % k) for k in range(2)]
            pt = [al("pt%d" % k, [128, 1024], BF16) for k in range(2)]
            R_pt = [Res("pt%d" % k) for k in range(2)]
            rrow = [al("rrow%d" % k, [128, TT], F32) for k in range(2)]
            R_rrow = [Res("rrow%d" % k) for k in range(2)]
            tmp = al("tmp", [128, TT], F32)
            R_tmp = Res("tmp")
            och = [al("och%d" % k, [128, S], BF16) for k in range(2)]
            R_och = [Res("och%d" % k) for k in range(2)]
            R_vinit = Res("vinit")
            s.op("pool", lambda e: e.memset(vaug[0][:, :, 64:128], 1.0), writes=[R_vinit])
            s.op("pool", lambda e: e.memset(vaug[1][:, :, 0:64], 1.0), writes=[R_vinit])

            def load_w(hp):
                k = hp % 2
                s.dma("sp", wst[:, :, :], a_in[j_l][hp].rearrange("p (c f) -> p c f", c=8), writes=[R_wst])
                s.op("pool", lambda e: e.tensor_copy(out=wbf[k][:, :, :], in_=wst[:, :, :]),
                     reads=[R_wst], writes=[R_wbf[k]])

            tcount = [0]

            def load_tab(h):
                k = tcount[0] % 2
                tcount[0] += 1
                s.dma("sp", tst[:, :], tb_in[j_l][h], writes=[R_tst])
                s.op("act", lambda e: e.activation(out=tbf[k][:, :, :, :].rearrange("p a e c -> p (a e c)"), in_=tst[:, :], func=AF.Exp),
                     reads=[R_tst], writes=[R_tbf[k]])
                return k

            load_w(0)
            for hp in range(8):
                wk = hp % 2
                w = wbf[wk]
                if hp + 1 < 8:
                    load_w(hp + 1)
                for tt in range(NT):
                    tsl = slice(tt * TT, (tt + 1) * TT)
                    for kind in range(3):
                        col = (0, 128, 384)[kind]
                        b = next_bank()

                        def mm(e, b=b, col=col, tsl=tsl, w=w):
                            for c in range(8):
                                ins = e.matmul(bank(b), w[:, c, col:col + 128], hT[:, c, tsl], start=(c == 0), stop=(c == 7))
                            return ins
                        s.op("pe", mm, reads=[R_wbf[wk]] + R_h[tt], writes=[R_b[b]])
                        if kind == 0:
                            s.op("dve", lambda e, b=b, tsl=tsl: e.tensor_copy(out=qT[:, tsl], in_=bank(b)),
                                 reads=[R_b[b]], writes=[R_q[tt]])
                        elif kind == 1:
                            s.op("dve", lambda e, b=b, tsl=tsl: e.tensor_copy(out=kz[0][0:64, tsl], in_=bank(b)[0:64, :]),
                                 reads=[R_b[b], R_kz], writes=[R_k[tt]])
                            s.op("dve", lambda e, b=b, tsl=tsl: e.tensor_copy(out=kz[1][64:128, tsl], in_=bank(b)[64:128, :]),
                                 reads=[R_b[b], R_kz], writes=[R_k[tt]])
                        else:
                            s.op("act", lambda e, b=b, tsl=tsl: e.activation(out=sg[:, tsl], in_=bank(b), func=AF.Silu),
                                 reads=[R_b[b]], writes=[R_sg[tt]])
                    b = next_bank()

                    def mmv(e, b=b, tt=tt, w=w):
                        for tcl in range(4):
                            t0 = tt * TT + tcl * 128
                            for c in range(8):
                                ins = e.matmul(bank(b, tcl * 128, (tcl + 1) * 128), hT[:, c, t0:t0 + 128], w[:, c, 256:384],
                                               start=(c == 0), stop=(c == 7))
                        return ins
                    s.op("pe", mmv, reads=[R_wbf[wk]] + R_h[tt], writes=[R_b[b]])
                    bv = bank(b).rearrange("p (t f) -> p t f", t=4)
                    s.op("dve", lambda e, bv=bv, tt=tt: e.tensor_copy(out=vaug[0][:, tt * 4:tt * 4 + 4, 0:64], in_=bv[:, :, 0:64]),
                         reads=[R_b[b], R_vinit], writes=[R_v[tt]])
                    s.op("dve", lambda e, bv=bv, tt=tt: e.tensor_copy(out=vaug[1][:, tt * 4:tt * 4 + 4, 64:128], in_=bv[:, :, 64:128]),
                         reads=[R_b[b], R_vinit], writes=[R_v[tt]])
                ok = hp % 2
                steps = []
                for a in range(2):
                    tk = load_tab(2 * hp + a)
                    for (m, js) in a_tiles():
                        prs = [js[z:z + 2] for z in range(0, len(js), 2)]
                        for pi_, pr in enumerate(prs):
                            steps.append((a, tk, m, pr, pi_ == 0, pi_ == len(prs) - 1))
                gq = [0]

                def a_qk(idx):
                    a, tk, m, pr, first, last = steps[idx]
                    sb0 = 2 * (idx % 2)
                    lo = 64 * a

                    def qk(e, pr=pr, sb0=sb0, m=m, lo=lo):
                        for z, j in enumerate(pr):
                            if z == 1:
                                for _f in range(NFILL):
                                    e.matmul(bank(7), kz[lo // 64][:, j * 128:(j + 1) * 128],
                                             qT[:, m * TT:(m + 1) * TT], start=True, stop=True)
                            ins = e.matmul(bank(sb0 + z), kz[lo // 64][:, j * 128:(j + 1) * 128],
                                           qT[:, m * TT:(m + 1) * TT], start=True, stop=True)
                        return ins
                    s.op("pe", qk, reads=[R_k[jj // 4] for jj in pr] + [R_q[m]], writes=[R_b[sb0], R_b[sb0 + 1], R_b[7]])

                def a_exp(idx):
                    a, tk, m, pr, first, last = steps[idx]
                    sb0 = 2 * (idx % 2)
                    ek = idx % 2
                    s.op("act", lambda e, sb0=sb0, ek=ek: e.activation(
                        out=exs[ek][:, :], in_=ps[:, sb0 * 512:(sb0 + 2) * 512], func=AF.Exp, scale=0.125),
                        reads=[R_b[sb0], R_b[sb0 + 1]], writes=[R_exs[ek]])

                def a_mask(idx):
                    a, tk, m, pr, first, last = steps[idx]
                    ek = idx % 2
                    for z, j in enumerate(pr):
                        e0 = 10 - (2 * j - 8 * m)
                        for (b0, b1, tab) in a_segments(m, j):
                            s.op("dve", lambda e, z=z, e0=e0, b0=b0, b1=b1, tab=tab, ek=ek, tk=tk: e.tensor_tensor(
                                out=pt[ek][:, z * 512 + b0 * 64:z * 512 + b1 * 64],
                                in0=exs[ek][:, z * 512 + b0 * 64:z * 512 + b1 * 64],
                                in1=tbf[tk][:, tab, e0 + b0:e0 + b1, :].rearrange("p e c -> p (e c)"), op=ALU.mult),
                                reads=[R_exs[ek], R_tbf[tk]], writes=[R_pt[ek]])

                pend = []

                def a_pv(idx):
                    a, tk, m, pr, first, last = steps[idx]
                    ek = idx % 2
                    g_ = gq[0]
                    ob = 4 + (g_ % 2)

                    def pv(e, pr=pr, ek=ek, ob=ob, first=first, last=last, a=a):
                        for z, j in enumerate(pr):
                            ins = e.matmul(bank(ob), vaug[a][:, j, :], pt[ek][:, z * 512:(z + 1) * 512],
                                           start=(first and z == 0), stop=(last and z == len(pr) - 1))
                        return ins
                    s.op("pe", pv, reads=[R_pt[ek]] + [R_v[jj // 4] for jj in pr], writes=[R_b[ob]])
                    if last:
                        finalize_a(a, ob, rrow[g_ % 2], R_rrow[g_ % 2])
                        pend.append((idx + 2, (a, ob, 6, sg, R_sg[m], m, rrow[g_ % 2], R_rrow[g_ % 2],
                                               tmp, R_tmp, och[ok], R_och[ok])))
                        gq[0] += 1

                a_qk(0)
                for idx in range(len(steps)):
                    a_exp(idx)
                    if idx + 1 < len(steps):
                        a_qk(idx + 1)
                    a_mask(idx)
                    a_pv(idx)
                    while pend and pend[0][0] <= idx:
                        finalize_b(*pend.pop(0)[1])
                while pend:
                    finalize_b(*pend.pop(0)[1])
                s.dma("sp", o_scr[:, hp, :], och[ok][:, :], reads=[R_och[ok]], writes=[R_oscr[hp]])
            s.barrier()

    def phase_gb(i):
        j_l = i // 2
        with contextlib.ExitStack() as st:
            al = lambda n, sh, dt: st.enter_context(nc.sbuf_tensor("s%d_%s" % (i, n), sh, dt))
            wst = al("wstb", [128, 4, 640], F32)
            R_wst = Res("wstb")
            wbf = [al("wbfb%d" % k, [128, 8, 704], BF16) for k in range(2)]
            R_wbf = [Res("wbfb%d" % k) for k in range(2)]
            qT = [al("qTb%d" % c, [128, S], BF16) for c in range(2)]
            kz = [al("kzb%d" % a, [128, S], BF16) for a in range(2)]
            R_kz = Res("kzinitb")
            s.op("pool", lambda e: e.memset(kz[0][64:128, :], 0.0), writes=[R_kz])
            s.op("pool", lambda e: e.memset(kz[1][0:64, :], 0.0), writes=[R_kz])
            sg = [al("sgb%d" % c, [128, S], BF16) for c in range(2)]
            R_q = [[Res("qb%d_%d" % (c, t)) for t in range(NT)] for c in range(2)]
            R_k = [Res("kb%d" % t) for t in range(NT)]
            R_sg = [[Res("sgb%d_%d" % (c, t)) for t in range(NT)] for c in range(2)]
            vaug = [al("vaugb%d" % a, [128, 32, 128], BF16) for a in range(2)]
            R_v = [Res("vb%d" % t) for t in range(NT)]
            cst = [al("cst%d" % k, [128, TT], F32) for k in range(2)]
            snt = [al("snt%d" % k, [128, TT], F32) for k in range(2)]
            R_cs = [Res("cs%d" % k) for k in range(2)]
            R_sn = [Res("sn%d" % k) for k in range(2)]
            sqb = al("sqb", [128, TT], BF16)
            R_sqb = Res("sqb")
            ub = al("ub", [128, TT], F32)
            R_ub = Res("ub")
            sdb = al("sdb", [128, TT], F32)
            R_sdb = Res("sdb")
            t1 = al("t1b", [128, TT], F32)
            R_t1 = Res("t1b")
            t2 = al("t2b", [128, TT], F32)
            R_t2 = Res("t2b")
            pt = [al("ptb%d" % k, [128, 1024], BF16) for k in range(3)]
            R_pt = [Res("ptb%d" % k) for k in range(3)]
            rrow = [al("rrowb%d" % k, [128, TT], F32) for k in range(2)]
            R_rrow = [Res("rrowb%d" % k) for k in range(2)]
            tmp = al("tmpb", [128, TT], F32)
            R_tmp = Res("tmpb")
            och = [al("ochb%d" % k, [128, S], BF16) for k in range(2)]
            R_och = [Res("ochb%d" % k) for k in range(2)]
            R_vinit = Res("vinitb")
            s.op("pool", lambda e: e.memset(vaug[0][:, :, 64:128], 1.0), writes=[R_vinit])
            s.op("pool", lambda e: e.memset(vaug[1][:, :, 0:64], 1.0), writes=[R_vinit])

            def load_w(g):
                k = g % 2
                wv = b_in[j_l][g].rearrange("p (c f) -> p c f", c=8)
                for hf in range(2):
                    cs_ = slice(4 * hf, 4 * hf + 4)
                    s.dma("sp", wst[:, :, :], wv[:, cs_, :], writes=[R_wst])
                    for (d0, d1, s0, s1) in ((0, 256, 0, 256), (256, 320, 256, 320), (320, 384, 256, 320), (384, 704, 320, 640)):
                        s.op("pool", lambda e, d0=d0, d1=d1, s0=s0, s1=s1, cs_=cs_: e.tensor_copy(
                            out=wbf[k][:, cs_, d0:d1], in_=wst[:, :, s0:s1]), reads=[R_wst], writes=[R_wbf[k]])

            cscount = [0]
            load_w(0)
            for g in range(4):
                wk = g % 2
                w = wbf[wk]
                if g + 1 < 4:
                    load_w(g + 1)
                for kind in range(3):
                    col = (0, 128, 256)[kind]
                    gcol = (G_BQ + j_l) if kind < 2 else (G_BK + j_l)
                    for tt in range(NT):
                        tsl = slice(tt * TT, (tt + 1) * TT)
                        ck = cscount[0] % 2
                        cscount[0] += 1
                        s.dma("sp", cst[ck][:, :], cos_in[:, tsl], writes=[R_cs[ck]])
                        s.dma("sp", snt[ck][:, :], sin_in[:, tsl], writes=[R_sn[ck]])
                        b = next_bank()

                        def mm(e, b=b, col=col, tsl=tsl, w=w):
                            for c in range(8):
                                ins = e.matmul(bank(b), w[:, c, col:col + 128], hT[:, c, tsl], start=(c == 0), stop=(c == 7))
                            return ins
                        s.op("pe", mm, reads=[R_wbf[wk]] + R_h[tt], writes=[R_b[b]])
                        s.op("act", lambda e, b=b: e.activation(out=sqb[:, :], in_=bank(b), func=AF.Square),
                             reads=[R_b[b]], writes=[R_sqb])
                        s.op("act", lambda e, b=b, gcol=gcol: e.activation(out=ub[:, :], in_=bank(b), func=AF.Copy,
                                                                          scale=gv[:, gcol:gcol + 1]),
                             reads=[R_b[b], R_c], writes=[R_ub])
                        b2 = next_bank()
                        s.op("pe", lambda e, b2=b2: e.matmul(bank(b2), bones[:, :], sqb[:, :], start=True, stop=True),
                             reads=[R_sqb, R_c], writes=[R_b[b2]])
                        b3 = next_bank()
                        s.op("pe", lambda e, b3=b3: e.matmul(bank(b3), perm[:, :], ub[:, :], start=True, stop=True),
                             reads=[R_ub, R_c], writes=[R_b[b3]])
                        s.op("act", lambda e, b2=b2: e.activation(out=sdb[:, :], in_=bank(b2), func=AF.Sqrt, scale=1.0 / 64,
                                                                   bias=epsc[:, 0:1]),
                             reads=[R_b[b2], R_c], writes=[R_sdb])
                        s.op("dve", lambda e: e.reciprocal(out=sdb[:, :], in_=sdb[:, :]), reads=[R_sdb], writes=[R_sdb])
                        s.op("pool", lambda e, ck=ck: e.tensor_tensor(out=t1[:, :], in0=ub[:, :], in1=cst[ck][:, :], op=ALU.mult),
                             reads=[R_ub, R_cs[ck]], writes=[R_t1])
                        s.op("dve", lambda e, b3=b3, ck=ck: e.tensor_tensor(out=t2[:, :], in0=bank(b3), in1=snt[ck][:, :], op=ALU.mult),
                             reads=[R_b[b3], R_sn[ck]], writes=[R_t2])
                        s.op("pool", lambda e: e.tensor_tensor(out=t1[:, :], in0=t1[:, :], in1=t2[:, :], op=ALU.add),
                             reads=[R_t1, R_t2], writes=[R_t1])
                        if kind < 2:
                            dst, R_dst = qT[kind][:, tsl], R_q[kind][tt]
                            s.op("dve", lambda e, dst=dst: e.tensor_tensor(out=dst, in0=t1[:, :], in1=sdb[:, :], op=ALU.mult),
                                 reads=[R_t1, R_sdb], writes=[R_dst])
                        else:
                            for a_ in range(2):
                                s.op("dve", lambda e, a_=a_, tsl=tsl: e.tensor_tensor(
                                    out=kz[a_][64 * a_:64 * a_ + 64, tsl], in0=t1[64 * a_:64 * a_ + 64, :],
                                    in1=sdb[64 * a_:64 * a_ + 64, :], op=ALU.mult),
                                    reads=[R_t1, R_sdb, R_kz], writes=[R_k[tt]])
                for tt in range(NT):
                    tsl = slice(tt * TT, (tt + 1) * TT)
                    for c2 in range(2):
                        col = 448 + 128 * c2
                        b = next_bank()

                        def mm(e, b=b, col=col, tsl=tsl, w=w):
                            for c in range(8):
                                ins = e.matmul(bank(b), w[:, c, col:col + 128], hT[:, c, tsl], start=(c == 0), stop=(c == 7))
                            return ins
                        s.op("pe", mm, reads=[R_wbf[wk]] + R_h[tt], writes=[R_b[b]])
                        s.op("act", lambda e, b=b, tsl=tsl, c2=c2: e.activation(out=sg[c2][:, tsl], in_=bank(b), func=AF.Silu),
                             reads=[R_b[b]], writes=[R_sg[c2][tt]])
                    b = next_bank()

                    def mmv(e, b=b, tt=tt, w=w):
                        for tcl in range(4):
                            t0 = tt * TT + tcl * 128
                            for c in range(8):
                                ins = e.matmul(bank(b, tcl * 64, (tcl + 1) * 64), hT[:, c, t0:t0 + 128], w[:, c, 384:448],
                                               start=(c == 0), stop=(c == 7))
                        return ins
                    s.op("pe", mmv, reads=[R_wbf[wk]] + R_h[tt], writes=[R_b[b]])
                    bv = bank(b, 0, 256).rearrange("p (t f) -> p t f", t=4)
                    s.op("dve", lambda e, bv=bv, tt=tt: e.tensor_copy(out=vaug[0][:, tt * 4:tt * 4 + 4, 0:64], in_=bv),
                         reads=[R_b[b], R_vinit], writes=[R_v[tt]])
                    s.op("dve", lambda e, bv=bv, tt=tt: e.tensor_copy(out=vaug[1][:, tt * 4:tt * 4 + 4, 64:128], in_=bv),
                         reads=[R_b[b], R_vinit], writes=[R_v[tt]])
                steps = [(hi, qt, kp) for hi in range(4) for qt in range(NT) for kp in range(16)]
                pend = []
                gq = [0]

                def b_qk(idx):
                    hi, qt, kp = steps[idx]
                    c2, a = hi // 2, hi % 2
                    lo = 64 * a
                    sb0 = 2 * (idx % 2)

                    def qk(e, kp=kp, sb0=sb0, qt=qt, lo=lo, c2=c2):
                        for z in range(2):
                            j = 2 * kp + z
                            if z == 1:
                                for _f in range(NFILL):
                                    e.matmul(bank(7), kz[lo // 64][:, j * 128:(j + 1) * 128],
                                             qT[c2][:, qt * TT:(qt + 1) * TT], start=True, stop=True)
                            ins = e.matmul(bank(sb0 + z), kz[lo // 64][:, j * 128:(j + 1) * 128],
                                           qT[c2][:, qt * TT:(qt + 1) * TT], start=True, stop=True)
                        return ins
                    s.op("pe", qk, reads=[R_k[kp // 2], R_q[c2][qt]], writes=[R_b[sb0], R_b[sb0 + 1], R_b[7]])

                def b_exp(idx):
                    sb0 = 2 * (idx % 2)
                    ek = idx % 3
                    s.op("act", lambda e, sb0=sb0, ek=ek: e.activation(
                        out=pt[ek][:, :], in_=ps[:, sb0 * 512:(sb0 + 2) * 512], func=AF.Exp, scale=0.125),
                        reads=[R_b[sb0], R_b[sb0 + 1]], writes=[R_pt[ek]])

                def b_pv(idx):
                    hi, qt, kp = steps[idx]
                    c2, a = hi // 2, hi % 2
                    ek = idx % 3
                    g_ = gq[0]
                    ob = 4 + (g_ % 2)

                    def pv(e, kp=kp, ek=ek, ob=ob, a=a):
                        for z in range(2):
                            j = 2 * kp + z
                            ins = e.matmul(bank(ob), vaug[a][:, j, :], pt[ek][:, z * 512:(z + 1) * 512],
                                           start=(kp == 0 and z == 0), stop=(kp == 15 and z == 1))
                        return ins
                    s.op("pe", pv, reads=[R_pt[ek], R_v[kp // 2]], writes=[R_b[ob]])
                    if kp == 15:
                        chunk = 2 * g + c2
                        ok = chunk % 2
                        finalize_a(a, ob, rrow[g_ % 2], R_rrow[g_ % 2])
                        pend.append((idx + 2, (a, ob, 6, sg[c2], R_sg[c2][qt], qt, rrow[g_ % 2], R_rrow[g_ % 2],
                                               tmp, R_tmp, och[ok], R_och[ok]), (chunk, ok) if (a == 1 and qt == NT - 1) else None))
                        gq[0] += 1

                def b_fin(item):
                    finalize_b(*item[1])
                    if item[2] is not None:
                        chunk, ok = item[2]
                        s.dma("sp", o_scr[:, chunk, :], och[ok][:, :], reads=[R_och[ok]], writes=[R_oscr[chunk]])

                b_qk(0)
                for idx in range(len(steps)):
                    b_exp(idx)
                    if idx + 1 < len(steps):
                        b_qk(idx + 1)
                    b_pv(idx)
                    while pend and pend[0][0] <= idx:
                        b_fin(pend.pop(0))
                while pend:
                    b_fin(pend.pop(0))
            s.barrier()

    def phase_o(i, last):
        with contextlib.ExitStack() as st:
            al = lambda n, sh, dt: st.enter_context(nc.sbuf_tensor("s%d_%s" % (i, n), sh, dt))
            wst = al("wsto", [128, 2, 1024], F32)
            R_wst = Res("wsto")
            wo = al("wo", [128, 8, 1024], BF16)
            wg = al("wg", [128, 8, 1024], BF16)
            wp = al("wp", [128, 2, 1024], BF16)
            R_w = Res("w_o")
            xt = [al("xtO%d" % k, [128, 8, TT], F32) for k in range(2)]
            R_xt = [[Res("xtO%d_%d" % (k, c)) for c in range(8)] for k in range(2)]
            ot = [al("otO%d" % k, [128, 8, TT], BF16) for k in range(2)]
            R_ot = [Res("otO%d" % k) for k in range(2)]
            pin = [al("pin%d" % k, [128, 4, 256], F32) for k in range(2)]
            R_pin = [Res("pin%d" % k) for k in range(2)]
            pT = al("pT", [128, 2, TT], BF16)
            R_pT = Res("pT")
            sq = al("sqO", [128, 8, TT], BF16)
            R_sq = Res("sqO")
            sd = al("sdO", [128, TT], F32)
            R_sd = Res("sdO")
            n1 = sq
            R_n1 = R_sq
            gs = [al("gs%d" % k, [128, TT], F32) for k in range(2)]
            R_gs = [Res("gs%d" % k) for k in range(2)]
            if last:
                yo = al("yo", [128, 4, D], F32)
                R_yo = Res("yo")
            for (wdst, wsrc, nch) in ((wo, wo_in[i], 8), (wg, wg_in[i], 8), (wp, wp_in[i], 2)):
                wv = wsrc.rearrange("p (c f) -> p c f", c=nch)
                for c0 in range(0, nch, 2):
                    s.dma("sp", wst[:, :, :], wv[:, c0:c0 + 2, :], writes=[R_wst])
                    s.op("pool", lambda e, wdst=wdst, c0=c0: e.tensor_copy(out=wdst[:, c0:c0 + 2, :], in_=wst[:, :, :]),
                         reads=[R_wst], writes=[R_w])
            pv_ = p_in[i].rearrange("(t b p) f -> t p b f", b=4, p=128)
            ov = out.rearrange("(t b p) d -> t p b d", b=4, p=128)

            def loads(tt):
                k = tt % 2
                tsl = slice(tt * TT, (tt + 1) * TT)
                s.dma("sp", xt[k][:, :, :], x_scr[:, :, tsl], reads=[R_xscr[tt]], writes=R_xt[k], sem_res=R_xt[k][0])
                s.dma("sp", ot[k][:, :, :], o_scr[:, :, tsl], reads=R_oscr, writes=[R_ot[k]])
                s.dma("sp", pin[k][:, :, :], pv_[tt], writes=[R_pin[k]])

            loads(0)
            for tt in range(NT):
                k = tt % 2
                tsl = slice(tt * TT, (tt + 1) * TT)
                if tt + 1 < NT:
                    loads(tt + 1)
                X = xt[k]
                for fc in range(8):
                    b = next_bank()

                    def mm(e, b=b, fc=fc, k=k):
                        for c in range(8):
                            ins = e.matmul(bank(b), wo[:, c, fc * 128:(fc + 1) * 128], ot[k][:, c, :], start=(c == 0), stop=(c == 7))
                        return ins
                    s.op("pe", mm, reads=[R_w, R_ot[k]], writes=[R_b[b]])
                    s.op("dve", lambda e, b=b, fc=fc, X=X: e.tensor_tensor(out=X[:, fc, :], in0=X[:, fc, :], in1=bank(b), op=ALU.add),
                         reads=[R_b[b], R_xt[k][fc]], writes=[R_xt[k][fc]])
                emit_rstd(X, R_xt[k], sq, R_sq, sd, R_sd, 1.0 / D)
                for c in range(8):
                    eng = "dve"
                    s.op(eng, lambda e, c=c, X=X: e.scalar_tensor_tensor(
                        out=n1[:, c, :], in0=X[:, c, :], scalar=gv[:, g_ple(i) + c:g_ple(i) + c + 1], in1=sd[:, :],
                        op0=ALU.mult, op1=ALU.mult), reads=[R_xt[k][c], R_sd, R_c], writes=[R_n1])
                for c2 in range(2):
                    b = next_bank()

                    def tr(e, b=b, c2=c2, k=k):
                        for tb in range(4):
                            ins = e.transpose(out=bank(b, tb * 128, (tb + 1) * 128), in_=pin[k][:, tb, c2 * 128:(c2 + 1) * 128],
                                              identity=ident[:, :])
                        return ins
                    s.op("pe", tr, reads=[R_pin[k], R_c], writes=[R_b[b]])
                    s.op("act", lambda e, b=b, c2=c2: e.activation(out=pT[:, c2, :], in_=bank(b), func=AF.Copy),
                         reads=[R_b[b]], writes=[R_pT])
                for fc in range(8):
                    b = next_bank()

                    def mmg(e, b=b, fc=fc):
                        for c in range(8):
                            ins = e.matmul(bank(b), wg[:, c, fc * 128:(fc + 1) * 128], n1[:, c, :], start=(c == 0), stop=(c == 7))
                        return ins
                    s.op("pe", mmg, reads=[R_w, R_n1], writes=[R_b[b]])
                    gk = fc % 2
                    s.op("act", lambda e, b=b, gk=gk: e.activation(out=gs[gk][:, :], in_=bank(b), func=AF.Sigmoid),
                         reads=[R_b[b]], writes=[R_gs[gk]])
                    b2 = next_bank()

                    def mmp(e, b2=b2, fc=fc):
                        for c in range(2):
                            ins = e.matmul(bank(b2), wp[:, c, fc * 128:(fc + 1) * 128], pT[:, c, :], start=(c == 0), stop=(c == 1))
                        return ins
                    s.op("pe", mmp, reads=[R_w, R_pT], writes=[R_b[b2]])
                    s.op("dve", lambda e, b2=b2, gk=gk: e.tensor_tensor(out=gs[gk][:, :], in0=gs[gk][:, :], in1=bank(b2), op=ALU.mult),
                         reads=[R_gs[gk], R_b[b2]], writes=[R_gs[gk]])
                    s.op("pool", lambda e, fc=fc, X=X, gk=gk: e.tensor_tensor(out=X[:, fc, :], in0=X[:, fc, :], in1=gs[gk][:, :], op=ALU.add),
                         reads=[R_gs[gk], R_xt[k][fc]], writes=[R_xt[k][fc]])
                if dbg == ("x", i):
                    s.dma("sp", dbg_out[:, :, tsl], X[:, :, :], reads=R_xt[k], writes=[R_out[tt]])
                if not last:
                    s.dma("sp", x_scr[:, :, tsl], X[:, :, :], reads=R_xt[k], writes=[R_xscr[tt]])
                    emit_h(X, R_xt[k], sq, R_sq, sd, R_sd, tt, g_norm(i + 1))
                else:
                    emit_rstd(X, R_xt[k], sq, R_sq, sd, R_sd, 1.0 / D)
                    for c in range(8):
                        eng = "dve"
                        s.op(eng, lambda e, c=c, X=X: e.scalar_tensor_tensor(
                            out=X[:, c, :], in0=X[:, c, :], scalar=gv[:, G_FINAL + c:G_FINAL + c + 1], in1=sd[:, :],
                            op0=ALU.mult, op1=ALU.mult), reads=[R_xt[k][c], R_sd, R_c], writes=[R_xt[k][c]])
                    xn = X
                    R_xn = R_xt[k]
                    for tb in range(4):
                        for cq in range(2):
                            b = next_bank()

                            def tr(e, b=b, tb=tb, cq=cq, xn=xn):
                                for cc in range(4):
                                    c = cq * 4 + cc
                                    ins = e.transpose(out=bank(b, cc * 128, (cc + 1) * 128), in_=xn[:, c, tb * 128:(tb + 1) * 128],
                                                      identity=ident[:, :])
                                return ins
                            s.op("pe", tr, reads=R_xn + [R_c], writes=[R_b[b]])
                            if cq == 0:
                                s.op("dve", lambda e, b=b, tb=tb: e.tensor_copy(out=yo[:, tb, 0:512], in_=bank(b)),
                                     reads=[R_b[b]], writes=[R_yo])
                            else:
                                s.op("act", lambda e, b=b, tb=tb: e.activation(out=yo[:, tb, 512:1024], in_=bank(b), func=AF.Copy),
                                     reads=[R_b[b]], writes=[R_yo])
                    s.dma("sp", ov[tt], yo[:, :, :], reads=[R_yo], writes=[R_out[tt]])
            s.barrier()

    for i in range(n_layers):
        if dbg == ("h", i):
            s.dma("sp", dbg_out[:, :, :], hT[:, :, :], reads=[r for t in R_h for r in t], writes=[R_out[0]])
            break
        if i % 2 == 0:
            phase_ga(i)
        else:
            phase_gb(i)
        if dbg == ("o", i):
            s.dma("sp", dbg_out[:, :, :], o_scr[:, :, :], reads=R_oscr, writes=[R_out[0]])
            break
        phase_o(i, last=(i == n_layers - 1))
    s.barrier()
    s.emit()
    return nc


_CACHE = {}
_PADDED = ("cosT", "sinT", "a_in", "b_in", "wo", "wg", "wp", "tb")


def _prep_shared(inp):
    f = lambda a: np.ascontiguousarray(np.asarray(a, dtype=np.float32))
    sh = {}
    cosT, sinT, perm = _rope_tables()
    sh["cosT"], sh["sinT"], sh["perm"] = cosT, sinT, perm
    sh["ident"] = np.eye(128, dtype=np.float32)
    gv = np.zeros((128, 80), np.float32)
    ng, pg = f(inp["norm_g"]), f(inp["ple_norm_g"])
    for i in range(DEPTH):
        gv[:, 16 * i:16 * i + 8] = _vec_pc(ng[i])
        gv[:, 16 * i + 8:16 * i + 16] = _vec_pc(pg[i])
    gv[:, 64:72] = _vec_pc(f(inp["final_norm_g"]))
    bq, bk = f(inp["b_q_norm"]), f(inp["b_k_norm"])
    for j in range(2):
        gv[:, 72 + j] = np.tile(bq[j], 2)
        gv[:, 74 + j] = np.tile(bk[j], 2)
    sh["gv"] = gv
    awi, bwi = f(inp["a_w_in"]), f(inp["b_w_in"])
    for j in range(2):
        W = awi[j]
        groups = []
        for hp in range(8):
            cols = np.concatenate([W[:, k * 1024 + hp * 128:k * 1024 + (hp + 1) * 128] for k in range(4)], axis=1)
            groups.append(_pc(cols))
        sh["a_in%d" % j] = np.ascontiguousarray(np.stack(groups))
        W = bwi[j]
        groups = []
        for g in range(4):
            cols = np.concatenate([W[:, 256 * g:256 * (g + 1)], W[:, 1024 + 64 * g:1024 + 64 * (g + 1)],
                                   W[:, 1280 + 64 * g:1280 + 64 * (g + 1)], W[:, 1536 + 256 * g:1536 + 256 * (g + 1)]], axis=1)
            groups.append(_pc(cols))
        sh["b_in%d" % j] = np.ascontiguousarray(np.stack(groups))
        sh["tb%d" % j] = _a_tables(f(inp["a_rpb"])[j])
    awo, bwo = f(inp["a_w_out"]), f(inp["b_w_out"])
    wgt, wpj = f(inp["ple_w_gate"]), f(inp["ple_w_proj"])
    for i in range(DEPTH):
        sh["wo%d" % i] = _pc(awo[i // 2] if i % 2 == 0 else bwo[i // 2])
        sh["wg%d" % i] = _pc(wgt[i])
        sh["wp%d" % i] = _pc(wpj[i])
    return sh


def _core_inputs(sh, xb, pb, b):
    m = {}
    for k_, v_ in sh.items():
        if k_ in _PADDED or k_[:-1] in _PADDED:
            pad = np.full((1,) + v_.shape[1:], float(b), np.float32)
            m[k_] = np.concatenate([v_, pad], axis=0)
        else:
            m[k_] = v_
    m["x"] = np.ascontiguousarray(xb)
    m["p"] = np.ascontiguousarray(pb)
    return m


def kernel(x, p, norm_g, a_w_in, a_rpb, a_w_out, b_w_in, b_q_norm, b_k_norm, b_w_out,
           ple_norm_g, ple_w_gate, ple_w_proj, final_norm_g):
    inp = dict(norm_g=norm_g, a_w_in=a_w_in, a_rpb=a_rpb, a_w_out=a_w_out, b_w_in=b_w_in, b_q_norm=b_q_norm,
               b_k_norm=b_k_norm, b_w_out=b_w_out, ple_norm_g=ple_norm_g, ple_w_gate=ple_w_gate,
               ple_w_proj=ple_w_proj, final_norm_g=final_norm_g)
    sh = _prep_shared(inp)
    x = np.asarray(x, dtype=np.float32)
    p = np.asarray(p, dtype=np.float32)
    if "nc" not in _CACHE:
        _CACHE["nc"] = build()
    nc = _CACHE["nc"]
    in_maps = [_core_inputs(sh, x[b], p[:, b], b) for b in range(8)]
    res = run_bass_kernel_spmd(nc, in_maps, core_ids=list(range(8)))
    return np.stack([np.asarray(r["out"], dtype=np.float32) for r in res.results], axis=0)
```

```python
import contextlib
import numpy as np
import ml_dtypes
import concourse.bass as bass
import concourse.mybir as mybir
from concourse.bass_utils import run_bass_kernel_spmd

F32 = mybir.dt.float32
BF16 = mybir.dt.bfloat16
AF = mybir.ActivationFunctionType
ALU = mybir.AluOpType

S = 4096
D = 1024
NT = 8
TT = 512
DEPTH = 4
EPS = 1e-6
NE = 22
ENGS = ("pe", "act", "dve", "pool", "sp")


class Res:
    __slots__ = ("name", "w", "r", "sem", "semval")

    def __init__(self, name):
        self.name = name
        self.w = None
        self.r = []
        self.sem = None
        self.semval = 0


class Sched:
    def __init__(self, nc):
        self.nc = nc
        self.ops = {e: [] for e in ENGS}
        self.known = {e: {} for e in ENGS}
        self.dma_sems = []

    def _need(self, eng, tok, waits):
        if tok is None:
            return
        if tok[0] == "e":
            _, f, idx = tok
            if f == eng and eng == "pe":
                return
            key = ("e", f)
            val = idx
        else:
            key = ("s", tok[1])
            val = tok[2]
        kn = self.known[eng]
        if kn.get(key, -1) >= val:
            return
        kn[key] = val
        waits.append(tok)
        if tok[0] == "e":
            self.ops[tok[1]][tok[2]][2] = True

    def _deps(self, eng, reads, writes):
        waits = []
        for r in reads:
            self._need(eng, r.w, waits)
        for w in writes:
            self._need(eng, w.w, waits)
            for t in w.r:
                self._need(eng, t, waits)
        return waits

    def _commit(self, tok, reads, writes):
        for r in reads:
            r.r.append(tok)
        for w in writes:
            w.w = tok
            w.r = []

    def op(self, eng, fn, reads=(), writes=()):
        waits = self._deps(eng, reads, writes)
        tok = ("e", eng, len(self.ops[eng]))
        self.ops[eng].append([fn, waits, False, None])
        self._commit(tok, reads, writes)
        return tok

    def dma(self, eng, out, in_, reads=(), writes=(), sem_res=None):
        waits = self._deps(eng, reads, writes)
        if sem_res is None:
            sem_res = writes[0]
        if sem_res.sem is None:
            sem_res.sem = len(self.dma_sems)
            self.dma_sems.append(sem_res)
        sem_res.semval += 16
        tok = ("s", sem_res.sem, sem_res.semval)

        def fn(e, out=out, in_=in_):
            return e.dma_start(out=out, in_=in_)

        self.ops[eng].append([fn, waits, False, sem_res.sem])
        self._commit(tok, reads, writes)
        return tok

    def barrier(self):
        last = {}
        for f in ENGS:
            for i in range(len(self.ops[f]) - 1, -1, -1):
                o = self.ops[f][i]
                if o[0] is not None and o[3] is None:
                    last[f] = ("e", f, i)
                    break
        for e in ENGS:
            waits = []
            for f in ENGS:
                if f != e and f in last:
                    self._need(e, last[f], waits)
            for r in self.dma_sems:
                self._need(e, ("s", r.sem, r.semval), waits)
            self.ops[e].append([None, waits, False, None])

    def emit(self):
        nc = self.nc
        val = {}
        for e in ENGS:
            c = 0
            for i, o in enumerate(self.ops[e]):
                if o[2]:
                    c += 1
                    val[(e, i)] = c
        with contextlib.ExitStack() as st:
            esem = {e: st.enter_context(nc.semaphore("prog_" + e)) for e in ENGS}
            dsem = [st.enter_context(nc.semaphore("dma_%d" % i)) for i in range(len(self.dma_sems))]
            block = st.enter_context(nc.Block())
            ops = self.ops

            def body(e, h):
                for fn, waits, needed, dinc in ops[e]:
                    for t in waits:
                        if t[0] == "e":
                            h.wait_ge(esem[t[1]], val[(t[1], t[2])])
                        else:
                            h.wait_ge(dsem[t[1]], t[2])
                    if fn is None:
                        continue
                    ins = fn(h)
                    if dinc is not None:
                        ins.then_inc(dsem[dinc], 16)
                    elif needed:
                        ins.then_inc(esem[e], 1)

            @block.tensor
            def _(h):
                body("pe", h)

            @block.scalar
            def _(h):
                body("act", h)

            @block.vector
            def _(h):
                body("dve", h)

            @block.gpsimd
            def _(h):
                body("pool", h)

            @block.sync
            def _(h):
                body("sp", h)


def _rope_tables():
    t = np.arange(S)
    row = (t // 64).astype(np.float32)
    col = (t % 64).astype(np.float32)
    inv = np.power(np.float32(10000.0), -np.arange(16, dtype=np.float32) * np.float32(2.0) / np.float32(32.0)).astype(np.float32)
    cosT = np.zeros((128, S), np.float32)
    sinT = np.zeros((128, S), np.float32)
    perm = np.zeros((128, 128), np.float32)
    for p in range(128):
        d = p % 64
        sec = d // 32
        dd = d % 32
        f = dd % 16
        pos = row if sec == 0 else col
        ang = (pos * inv[f]).astype(np.float32)
        cosT[p] = np.cos(ang)
        sn = np.sin(ang)
        if dd < 16:
            sinT[p] = -sn
            src = p + 16
        else:
            sinT[p] = sn
            src = p - 16
        perm[src, p] = 1.0
    return cosT, sinT, perm


def _a_tables(rpb):
    NEG = np.float32(-30000.0)
    kc = np.arange(64)[:, None]
    c = np.arange(64)[None, :]
    cs = np.clip(c - 8, 0, 48)
    colvalid = (kc >= cs) & (kc < cs + 16)
    dc = np.clip(kc - c + 15, 0, 30)
    out = np.full((16, 2, 64, 2, NE, 64), NEG, np.float32)
    for a in range(2):
        for e in range(NE):
            dr = 17 + a - e
            if 0 <= dr <= 14:
                blk = np.where(colvalid[None], rpb[:, dr][:, dc], NEG)
                out[:, a, :, 1, e, :] = blk
                if 3 <= dr <= 10:
                    out[:, a, :, 0, e, :] = blk
    return np.ascontiguousarray(out.reshape(16, 128, 2 * NE * 64))


def _pc(w):
    K = w.shape[0] // 128
    return np.ascontiguousarray(w.reshape(K, 128, w.shape[1]).transpose(1, 0, 2).reshape(128, K * w.shape[1]))


def _vec_pc(g):
    return np.ascontiguousarray(g.reshape(-1, 128).T)


def a_tiles():
    res = []
    for m in range(8):
        if m == 0:
            js = list(range(0, 6))
        elif m == 7:
            js = list(range(26, 32))
        else:
            js = list(range(4 * m - 2, 4 * m + 6))
        res.append((m, js))
    return res


def a_segments(m, j):
    if m == 0 and j <= 3:
        return [(0, 5, 1), (5, 8, 0)]
    if m == 7 and j >= 28:
        return [(0, 5, 0), (5, 8, 1)]
    return [(0, 8, 0)]


def build(n_layers=DEPTH, dbg=None):
    nc = bass.Bass("TRN2", target_bir_lowering=False)
    s = Sched(nc)

    def din(name, shape, dt=F32, pad=False):
        if pad:
            t = nc.dram_tensor(name, [shape[0] + 1] + list(shape[1:]), dt, kind="ExternalInput").ap()
            return t[0:shape[0]]
        return nc.dram_tensor(name, list(shape), dt, kind="ExternalInput").ap()

    x_in = din("x", [S, D])
    p_in = din("p", [DEPTH, S, 256])
    gv_in = din("gv", [128, 80])
    ident_in = din("ident", [128, 128])
    perm_in = din("perm", [128, 128])
    cos_in = din("cosT", [128, S], pad=True)
    sin_in = din("sinT", [128, S], pad=True)
    a_in = [din("a_in%d" % j, [8, 128, 8 * 512], pad=True) for j in range(2)]
    b_in = [din("b_in%d" % j, [4, 128, 8 * 640], pad=True) for j in range(2)]
    wo_in = [din("wo%d" % i, [128, 8 * 1024], pad=True) for i in range(DEPTH)]
    wg_in = [din("wg%d" % i, [128, 8 * 1024], pad=True) for i in range(DEPTH)]
    wp_in = [din("wp%d" % i, [128, 2 * 1024], pad=True) for i in range(DEPTH)]
    tb_in = [din("tb%d" % j, [16, 128, 2 * NE * 64], pad=True) for j in range(2)]
    out = nc.dram_tensor("out", [S, D], F32, kind="ExternalOutput").ap()
    x_scr = nc.dram_tensor("x_scr", [128, 8, S], F32, kind="Internal").ap()
    o_scr = nc.dram_tensor("o_scr", [128, 8, S], BF16, kind="Internal").ap()
    R_xscr = [Res("xscr%d" % t) for t in range(NT)]
    R_oscr = [Res("oscr%d" % c) for c in range(8)]
    R_out = [Res("out%d" % t) for t in range(NT)]
    dbg_out = None
    if dbg:
        dbg_out = nc.dram_tensor("dbg", [128, 8, S], F32 if dbg[0] == "x" else BF16, kind="ExternalOutput").ap()

    hT = nc.alloc_sbuf_tensor("sb_hT", [128, 8, S], BF16)
    R_h = [[Res("h%d_%d" % (t, c)) for c in range(8)] for t in range(NT)]
    ident = nc.alloc_sbuf_tensor("sb_ident", [128, 128], F32)
    perm = nc.alloc_sbuf_tensor("sb_perm", [128, 128], F32)
    gv = nc.alloc_sbuf_tensor("sb_gv", [128, 80], F32)
    ones_bf = nc.alloc_sbuf_tensor("sb_ones_bf", [128, 128], BF16)
    ones_f = nc.alloc_sbuf_tensor("sb_ones_f", [128, 128], F32)
    bones = nc.alloc_sbuf_tensor("sb_bones", [128, 128], BF16)
    epsc = nc.alloc_sbuf_tensor("sb_epsc", [128, 1], F32)
    R_c = Res("consts")
    ps = nc.alloc_psum_tensor("ps", [128, 8 * 512], F32)
    R_b = [Res("bank%d" % b) for b in range(8)]

    def bank(b, lo=0, hi=512):
        return ps[:, b * 512 + lo:b * 512 + hi]

    s.dma("sp", ident[:, :], ident_in[:, :], writes=[R_c])
    s.dma("sp", perm[:, :], perm_in[:, :], writes=[R_c])
    s.dma("sp", gv[:, :], gv_in[:, :], writes=[R_c])
    s.op("pool", lambda e: e.memset(ones_bf[:, :], 1.0), writes=[R_c])
    s.op("pool", lambda e: e.memset(ones_f[:, :], 1.0), writes=[R_c])
    s.op("pool", lambda e: e.memset(bones[:, :], 0.0), writes=[R_c])
    s.op("pool", lambda e: e.memset(bones[0:64, 0:64], 1.0), writes=[R_c])
    s.op("pool", lambda e: e.memset(bones[64:128, 64:128], 1.0), writes=[R_c])
    s.op("pool", lambda e: e.memset(epsc[:, :], EPS), writes=[R_c])
    s.barrier()

    def g_norm(i):
        return 16 * i

    def g_ple(i):
        return 16 * i + 8

    G_FINAL = 64
    G_BQ = 72
    G_BK = 74

    bank_rr = [0]

    def next_bank():
        b = bank_rr[0]
        bank_rr[0] = (b + 1) % 8
        return b

    def emit_rstd(xt, R_xt, sq, R_sq, sd, R_sd, width_scale):
        s.op("act", lambda e: e.activation(out=sq[:, :, :], in_=xt[:, :, :], func=AF.Square),
             reads=R_xt, writes=[R_sq])
        b = next_bank()

        def mm(e):
            for c in range(8):
                ins = e.matmul(bank(b), ones_bf[:, :], sq[:, c, :], start=(c == 0), stop=(c == 7))
            return ins
        s.op("pe", mm, reads=[R_sq, R_c], writes=[R_b[b]])
        s.op("act", lambda e: e.activation(out=sd[:, :], in_=bank(b), func=AF.Sqrt, scale=width_scale, bias=epsc[:, 0:1]),
             reads=[R_b[b], R_c], writes=[R_sd])
        s.op("dve", lambda e: e.reciprocal(out=sd[:, :], in_=sd[:, :]), reads=[R_sd], writes=[R_sd])

    def emit_h(xt, R_xt, sq, R_sq, sd, R_sd, tt, gcol):
        emit_rstd(xt, R_xt, sq, R_sq, sd, R_sd, 1.0 / D)
        for c in range(8):
            eng = "dve"
            s.op(eng, lambda e, c=c: e.scalar_tensor_tensor(
                out=hT[:, c, tt * TT:(tt + 1) * TT], in0=xt[:, c, :], scalar=gv[:, gcol + c:gcol + c + 1],
                in1=sd[:, :], op0=ALU.mult, op1=ALU.mult),
                reads=[R_xt[c], R_sd, R_c], writes=[R_h[tt][c]])

    with contextlib.ExitStack() as st:
        xin = [st.enter_context(nc.sbuf_tensor("sT_xin%d" % i, [128, 4, D], F32)) for i in range(2)]
        R_xin = [Res("xin%d" % i) for i in range(2)]
        xt = [st.enter_context(nc.sbuf_tensor("xtT%d" % i, [128, 8, TT], F32)) for i in range(2)]
        R_xt = [[Res("xtT%d_%d" % (i, c)) for c in range(8)] for i in range(2)]
        sq = st.enter_context(nc.sbuf_tensor("sqT", [128, 8, TT], BF16))
        R_sq = Res("sqT")
        sd = st.enter_context(nc.sbuf_tensor("sdT", [128, TT], F32))
        R_sd = Res("sdT")
        xv = x_in.rearrange("(t b p) d -> t p b d", b=4, p=128)
        for tt in range(NT):
            k = tt % 2
            s.dma("sp", xin[k][:, :, :], xv[tt], writes=[R_xin[k]])
            for c in range(8):
                b = next_bank()

                def tr(e, c=c, b=b, k=k):
                    for tb in range(4):
                        ins = e.transpose(out=bank(b, tb * 128, (tb + 1) * 128), in_=xin[k][:, tb, c * 128:(c + 1) * 128],
                                          identity=ident[:, :])
                    return ins
                s.op("pe", tr, reads=[R_xin[k], R_c], writes=[R_b[b]])
                if c % 2 == 0:
                    s.op("dve", lambda e, c=c, b=b, k=k: e.tensor_copy(out=xt[k][:, c, :], in_=bank(b)),
                         reads=[R_b[b]], writes=[R_xt[k][c]])
                else:
                    s.op("act", lambda e, c=c, b=b, k=k: e.activation(out=xt[k][:, c, :], in_=bank(b), func=AF.Copy),
                         reads=[R_b[b]], writes=[R_xt[k][c]])
            s.dma("sp", x_scr[:, :, tt * TT:(tt + 1) * TT], xt[k][:, :, :], reads=R_xt[k], writes=[R_xscr[tt]])
            emit_h(xt[k], R_xt[k], sq, R_sq, sd, R_sd, tt, g_norm(0))
        s.barrier()

    def finalize_a(a, ob, rrow, R_rrow):
        dp = 64 if a == 0 else 0
        s.op("dve", lambda e: e.reciprocal(out=rrow[dp:dp + 1, :], in_=bank(ob)[dp:dp + 1, :]),
             reads=[R_b[ob]], writes=[R_rrow])

    def finalize_b(a, ob, obc, sg, R_sg_t, qt, rrow, R_rrow, tmp, R_tmp, ochunk, R_ochunk):
        dp = 64 if a == 0 else 0
        lo = 64 * a
        s.op("pe", lambda e: e.matmul(bank(obc), ones_f[dp:dp + 1, :], rrow[dp:dp + 1, :], start=True, stop=True),
             reads=[R_rrow, R_c], writes=[R_b[obc]])
        s.op("dve", lambda e: e.tensor_tensor(out=tmp[lo:lo + 64, :], in0=bank(ob)[lo:lo + 64, :],
                                              in1=sg[lo:lo + 64, qt * TT:(qt + 1) * TT], op=ALU.mult),
             reads=[R_b[ob], R_sg_t], writes=[R_tmp])
        s.op("dve", lambda e: e.tensor_tensor(out=ochunk[lo:lo + 64, qt * TT:(qt + 1) * TT], in0=tmp[lo:lo + 64, :],
                                              in1=bank(obc)[lo:lo + 64, :], op=ALU.mult),
             reads=[R_tmp, R_b[obc]], writes=[R_ochunk])

    def phase_ga(i):
        j_l = i // 2
        with contextlib.ExitStack() as st:
            al = lambda n, sh, dt: st.enter_context(nc.sbuf_tensor("s%d_%s" % (i, n), sh, dt))
            wst = al("wst", [128, 8, 512], F32)
            R_wst = Res("wst")
            wbf = [al("wbf%d" % k, [128, 8, 512], BF16) for k in range(2)]
            R_wbf = [Res("wbf%d" % k) for k in range(2)]
            qT = al("qT", [128, S], BF16)
            kz = [al("kz%d" % a, [128, S], BF16) for a in range(2)]
            R_kz = Res("kzinit")
            s.op("pool", lambda e: e.memset(kz[0][64:128, :], 0.0), writes=[R_kz])
            s.op("pool", lambda e: e.memset(kz[1][0:64, :], 0.0), writes=[R_kz])
            sg = al("sg", [128, S], BF16)
            R_q = [Res("q%d" % t) for t in range(NT)]
            R_k = [Res("k%d" % t) for t in range(NT)]
            R_sg = [Res("sg%d" % t) for t in range(NT)]
            vaug = [al("vaug%d" % a, [128, 32, 128], BF16) for a in range(2)]
            R_v = [Res("v%d" % t) for t in range(NT)]
            tst = al("tst", [128, 2 * NE * 64], F32)
            R_tst = Res("tst")
            tbf = [al("tbf%d" % k, [128, 2, NE, 64], BF16) for k in range(2)]
            R_tbf = [Res("tbf%d" % k) for k in range(2)]
            exs = [al("exs%d" % k, [128, 1024], BF16) for k in range(2)]
            R_exs = [Res("exs%d" % k) for k in range(2)]
            pt = [al("pt%d" % k, [128, 1024], BF16) for k in range(2)]
            R_pt = [Res("pt%d" % k) for k in range(2)]
            rrow = [al("rrow%d" % k, [128, TT], F32) for k in range(2)]
            R_rrow = [Res("rrow%d" % k) for k in range(2)]
            tmp = al("tmp", [128, TT], F32)
            R_tmp = Res("tmp")
            och = [al("och%d" % k, [128, S], BF16) for k in range(2)]
            R_och = [Res("och%d" % k) for k in range(2)]
            R_vinit = Res("vinit")
            s.op("pool", lambda e: e.memset(vaug[0][:, :, 64:128], 1.0), writes=[R_vinit])
            s.op("pool", lambda e: e.memset(vaug[1][:, :, 0:64], 1.0), writes=[R_vinit])

            def load_w(hp):
                k = hp % 2
                s.dma("sp", wst[:, :, :], a_in[j_l][hp].rearrange("p (c f) -> p c f", c=8), writes=[R_wst])
                s.op("pool", lambda e: e.tensor_copy(out=wbf[k][:, :, :], in_=wst[:, :, :]),
                     reads=[R_wst], writes=[R_wbf[k]])

            tcount = [0]

            def load_tab(h):
                k = tcount[0] % 2
                tcount[0] += 1
                s.dma("sp", tst[:, :], tb_in[j_l][h], writes=[R_tst])
                s.op("act", lambda e: e.activation(out=tbf[k][:, :, :, :].rearrange("p a e c -> p (a e c)"), in_=tst[:, :], func=AF.Exp),
                     reads=[R_tst], writes=[R_tbf[k]])
                return k

            load_w(0)
            for hp in range(8):
                wk = hp % 2
                w = wbf[wk]
                if hp + 1 < 8:
                    load_w(hp + 1)
                for tt in range(NT):
                    tsl = slice(tt * TT, (tt + 1) * TT)
                    for kind in range(3):
                        col = (0, 128, 384)[kind]
                        b = next_bank()

                        def mm(e, b=b, col=col, tsl=tsl, w=w):
                            for c in range(8):
                                ins = e.matmul(bank(b), w[:, c, col:col + 128], hT[:, c, tsl], start=(c == 0), stop=(c == 7))
                            return ins
                        s.op("pe", mm, reads=[R_wbf[wk]] + R_h[tt], writes=[R_b[b]])
                        if kind == 0:
                            s.op("dve", lambda e, b=b, tsl=tsl: e.tensor_copy(out=qT[:, tsl], in_=bank(b)),
                                 reads=[R_b[b]], writes=[R_q[tt]])
                        elif kind == 1:
                            s.op("dve", lambda e, b=b, tsl=tsl: e.tensor_copy(out=kz[0][0:64, tsl], in_=bank(b)[0:64, :]),
                                 reads=[R_b[b], R_kz], writes=[R_k[tt]])
                            s.op("dve", lambda e, b=b, tsl=tsl: e.tensor_copy(out=kz[1][64:128, tsl], in_=bank(b)[64:128, :]),
                                 reads=[R_b[b], R_kz], writes=[R_k[tt]])
                        else:
                            s.op("act", lambda e, b=b, tsl=tsl: e.activation(out=sg[:, tsl], in_=bank(b), func=AF.Silu),
                                 reads=[R_b[b]], writes=[R_sg[tt]])
                    b = next_bank()

                    def mmv(e, b=b, tt=tt, w=w):
                        for tcl in range(4):
                            t0 = tt * TT + tcl * 128
                            for c in range(8):
                                ins = e.matmul(bank(b, tcl * 128, (tcl + 1) * 128), hT[:, c, t0:t0 + 128], w[:, c, 256:384],
                                               start=(c == 0), stop=(c == 7))
                        return ins
                    s.op("pe", mmv, reads=[R_wbf[wk]] + R_h[tt], writes=[R_b[b]])
                    bv = bank(b).rearrange("p (t f) -> p t f", t=4)
                    s.op("dve", lambda e, bv=bv, tt=tt: e.tensor_copy(out=vaug[0][:, tt * 4:tt * 4 + 4, 0:64], in_=bv[:, :, 0:64]),
                         reads=[R_b[b], R_vinit], writes=[R_v[tt]])
                    s.op("dve", lambda e, bv=bv, tt=tt: e.tensor_copy(out=vaug[1][:, tt * 4:tt * 4 + 4, 64:128], in_=bv[:, :, 64:128]),
                         reads=[R_b[b], R_vinit], writes=[R_v[tt]])
                ok = hp % 2
                steps = []
                for a in range(2):
                    tk = load_tab(2 * hp + a)
                    for (m, js) in a_tiles():
                        prs = [js[z:z + 2] for z in range(0, len(js), 2)]
                        for pi_, pr in enumerate(prs):
                            steps.append((a, tk, m, pr, pi_ == 0, pi_ == len(prs) - 1))
                gq = [0]

                def a_qk(idx):
                    a, tk, m, pr, first, last = steps[idx]
                    sb0 = 2 * (idx % 2)
                    lo = 64 * a

                    def qk(e, pr=pr, sb0=sb0, m=m, lo=lo):
                        for z, j in enumerate(pr):
                            ins = e.matmul(bank(sb0 + z), kz[lo // 64][:, j * 128:(j + 1) * 128],
                                           qT[:, m * TT:(m + 1) * TT], start=True, stop=True)
                        return ins
                    s.op("pe", qk, reads=[R_k[jj // 4] for jj in pr] + [R_q[m]], writes=[R_b[sb0], R_b[sb0 + 1]])

                def a_exp(idx):
                    a, tk, m, pr, first, last = steps[idx]
                    sb0 = 2 * (idx % 2)
                    ek = idx % 2
                    s.op("act", lambda e, sb0=sb0, ek=ek: e.activation(
                        out=exs[ek][:, :], in_=ps[:, sb0 * 512:(sb0 + 2) * 512], func=AF.Exp, scale=0.125),
                        reads=[R_b[sb0], R_b[sb0 + 1]], writes=[R_exs[ek]])

                def a_mask(idx):
                    a, tk, m, pr, first, last = steps[idx]
                    ek = idx % 2
                    for z, j in enumerate(pr):
                        e0 = 10 - (2 * j - 8 * m)
                        for (b0, b1, tab) in a_segments(m, j):
                            s.op("dve", lambda e, z=z, e0=e0, b0=b0, b1=b1, tab=tab, ek=ek, tk=tk: e.tensor_tensor(
                                out=pt[ek][:, z * 512 + b0 * 64:z * 512 + b1 * 64],
                                in0=exs[ek][:, z * 512 + b0 * 64:z * 512 + b1 * 64],
                                in1=tbf[tk][:, tab, e0 + b0:e0 + b1, :].rearrange("p e c -> p (e c)"), op=ALU.mult),
                                reads=[R_exs[ek], R_tbf[tk]], writes=[R_pt[ek]])

                pend = []

                def a_pv(idx):
                    a, tk, m, pr, first, last = steps[idx]
                    ek = idx % 2
                    g_ = gq[0]
                    ob = 4 + (g_ % 2)

                    def pv(e, pr=pr, ek=ek, ob=ob, first=first, last=last, a=a):
                        for z, j in enumerate(pr):
                            ins = e.matmul(bank(ob), vaug[a][:, j, :], pt[ek][:, z * 512:(z + 1) * 512],
                                           start=(first and z == 0), stop=(last and z == len(pr) - 1))
                        return ins
                    s.op("pe", pv, reads=[R_pt[ek]] + [R_v[jj // 4] for jj in pr], writes=[R_b[ob]])
                    if last:
                        finalize_a(a, ob, rrow[g_ % 2], R_rrow[g_ % 2])
                        pend.append((idx + 2, (a, ob, 6 + (g_ % 2), sg, R_sg[m], m, rrow[g_ % 2], R_rrow[g_ % 2],
                                               tmp, R_tmp, och[ok], R_och[ok])))
                        gq[0] += 1

                a_qk(0)
                for idx in range(len(steps)):
                    a_exp(idx)
                    if idx + 1 < len(steps):
                        a_qk(idx + 1)
                    a_mask(idx)
                    a_pv(idx)
                    while pend and pend[0][0] <= idx:
                        finalize_b(*pend.pop(0)[1])
                while pend:
                    finalize_b(*pend.pop(0)[1])
                s.dma("sp", o_scr[:, hp, :], och[ok][:, :], reads=[R_och[ok]], writes=[R_oscr[hp]])
            s.barrier()

    def phase_gb(i):
        j_l = i // 2
        with contextlib.ExitStack() as st:
            al = lambda n, sh, dt: st.enter_context(nc.sbuf_tensor("s%d_%s" % (i, n), sh, dt))
            wst = al("wstb", [128, 2, 640], F32)
            R_wst = Res("wstb")
            wbf = [al("wbfb%d" % k, [128, 8, 704], BF16) for k in range(2)]
            R_wbf = [Res("wbfb%d" % k) for k in range(2)]
            qT = [al("qTb%d" % c, [128, S], BF16) for c in range(2)]
            kz = [al("kzb%d" % a, [128, S], BF16) for a in range(2)]
            R_kz = Res("kzinitb")
            s.op("pool", lambda e: e.memset(kz[0][64:128, :], 0.0), writes=[R_kz])
            s.op("pool", lambda e: e.memset(kz[1][0:64, :], 0.0), writes=[R_kz])
            sg = [al("sgb%d" % c, [128, S], BF16) for c in range(2)]
            R_q = [[Res("qb%d_%d" % (c, t)) for t in range(NT)] for c in range(2)]
            R_k = [Res("kb%d" % t) for t in range(NT)]
            R_sg = [[Res("sgb%d_%d" % (c, t)) for t in range(NT)] for c in range(2)]
            vaug = [al("vaugb%d" % a, [128, 32, 128], BF16) for a in range(2)]
            R_v = [Res("vb%d" % t) for t in range(NT)]
            cst = [al("cst%d" % k, [128, TT], F32) for k in range(2)]
            snt = [al("snt%d" % k, [128, TT], F32) for k in range(2)]
            R_cs = [Res("cs%d" % k) for k in range(2)]
            R_sn = [Res("sn%d" % k) for k in range(2)]
            sqb = [al("sqb%d" % k, [128, TT], BF16) for k in range(2)]
            R_sqb = [Res("sqb%d" % k) for k in range(2)]
            ub = [al("ub%d" % k, [128, TT], F32) for k in range(2)]
            R_ub = [Res("ub%d" % k) for k in range(2)]
            sdb = [al("sdb%d" % k, [128, TT], F32) for k in range(2)]
            R_sdb = [Res("sdb%d" % k) for k in range(2)]
            t1 = [al("t1b%d" % k, [128, TT], F32) for k in range(2)]
            R_t1 = [Res("t1b%d" % k) for k in range(2)]
            t2 = [al("t2b%d" % k, [128, TT], F32) for k in range(2)]
            R_t2 = [Res("t2b%d" % k) for k in range(2)]
            pt = [al("ptb%d" % k, [128, 1024], BF16) for k in range(3)]
            R_pt = [Res("ptb%d" % k) for k in range(3)]
            rrow = [al("rrowb%d" % k, [128, TT], F32) for k in range(2)]
            R_rrow = [Res("rrowb%d" % k) for k in range(2)]
            tmp = al("tmpb", [128, TT], F32)
            R_tmp = Res("tmpb")
            och = [al("ochb0", [128, S], BF16)] * 2
            R_och = [Res("ochb0")] * 2
            R_vinit = Res("vinitb")
            s.op("pool", lambda e: e.memset(vaug[0][:, :, 64:128], 1.0), writes=[R_vinit])
            s.op("pool", lambda e: e.memset(vaug[1][:, :, 0:64], 1.0), writes=[R_vinit])

            def load_w(g):
                k = g % 2
                wv = b_in[j_l][g].rearrange("p (c f) -> p c f", c=8)
                for hf in range(4):
                    cs_ = slice(2 * hf, 2 * hf + 2)
                    s.dma("sp", wst[:, :, :], wv[:, cs_, :], writes=[R_wst])
                    for (d0, d1, s0, s1) in ((0, 256, 0, 256), (256, 320, 256, 320), (320, 384, 256, 320), (384, 704, 320, 640)):
                        s.op("pool", lambda e, d0=d0, d1=d1, s0=s0, s1=s1, cs_=cs_: e.tensor_copy(
                            out=wbf[k][:, cs_, d0:d1], in_=wst[:, :, s0:s1]), reads=[R_wst], writes=[R_wbf[k]])

            cscount = [0]
            load_w(0)
            for g in range(4):
                wk = g % 2
                w = wbf[wk]
                if g + 1 < 4:
                    load_w(g + 1)
                for kind in range(3):
                    col = (0, 128, 256)[kind]
                    gcol = (G_BQ + j_l) if kind < 2 else (G_BK + j_l)
                    for tt in range(NT):
                        tsl = slice(tt * TT, (tt + 1) * TT)
                        ck = cscount[0] % 2
                        cscount[0] += 1
                        s.dma("sp", cst[ck][:, :], cos_in[:, tsl], writes=[R_cs[ck]])
                        s.dma("sp", snt[ck][:, :], sin_in[:, tsl], writes=[R_sn[ck]])
                        b = next_bank()

                        def mm(e, b=b, col=col, tsl=tsl, w=w):
                            for c in range(8):
                                ins = e.matmul(bank(b), w[:, c, col:col + 128], hT[:, c, tsl], start=(c == 0), stop=(c == 7))
                            return ins
                        s.op("pe", mm, reads=[R_wbf[wk]] + R_h[tt], writes=[R_b[b]])
                        s.op("act", lambda e, b=b, ck=ck: e.activation(out=sqb[ck][:, :], in_=bank(b), func=AF.Square),
                             reads=[R_b[b]], writes=[R_sqb[ck]])
                        s.op("act", lambda e, b=b, gcol=gcol, ck=ck: e.activation(out=ub[ck][:, :], in_=bank(b), func=AF.Copy,
                                                                                 scale=gv[:, gcol:gcol + 1]),
                             reads=[R_b[b], R_c], writes=[R_ub[ck]])
                        b2 = next_bank()
                        s.op("pe", lambda e, b2=b2, ck=ck: e.matmul(bank(b2), bones[:, :], sqb[ck][:, :], start=True, stop=True),
                             reads=[R_sqb[ck], R_c], writes=[R_b[b2]])
                        b3 = next_bank()
                        s.op("pe", lambda e, b3=b3, ck=ck: e.matmul(bank(b3), perm[:, :], ub[ck][:, :], start=True, stop=True),
                             reads=[R_ub[ck], R_c], writes=[R_b[b3]])
                        s.op("act", lambda e, b2=b2, ck=ck: e.activation(out=sdb[ck][:, :], in_=bank(b2), func=AF.Sqrt, scale=1.0 / 64,
                                                                          bias=epsc[:, 0:1]),
                             reads=[R_b[b2], R_c], writes=[R_sdb[ck]])
                        s.op("dve", lambda e, ck=ck: e.reciprocal(out=sdb[ck][:, :], in_=sdb[ck][:, :]), reads=[R_sdb[ck]], writes=[R_sdb[ck]])
                        s.op("pool", lambda e, ck=ck: e.tensor_tensor(out=t1[ck][:, :], in0=ub[ck][:, :], in1=cst[ck][:, :], op=ALU.mult),
                             reads=[R_ub[ck], R_cs[ck]], writes=[R_t1[ck]])
                        s.op("dve", lambda e, b3=b3, ck=ck: e.tensor_tensor(out=t2[ck][:, :], in0=bank(b3), in1=snt[ck][:, :], op=ALU.mult),
                             reads=[R_b[b3], R_sn[ck]], writes=[R_t2[ck]])
                        s.op("pool", lambda e, ck=ck: e.tensor_tensor(out=t1[ck][:, :], in0=t1[ck][:, :], in1=t2[ck][:, :], op=ALU.add),
                             reads=[R_t1[ck], R_t2[ck]], writes=[R_t1[ck]])
                        if kind < 2:
                            dst, R_dst = qT[kind][:, tsl], R_q[kind][tt]
                            s.op("dve", lambda e, dst=dst, ck=ck: e.tensor_tensor(out=dst, in0=t1[ck][:, :], in1=sdb[ck][:, :], op=ALU.mult),
                                 reads=[R_t1[ck], R_sdb[ck]], writes=[R_dst])
                        else:
                            for a_ in range(2):
                                s.op("dve", lambda e, a_=a_, tsl=tsl, ck=ck: e.tensor_tensor(
                                    out=kz[a_][64 * a_:64 * a_ + 64, tsl], in0=t1[ck][64 * a_:64 * a_ + 64, :],
                                    in1=sdb[ck][64 * a_:64 * a_ + 64, :], op=ALU.mult),
                                    reads=[R_t1[ck], R_sdb[ck], R_kz], writes=[R_k[tt]])
                for tt in range(NT):
                    tsl = slice(tt * TT, (tt + 1) * TT)
                    for c2 in range(2):
                        col = 448 + 128 * c2
                        b = next_bank()

                        def mm(e, b=b, col=col, tsl=tsl, w=w):
                            for c in range(8):
                                ins = e.matmul(bank(b), w[:, c, col:col + 128], hT[:, c, tsl], start=(c == 0), stop=(c == 7))
                            return ins
                        s.op("pe", mm, reads=[R_wbf[wk]] + R_h[tt], writes=[R_b[b]])
                        s.op("act", lambda e, b=b, tsl=tsl, c2=c2: e.activation(out=sg[c2][:, tsl], in_=bank(b), func=AF.Silu),
                             reads=[R_b[b]], writes=[R_sg[c2][tt]])
                    b = next_bank()

                    def mmv(e, b=b, tt=tt, w=w):
                        for tcl in range(4):
                            t0 = tt * TT + tcl * 128
                            for c in range(8):
                                ins = e.matmul(bank(b, tcl * 64, (tcl + 1) * 64), hT[:, c, t0:t0 + 128], w[:, c, 384:448],
                                               start=(c == 0), stop=(c == 7))
                        return ins
                    s.op("pe", mmv, reads=[R_wbf[wk]] + R_h[tt], writes=[R_b[b]])
                    bv = bank(b, 0, 256).rearrange("p (t f) -> p t f", t=4)
                    s.op("dve", lambda e, bv=bv, tt=tt: e.tensor_copy(out=vaug[0][:, tt * 4:tt * 4 + 4, 0:64], in_=bv),
                         reads=[R_b[b], R_vinit], writes=[R_v[tt]])
                    s.op("dve", lambda e, bv=bv, tt=tt: e.tensor_copy(out=vaug[1][:, tt * 4:tt * 4 + 4, 64:128], in_=bv),
                         reads=[R_b[b], R_vinit], writes=[R_v[tt]])
                steps = [(hi, qt, kp) for hi in range(4) for qt in range(NT) for kp in range(16)]
                pend = []
                gq = [0]

                def b_qk(idx):
                    hi, qt, kp = steps[idx]
                    c2, a = hi // 2, hi % 2
                    lo = 64 * a
                    sb0 = 2 * (idx % 2)

                    def qk(e, kp=kp, sb0=sb0, qt=qt, lo=lo, c2=c2):
                        for z in range(2):
                            j = 2 * kp + z
                            ins = e.matmul(bank(sb0 + z), kz[lo // 64][:, j * 128:(j + 1) * 128],
                                           qT[c2][:, qt * TT:(qt + 1) * TT], start=True, stop=True)
                        return ins
                    s.op("pe", qk, reads=[R_k[kp // 2], R_q[c2][qt]], writes=[R_b[sb0], R_b[sb0 + 1]])

                def b_exp(idx):
                    sb0 = 2 * (idx % 2)
                    ek = idx % 3
                    s.op("act", lambda e, sb0=sb0, ek=ek: e.activation(
                        out=pt[ek][:, :], in_=ps[:, sb0 * 512:(sb0 + 2) * 512], func=AF.Exp, scale=0.125),
                        reads=[R_b[sb0], R_b[sb0 + 1]], writes=[R_pt[ek]])

                def b_pv(idx):
                    hi, qt, kp = steps[idx]
                    c2, a = hi // 2, hi % 2
                    ek = idx % 3
                    g_ = gq[0]
                    ob = 4 + (g_ % 2)

                    def pv(e, kp=kp, ek=ek, ob=ob, a=a):
                        for z in range(2):
                            j = 2 * kp + z
                            ins = e.matmul(bank(ob), vaug[a][:, j, :], pt[ek][:, z * 512:(z + 1) * 512],
                                           start=(kp == 0 and z == 0), stop=(kp == 15 and z == 1))
                        return ins
                    s.op("pe", pv, reads=[R_pt[ek], R_v[kp // 2]], writes=[R_b[ob]])
                    if kp == 15:
                        chunk = 2 * g + c2
                        ok = chunk % 2
                        finalize_a(a, ob, rrow[g_ % 2], R_rrow[g_ % 2])
                        pend.append((idx + 2, (a, ob, 6 + (g_ % 2), sg[c2], R_sg[c2][qt], qt, rrow[g_ % 2], R_rrow[g_ % 2],
                                               tmp, R_tmp, och[ok], R_och[ok]), (chunk, ok) if (a == 1 and qt == NT - 1) else None))
                        gq[0] += 1

                def b_fin(item):
                    finalize_b(*item[1])
                    if item[2] is not None:
                        chunk, ok = item[2]
                        s.dma("sp", o_scr[:, chunk, :], och[ok][:, :], reads=[R_och[ok]], writes=[R_oscr[chunk]])

                b_qk(0)
                for idx in range(len(steps)):
                    b_exp(idx)
                    if idx + 1 < len(steps):
                        b_qk(idx + 1)
                    b_pv(idx)
                    while pend and pend[0][0] <= idx:
                        b_fin(pend.pop(0))
                while pend:
                    b_fin(pend.pop(0))
            s.barrier()

    def phase_o(i, last):
        with contextlib.ExitStack() as st:
            al = lambda n, sh, dt: st.enter_context(nc.sbuf_tensor("s%d_%s" % (i, n), sh, dt))
            wst = al("wsto", [128, 2, 1024], F32)
            R_wst = Res("wsto")
            wo = al("wo", [128, 8, 1024], BF16)
            wg = al("wg", [128, 8, 1024], BF16)
            wp = al("wp", [128, 2, 1024], BF16)
            R_w = Res("w_o")
            xt = [al("xtO%d" % k, [128, 8, TT], F32) for k in range(2)]
            R_xt = [[Res("xtO%d_%d" % (k, c)) for c in range(8)] for k in range(2)]
            ot = [al("otO%d" % k, [128, 8, TT], BF16) for k in range(2)]
            R_ot = [Res("otO%d" % k) for k in range(2)]
            pin = [al("pin%d" % k, [128, 4, 256], F32) for k in range(2)]
            R_pin = [Res("pin%d" % k) for k in range(2)]
            pT = al("pT", [128, 2, TT], BF16)
            R_pT = Res("pT")
            sq = al("sqO", [128, 8, TT], BF16)
            R_sq = Res("sqO")
            sd = al("sdO", [128, TT], F32)
            R_sd = Res("sdO")
            n1 = sq
            R_n1 = R_sq
            gs = [al("gs%d" % k, [128, TT], F32) for k in range(2)]
            R_gs = [Res("gs%d" % k) for k in range(2)]
            if last:
                yo = al("yo", [128, 4, D], F32)
                R_yo = Res("yo")
            for (wdst, wsrc, nch) in ((wo, wo_in[i], 8), (wg, wg_in[i], 8), (wp, wp_in[i], 2)):
                wv = wsrc.rearrange("p (c f) -> p c f", c=nch)
                for c0 in range(0, nch, 2):
                    s.dma("sp", wst[:, :, :], wv[:, c0:c0 + 2, :], writes=[R_wst])
                    s.op("pool", lambda e, wdst=wdst, c0=c0: e.tensor_copy(out=wdst[:, c0:c0 + 2, :], in_=wst[:, :, :]),
                         reads=[R_wst], writes=[R_w])
            pv_ = p_in[i].rearrange("(t b p) f -> t p b f", b=4, p=128)
            ov = out.rearrange("(t b p) d -> t p b d", b=4, p=128)

            def loads(tt):
                k = tt % 2
                tsl = slice(tt * TT, (tt + 1) * TT)
                s.dma("sp", xt[k][:, :, :], x_scr[:, :, tsl], reads=[R_xscr[tt]], writes=R_xt[k], sem_res=R_xt[k][0])
                s.dma("sp", ot[k][:, :, :], o_scr[:, :, tsl], reads=R_oscr, writes=[R_ot[k]])
                s.dma("sp", pin[k][:, :, :], pv_[tt], writes=[R_pin[k]])

            loads(0)
            for tt in range(NT):
                k = tt % 2
                tsl = slice(tt * TT, (tt + 1) * TT)
                if tt + 1 < NT:
                    loads(tt + 1)
                X = xt[k]
                for fc in range(8):
                    b = next_bank()

                    def mm(e, b=b, fc=fc, k=k):
                        for c in range(8):
                            ins = e.matmul(bank(b), wo[:, c, fc * 128:(fc + 1) * 128], ot[k][:, c, :], start=(c == 0), stop=(c == 7))
                        return ins
                    s.op("pe", mm, reads=[R_w, R_ot[k]], writes=[R_b[b]])
                    s.op("dve", lambda e, b=b, fc=fc, X=X: e.tensor_tensor(out=X[:, fc, :], in0=X[:, fc, :], in1=bank(b), op=ALU.add),
                         reads=[R_b[b], R_xt[k][fc]], writes=[R_xt[k][fc]])
                emit_rstd(X, R_xt[k], sq, R_sq, sd, R_sd, 1.0 / D)
                for c in range(8):
                    eng = "dve"
                    s.op(eng, lambda e, c=c, X=X: e.scalar_tensor_tensor(
                        out=n1[:, c, :], in0=X[:, c, :], scalar=gv[:, g_ple(i) + c:g_ple(i) + c + 1], in1=sd[:, :],
                        op0=ALU.mult, op1=ALU.mult), reads=[R_xt[k][c], R_sd, R_c], writes=[R_n1])
                for c2 in range(2):
                    b = next_bank()

                    def tr(e, b=b, c2=c2, k=k):
                        for tb in range(4):
                            ins = e.transpose(out=bank(b, tb * 128, (tb + 1) * 128), in_=pin[k][:, tb, c2 * 128:(c2 + 1) * 128],
                                              identity=ident[:, :])
                        return ins
                    s.op("pe", tr, reads=[R_pin[k], R_c], writes=[R_b[b]])
                    s.op("act", lambda e, b=b, c2=c2: e.activation(out=pT[:, c2, :], in_=bank(b), func=AF.Copy),
                         reads=[R_b[b]], writes=[R_pT])
                for fc in range(8):
                    b = next_bank()

                    def mmg(e, b=b, fc=fc):
                        for c in range(8):
                            ins = e.matmul(bank(b), wg[:, c, fc * 128:(fc + 1) * 128], n1[:, c, :], start=(c == 0), stop=(c == 7))
                        return ins
                    s.op("pe", mmg, reads=[R_w, R_n1], writes=[R_b[b]])
                    gk = fc % 2
                    s.op("act", lambda e, b=b, gk=gk: e.activation(out=gs[gk][:, :], in_=bank(b), func=AF.Sigmoid),
                         reads=[R_b[b]], writes=[R_gs[gk]])
                    b2 = next_bank()

                    def mmp(e, b2=b2, fc=fc):
                        for c in range(2):
                            ins = e.matmul(bank(b2), wp[:, c, fc * 128:(fc + 1) * 128], pT[:, c, :], start=(c == 0), stop=(c == 1))
                        return ins
                    s.op("pe", mmp, reads=[R_w, R_pT], writes=[R_b[b2]])
                    s.op("dve", lambda e, b2=b2, gk=gk: e.tensor_tensor(out=gs[gk][:, :], in0=gs[gk][:, :], in1=bank(b2), op=ALU.mult),
                         reads=[R_gs[gk], R_b[b2]], writes=[R_gs[gk]])
                    s.op("pool", lambda e, fc=fc, X=X, gk=gk: e.tensor_tensor(out=X[:, fc, :], in0=X[:, fc, :], in1=gs[gk][:, :], op=ALU.add),
                         reads=[R_gs[gk], R_xt[k][fc]], writes=[R_xt[k][fc]])
                if dbg == ("x", i):
                    s.dma("sp", dbg_out[:, :, tsl], X[:, :, :], reads=R_xt[k], writes=[R_out[tt]])
                if not last:
                    s.dma("sp", x_scr[:, :, tsl], X[:, :, :], reads=R_xt[k], writes=[R_xscr[tt]])
                    emit_h(X, R_xt[k], sq, R_sq, sd, R_sd, tt, g_norm(i + 1))
                else:
                    emit_rstd(X, R_xt[k], sq, R_sq, sd, R_sd, 1.0 / D)
                    for c in range(8):
                        eng = "dve"
                        s.op(eng, lambda e, c=c, X=X: e.scalar_tensor_tensor(
                            out=X[:, c, :], in0=X[:, c, :], scalar=gv[:, G_FINAL + c:G_FINAL + c + 1], in1=sd[:, :],
                            op0=ALU.mult, op1=ALU.mult), reads=[R_xt[k][c], R_sd, R_c], writes=[R_xt[k][c]])
                    xn = X
                    R_xn = R_xt[k]
                    for tb in range(4):
                        for cq in range(2):
                            b = next_bank()

                            def tr(e, b=b, tb=tb, cq=cq, xn=xn):
                                for cc in range(4):
                                    c = cq * 4 + cc
                                    ins = e.transpose(out=bank(b, cc * 128, (cc + 1) * 128), in_=xn[:, c, tb * 128:(tb + 1) * 128],
                                                      identity=ident[:, :])
                                return ins
                            s.op("pe", tr, reads=R_xn + [R_c], writes=[R_b[b]])
                            if cq == 0:
                                s.op("dve", lambda e, b=b, tb=tb: e.tensor_copy(out=yo[:, tb, 0:512], in_=bank(b)),
                                     reads=[R_b[b]], writes=[R_yo])
                            else:
                                s.op("act", lambda e, b=b, tb=tb: e.activation(out=yo[:, tb, 512:1024], in_=bank(b), func=AF.Copy),
                                     reads=[R_b[b]], writes=[R_yo])
                    s.dma("sp", ov[tt], yo[:, :, :], reads=[R_yo], writes=[R_out[tt]])
            s.barrier()

    for i in range(n_layers):
        if dbg == ("h", i):
            s.dma("sp", dbg_out[:, :, :], hT[:, :, :], reads=[r for t in R_h for r in t], writes=[R_out[0]])
            break
        if i % 2 == 0:
            phase_ga(i)
        else:
            phase_gb(i)
        if dbg == ("o", i):
            s.dma("sp", dbg_out[:, :, :], o_scr[:, :, :], reads=R_oscr, writes=[R_out[0]])
            break
        phase_o(i, last=(i == n_layers - 1))
    s.barrier()
    s.emit()
    return nc


_CACHE = {}
_PADDED = ("cosT", "sinT", "a_in", "b_in", "wo", "wg", "wp", "tb")


def _prep_shared(inp):
    f = lambda a: np.ascontiguousarray(np.asarray(a, dtype=np.float32))
    sh = {}
    cosT, sinT, perm = _rope_tables()
    sh["cosT"], sh["sinT"], sh["perm"] = cosT, sinT, perm
    sh["ident"] = np.eye(128, dtype=np.float32)
    gv = np.zeros((128, 80), np.float32)
    ng, pg = f(inp["norm_g"]), f(inp["ple_norm_g"])
    for i in range(DEPTH):
        gv[:, 16 * i:16 * i + 8] = _vec_pc(ng[i])
        gv[:, 16 * i + 8:16 * i + 16] = _vec_pc(pg[i])
    gv[:, 64:72] = _vec_pc(f(inp["final_norm_g"]))
    bq, bk = f(inp["b_q_norm"]), f(inp["b_k_norm"])
    for j in range(2):
        gv[:, 72 + j] = np.tile(bq[j], 2)
        gv[:, 74 + j] = np.tile(bk[j], 2)
    sh["gv"] = gv
    awi, bwi = f(inp["a_w_in"]), f(inp["b_w_in"])
    for j in range(2):
        W = awi[j]
        groups = []
        for hp in range(8):
            cols = np.concatenate([W[:, k * 1024 + hp * 128:k * 1024 + (hp + 1) * 128] for k in range(4)], axis=1)
            groups.append(_pc(cols))
        sh["a_in%d" % j] = np.ascontiguousarray(np.stack(groups))
        W = bwi[j]
        groups = []
        for g in range(4):
            cols = np.concatenate([W[:, 256 * g:256 * (g + 1)], W[:, 1024 + 64 * g:1024 + 64 * (g + 1)],
                                   W[:, 1280 + 64 * g:1280 + 64 * (g + 1)], W[:, 1536 + 256 * g:1536 + 256 * (g + 1)]], axis=1)
            groups.append(_pc(cols))
        sh["b_in%d" % j] = np.ascontiguousarray(np.stack(groups))
        sh["tb%d" % j] = _a_tables(f(inp["a_rpb"])[j])
    awo, bwo = f(inp["a_w_out"]), f(inp["b_w_out"])
    wgt, wpj = f(inp["ple_w_gate"]), f(inp["ple_w_proj"])
    for i in range(DEPTH):
        sh["wo%d" % i] = _pc(awo[i // 2] if i % 2 == 0 else bwo[i // 2])
        sh["wg%d" % i] = _pc(wgt[i])
        sh["wp%d" % i] = _pc(wpj[i])
    return sh


def _core_inputs(sh, xb, pb, b):
    m = {}
    for k_, v_ in sh.items():
        if k_ in _PADDED or k_[:-1] in _PADDED:
            pad = np.full((1,) + v_.shape[1:], float(b), np.float32)
            m[k_] = np.concatenate([v_, pad], axis=0)
        else:
            m[k_] = v_
    m["x"] = np.ascontiguousarray(xb)
    m["p"] = np.ascontiguousarray(pb)
    return m


def kernel(x, p, norm_g, a_w_in, a_rpb, a_w_out, b_w_in, b_q_norm, b_k_norm, b_w_out,
           ple_norm_g, ple_w_gate, ple_w_proj, final_norm_g):
    inp = dict(norm_g=norm_g, a_w_in=a_w_in, a_rpb=a_rpb, a_w_out=a_w_out, b_w_in=b_w_in, b_q_norm=b_q_norm,
               b_k_norm=b_k_norm, b_w_out=b_w_out, ple_norm_g=ple_norm_g, ple_w_gate=ple_w_gate,
               ple_w_proj=ple_w_proj, final_norm_g=final_norm_g)
    sh = _prep_shared(inp)
    x = np.asarray(x, dtype=np.float32)
    p = np.asarray(p, dtype=np.float32)
    if "nc" not in _CACHE:
        _CACHE["nc"] = build()
    nc = _CACHE["nc"]
    in_maps = [_core_inputs(sh, x[b], p[:, b], b) for b in range(8)]
    res = run_bass_kernel_spmd(nc, in_maps, core_ids=list(range(8)))
    return np.stack([np.asarray(r["out"], dtype=np.float32) for r in res.results], axis=0)
```

```python
import contextlib
import numpy as np
import ml_dtypes
import concourse.bass as bass
import concourse.mybir as mybir
from concourse.bass_utils import run_bass_kernel_spmd

F32 = mybir.dt.float32
BF16 = mybir.dt.bfloat16
AF = mybir.ActivationFunctionType
ALU = mybir.AluOpType

S = 4096
D = 1024
NT = 8
TT = 512
DEPTH = 4
EPS = 1e-6
NE = 22
ENGS = ("pe", "act", "dve", "pool", "sp")


class Res:
    __slots__ = ("name", "w", "r", "sem", "semval")

    def __init__(self, name):
        self.name = name
        self.w = None
        self.r = []
        self.sem = None
        self.semval = 0


class Sched:
    def __init__(self, nc):
        self.nc = nc
        self.ops = {e: [] for e in ENGS}
        self.known = {e: {} for e in ENGS}
        self.dma_sems = []

    def _need(self, eng, tok, waits):
        if tok is None:
            return
        if tok[0] == "e":
            _, f, idx = tok
            if f == eng and eng == "pe":
                return
            key = ("e", f)
            val = idx
        else:
            key = ("s", tok[1])
            val = tok[2]
        kn = self.known[eng]
        if kn.get(key, -1) >= val:
            return
        kn[key] = val
        waits.append(tok)
        if tok[0] == "e":
            self.ops[tok[1]][tok[2]][2] = True

    def _deps(self, eng, reads, writes):
        waits = []
        for r in reads:
            self._need(eng, r.w, waits)
        for w in writes:
            self._need(eng, w.w, waits)
            for t in w.r:
                self._need(eng, t, waits)
        return waits

    def _commit(self, tok, reads, writes):
        for r in reads:
            r.r.append(tok)
        for w in writes:
            w.w = tok
            w.r = []

    def op(self, eng, fn, reads=(), writes=()):
        waits = self._deps(eng, reads, writes)
        tok = ("e", eng, len(self.ops[eng]))
        self.ops[eng].append([fn, waits, False, None])
        self._commit(tok, reads, writes)
        return tok

    def dma(self, eng, out, in_, reads=(), writes=(), sem_res=None):
        waits = self._deps(eng, reads, writes)
        if sem_res is None:
            sem_res = writes[0]
        if sem_res.sem is None:
            sem_res.sem = len(self.dma_sems)
            self.dma_sems.append(sem_res)
        sem_res.semval += 16
        tok = ("s", sem_res.sem, sem_res.semval)

        def fn(e, out=out, in_=in_):
            return e.dma_start(out=out, in_=in_)

        self.ops[eng].append([fn, waits, False, sem_res.sem])
        self._commit(tok, reads, writes)
        return tok

    def barrier(self):
        last = {}
        for f in ENGS:
            for i in range(len(self.ops[f]) - 1, -1, -1):
                o = self.ops[f][i]
                if o[0] is not None and o[3] is None:
                    last[f] = ("e", f, i)
                    break
        for e in ENGS:
            waits = []
            for f in ENGS:
                if f != e and f in last:
                    self._need(e, last[f], waits)
            for r in self.dma_sems:
                self._need(e, ("s", r.sem, r.semval), waits)
            self.ops[e].append([None, waits, False, None])

    def emit(self):
        nc = self.nc
        val = {}
        for e in ENGS:
            c = 0
            for i, o in enumerate(self.ops[e]):
                if o[2]:
                    c += 1
                    val[(e, i)] = c
        with contextlib.ExitStack() as st:
            esem = {e: st.enter_context(nc.semaphore("prog_" + e)) for e in ENGS}
            dsem = [st.enter_context(nc.semaphore("dma_%d" % i)) for i in range(len(self.dma_sems))]
            block = st.enter_context(nc.Block())
            ops = self.ops

            def body(e, h):
                for fn, waits, needed, dinc in ops[e]:
                    for t in waits:
                        if t[0] == "e":
                            h.wait_ge(esem[t[1]], val[(t[1], t[2])])
                        else:
                            h.wait_ge(dsem[t[1]], t[2])
                    if fn is None:
                        continue
                    ins = fn(h)
                    if dinc is not None:
                        ins.then_inc(dsem[dinc], 16)
                    elif needed:
                        ins.then_inc(esem[e], 1)

            @block.tensor
            def _(h):
                body("pe", h)

            @block.scalar
            def _(h):
                body("act", h)

            @block.vector
            def _(h):
                body("dve", h)

            @block.gpsimd
            def _(h):
                body("pool", h)

            @block.sync
            def _(h):
                body("sp", h)


def _rope_tables():
    t = np.arange(S)
    row = (t // 64).astype(np.float32)
    col = (t % 64).astype(np.float32)
    inv = np.power(np.float32(10000.0), -np.arange(16, dtype=np.float32) * np.float32(2.0) / np.float32(32.0)).astype(np.float32)
    cosT = np.zeros((128, S), np.float32)
    sinT = np.zeros((128, S), np.float32)
    perm = np.zeros((128, 128), np.float32)
    for p in range(128):
        d = p % 64
        sec = d // 32
        dd = d % 32
        f = dd % 16
        pos = row if sec == 0 else col
        ang = (pos * inv[f]).astype(np.float32)
        cosT[p] = np.cos(ang)
        sn = np.sin(ang)
        if dd < 16:
            sinT[p] = -sn
            src = p + 16
        else:
            sinT[p] = sn
            src = p - 16
        perm[src, p] = 1.0
    return cosT, sinT, perm


def _a_tables(rpb):
    NEG = np.float32(-30000.0)
    kc = np.arange(64)[:, None]
    c = np.arange(64)[None, :]
    cs = np.clip(c - 8, 0, 48)
    colvalid = (kc >= cs) & (kc < cs + 16)
    dc = np.clip(kc - c + 15, 0, 30)
    out = np.full((16, 2, 64, 2, NE, 64), NEG, np.float32)
    for a in range(2):
        for e in range(NE):
            dr = 17 + a - e
            if 0 <= dr <= 14:
                blk = np.where(colvalid[None], rpb[:, dr][:, dc], NEG)
                out[:, a, :, 1, e, :] = blk
                if 3 <= dr <= 10:
                    out[:, a, :, 0, e, :] = blk
    return np.ascontiguousarray(out.reshape(16, 128, 2 * NE * 64))


def _pc(w):
    K = w.shape[0] // 128
    return np.ascontiguousarray(w.reshape(K, 128, w.shape[1]).transpose(1, 0, 2).reshape(128, K * w.shape[1]))


def _vec_pc(g):
    return np.ascontiguousarray(g.reshape(-1, 128).T)


def a_tiles():
    res = []
    for m in range(8):
        if m == 0:
            js = list(range(0, 6))
        elif m == 7:
            js = list(range(26, 32))
        else:
            js = list(range(4 * m - 2, 4 * m + 6))
        res.append((m, js))
    return res


def a_segments(m, j):
    if m == 0 and j <= 3:
        return [(0, 5, 1), (5, 8, 0)]
    if m == 7 and j >= 28:
        return [(0, 5, 0), (5, 8, 1)]
    return [(0, 8, 0)]


def build(n_layers=DEPTH, dbg=None):
    nc = bass.Bass("TRN2", target_bir_lowering=False)
    s = Sched(nc)

    def din(name, shape, dt=F32, pad=False):
        if pad:
            t = nc.dram_tensor(name, [shape[0] + 1] + list(shape[1:]), dt, kind="ExternalInput").ap()
            return t[0:shape[0]]
        return nc.dram_tensor(name, list(shape), dt, kind="ExternalInput").ap()

    x_in = din("x", [S, D])
    p_in = din("p", [DEPTH, S, 256])
    gv_in = din("gv", [128, 80])
    ident_in = din("ident", [128, 128])
    perm_in = din("perm", [128, 128])
    cos_in = din("cosT", [128, S], pad=True)
    sin_in = din("sinT", [128, S], pad=True)
    a_in = [din("a_in%d" % j, [8, 128, 8 * 512], pad=True) for j in range(2)]
    b_in = [din("b_in%d" % j, [4, 128, 8 * 640], pad=True) for j in range(2)]
    wo_in = [din("wo%d" % i, [128, 8 * 1024], pad=True) for i in range(DEPTH)]
    wg_in = [din("wg%d" % i, [128, 8 * 1024], pad=True) for i in range(DEPTH)]
    wp_in = [din("wp%d" % i, [128, 2 * 1024], pad=True) for i in range(DEPTH)]
    tb_in = [din("tb%d" % j, [16, 128, 2 * NE * 64], pad=True) for j in range(2)]
    out = nc.dram_tensor("out", [S, D], F32, kind="ExternalOutput").ap()
    x_scr = nc.dram_tensor("x_scr", [128, 8, S], F32, kind="Internal").ap()
    o_scr = nc.dram_tensor("o_scr", [128, 8, S], BF16, kind="Internal").ap()
    R_xscr = [Res("xscr%d" % t) for t in range(NT)]
    R_oscr = [Res("oscr%d" % c) for c in range(8)]
    R_out = [Res("out%d" % t) for t in range(NT)]
    dbg_out = None
    if dbg:
        dbg_out = nc.dram_tensor("dbg", [128, 8, S], F32 if dbg[0] == "x" else BF16, kind="ExternalOutput").ap()

    hT = nc.alloc_sbuf_tensor("sb_hT", [128, 8, S], BF16)
    R_h = [[Res("h%d_%d" % (t, c)) for c in range(8)] for t in range(NT)]
    ident = nc.alloc_sbuf_tensor("sb_ident", [128, 128], F32)
    perm = nc.alloc_sbuf_tensor("sb_perm", [128, 128], F32)
    gv = nc.alloc_sbuf_tensor("sb_gv", [128, 80], F32)
    ones_bf = nc.alloc_sbuf_tensor("sb_ones_bf", [128, 128], BF16)
    ones_f = nc.alloc_sbuf_tensor("sb_ones_f", [128, 128], F32)
    bones = nc.alloc_sbuf_tensor("sb_bones", [128, 128], BF16)
    epsc = nc.alloc_sbuf_tensor("sb_epsc", [128, 1], F32)
    R_c = Res("consts")
    ps = nc.alloc_psum_tensor("ps", [128, 8 * 512], F32)
    R_b = [Res("bank%d" % b) for b in range(8)]

    def bank(b, lo=0, hi=512):
        return ps[:, b * 512 + lo:b * 512 + hi]

    s.dma("sp", ident[:, :], ident_in[:, :], writes=[R_c])
    s.dma("sp", perm[:, :], perm_in[:, :], writes=[R_c])
    s.dma("sp", gv[:, :], gv_in[:, :], writes=[R_c])
    s.op("pool", lambda e: e.memset(ones_bf[:, :], 1.0), writes=[R_c])
    s.op("pool", lambda e: e.memset(ones_f[:, :], 1.0), writes=[R_c])
    s.op("pool", lambda e: e.memset(bones[:, :], 0.0), writes=[R_c])
    s.op("pool", lambda e: e.memset(bones[0:64, 0:64], 1.0), writes=[R_c])
    s.op("pool", lambda e: e.memset(bones[64:128, 64:128], 1.0), writes=[R_c])
    s.op("pool", lambda e: e.memset(epsc[:, :], EPS), writes=[R_c])
    s.barrier()

    def g_norm(i):
        return 16 * i

    def g_ple(i):
        return 16 * i + 8

    G_FINAL = 64
    G_BQ = 72
    G_BK = 74

    bank_rr = [0]

    def next_bank():
        b = bank_rr[0]
        bank_rr[0] = (b + 1) % 8
        return b

    def emit_rstd(xt, R_xt, sq, R_sq, sd, R_sd, width_scale):
        s.op("act", lambda e: e.activation(out=sq[:, :, :], in_=xt[:, :, :], func=AF.Square),
             reads=R_xt, writes=[R_sq])
        b = next_bank()

        def mm(e):
            for c in range(8):
                ins = e.matmul(bank(b), ones_bf[:, :], sq[:, c, :], start=(c == 0), stop=(c == 7))
            return ins
        s.op("pe", mm, reads=[R_sq, R_c], writes=[R_b[b]])
        s.op("act", lambda e: e.activation(out=sd[:, :], in_=bank(b), func=AF.Ln, scale=width_scale, bias=epsc[:, 0:1]),
             reads=[R_b[b], R_c], writes=[R_sd])
        s.op("act", lambda e: e.activation(out=sd[:, :], in_=sd[:, :], func=AF.Exp, scale=-0.5), reads=[R_sd], writes=[R_sd])

    def emit_h(xt, R_xt, sq, R_sq, sd, R_sd, tt, gcol):
        emit_rstd(xt, R_xt, sq, R_sq, sd, R_sd, 1.0 / D)
        for c in range(8):
            eng = "dve"
            s.op(eng, lambda e, c=c: e.scalar_tensor_tensor(
                out=hT[:, c, tt * TT:(tt + 1) * TT], in0=xt[:, c, :], scalar=gv[:, gcol + c:gcol + c + 1],
                in1=sd[:, :], op0=ALU.mult, op1=ALU.mult),
                reads=[R_xt[c], R_sd, R_c], writes=[R_h[tt][c]])

    with contextlib.ExitStack() as st:
        xin = [st.enter_context(nc.sbuf_tensor("sT_xin%d" % i, [128, 4, D], F32)) for i in range(2)]
        R_xin = [Res("xin%d" % i) for i in range(2)]
        xt = [st.enter_context(nc.sbuf_tensor("xtT%d" % i, [128, 8, TT], F32)) for i in range(2)]
        R_xt = [[Res("xtT%d_%d" % (i, c)) for c in range(8)] for i in range(2)]
        sq = st.enter_context(nc.sbuf_tensor("sqT", [128, 8, TT], BF16))
        R_sq = Res("sqT")
        sd = st.enter_context(nc.sbuf_tensor("sdT", [128, TT], F32))
        R_sd = Res("sdT")
        xv = x_in.rearrange("(t b p) d -> t p b d", b=4, p=128)
        for tt in range(NT):
            k = tt % 2
            s.dma("sp", xin[k][:, :, :], xv[tt], writes=[R_xin[k]])
            for c in range(8):
                b = next_bank()

                def tr(e, c=c, b=b, k=k):
                    for tb in range(4):
                        ins = e.transpose(out=bank(b, tb * 128, (tb + 1) * 128), in_=xin[k][:, tb, c * 128:(c + 1) * 128],
                                          identity=ident[:, :])
                    return ins
                s.op("pe", tr, reads=[R_xin[k], R_c], writes=[R_b[b]])
                if c % 2 == 0:
                    s.op("dve", lambda e, c=c, b=b, k=k: e.tensor_copy(out=xt[k][:, c, :], in_=bank(b)),
                         reads=[R_b[b]], writes=[R_xt[k][c]])
                else:
                    s.op("act", lambda e, c=c, b=b, k=k: e.activation(out=xt[k][:, c, :], in_=bank(b), func=AF.Copy),
                         reads=[R_b[b]], writes=[R_xt[k][c]])
            s.dma("sp", x_scr[:, :, tt * TT:(tt + 1) * TT], xt[k][:, :, :], reads=R_xt[k], writes=[R_xscr[tt]])
            emit_h(xt[k], R_xt[k], sq, R_sq, sd, R_sd, tt, g_norm(0))
        s.barrier()

    def finalize_a(a, ob, rrow, R_rrow):
        dp = 64 if a == 0 else 0
        s.op("act", lambda e: e.activation(out=rrow[dp:dp + 1, :], in_=bank(ob)[dp:dp + 1, :], func=AF.Ln),
             reads=[R_b[ob]], writes=[R_rrow])
        s.op("act", lambda e: e.activation(out=rrow[dp:dp + 1, :], in_=rrow[dp:dp + 1, :], func=AF.Exp, scale=-1.0),
             reads=[R_rrow], writes=[R_rrow])

    def finalize_b(a, ob, obc, sg, R_sg_t, qt, rrow, R_rrow, tmp, R_tmp, ochunk, R_ochunk):
        dp = 64 if a == 0 else 0
        lo = 64 * a
        s.op("pe", lambda e: e.matmul(bank(obc), ones_f[dp:dp + 1, :], rrow[dp:dp + 1, :], start=True, stop=True),
             reads=[R_rrow, R_c], writes=[R_b[obc]])
        s.op("dve", lambda e: e.tensor_tensor(out=tmp[lo:lo + 64, :], in0=bank(ob)[lo:lo + 64, :],
                                              in1=sg[lo:lo + 64, qt * TT:(qt + 1) * TT], op=ALU.mult),
             reads=[R_b[ob], R_sg_t], writes=[R_tmp])
        s.op("dve", lambda e: e.tensor_tensor(out=ochunk[lo:lo + 64, qt * TT:(qt + 1) * TT], in0=tmp[lo:lo + 64, :],
                                              in1=bank(obc)[lo:lo + 64, :], op=ALU.mult),
             reads=[R_tmp, R_b[obc]], writes=[R_ochunk])

    def phase_ga(i):
        j_l = i // 2
        with contextlib.ExitStack() as st:
            al = lambda n, sh, dt: st.enter_context(nc.sbuf_tensor("s%d_%s" % (i, n), sh, dt))
            wst = al("wst", [128, 8, 512], F32)
            R_wst = Res("wst")
            wbf = [al("wbf%d" % k, [128, 8, 512], BF16) for k in range(2)]
            R_wbf = [Res("wbf%d" % k) for k in range(2)]
            qT = al("qT", [128, S], BF16)
            kz = [al("kz%d" % a, [128, S], BF16) for a in range(2)]
            R_kz = Res("kzinit")
            s.op("dve", lambda e: e.memset(kz[0][64:128, :], 0.0), writes=[R_kz])
            s.op("dve", lambda e: e.memset(kz[1][0:64, :], 0.0), writes=[R_kz])
            sg = al("sg", [128, S], BF16)
            R_q = [Res("q%d" % t) for t in range(NT)]
            R_k = [Res("k%d" % t) for t in range(NT)]
            R_sg = [Res("sg%d" % t) for t in range(NT)]
            vaug = [al("vaug%d" % a, [128, 32, 128], BF16) for a in range(2)]
            R_v = [Res("v%d" % t) for t in range(NT)]
            tst = al("tst", [128, 2 * NE * 64], F32)
            R_tst = Res("tst")
            tbf = [al("tbf%d" % k, [128, 2, NE, 64], BF16) for k in range(2)]
            R_tbf = [Res("tbf%d" % k) for k in range(2)]
            exs = [al("exs%d" % k, [128, 1024], BF16) for k in range(2)]
            R_exs = [Res("exs%d" % k) for k in range(2)]
            pt = [al("pt%d" % k, [128, 1024], BF16) for k in range(2)]
            R_pt = [Res("pt%d" % k) for k in range(2)]
            rrow = [al("rrow%d" % k, [128, TT], F32) for k in range(2)]
            R_rrow = [Res("rrow%d" % k) for k in range(2)]
            tmp = al("tmp", [128, TT], F32)
            R_tmp = Res("tmp")
            och = [al("och%d" % k, [128, S], BF16) for k in range(2)]
            R_och = [Res("och%d" % k) for k in range(2)]
            R_vinit = Res("vinit")
            s.op("dve", lambda e: e.memset(vaug[0][:, :, 64:128], 1.0), writes=[R_vinit])
            s.op("dve", lambda e: e.memset(vaug[1][:, :, 0:64], 1.0), writes=[R_vinit])

            def load_w(hp):
                k = hp % 2
                s.dma("sp", wst[:, :, :], a_in[j_l][hp].rearrange("p (c f) -> p c f", c=8), writes=[R_wst])
                s.op("pool", lambda e: e.tensor_copy(out=wbf[k][:, :, :], in_=wst[:, :, :]),
                     reads=[R_wst], writes=[R_wbf[k]])

            tcount = [0]

            def load_tab(h):
                k = tcount[0] % 2
                tcount[0] += 1
                s.dma("sp", tst[:, :], tb_in[j_l][h], writes=[R_tst])
                s.op("act", lambda e: e.activation(out=tbf[k][:, :, :, :].rearrange("p a e c -> p (a e c)"), in_=tst[:, :], func=AF.Exp),
                     reads=[R_tst], writes=[R_tbf[k]])
                return k

            load_w(0)
            for hp in range(8):
                wk = hp % 2
                w = wbf[wk]
                for tt in range(NT):
                    tsl = slice(tt * TT, (tt + 1) * TT)
                    for kind in range(3):
                        col = (0, 128, 384)[kind]
                        b = next_bank()

                        def mm(e, b=b, col=col, tsl=tsl, w=w):
                            for c in range(8):
                                ins = e.matmul(bank(b), w[:, c, col:col + 128], hT[:, c, tsl], start=(c == 0), stop=(c == 7))
                            return ins
                        s.op("pe", mm, reads=[R_wbf[wk]] + R_h[tt], writes=[R_b[b]])
                        if kind == 0:
                            s.op("dve", lambda e, b=b, tsl=tsl: e.tensor_copy(out=qT[:, tsl], in_=bank(b)),
                                 reads=[R_b[b]], writes=[R_q[tt]])
                        elif kind == 1:
                            s.op("dve", lambda e, b=b, tsl=tsl: e.tensor_copy(out=kz[0][0:64, tsl], in_=bank(b)[0:64, :]),
                                 reads=[R_b[b], R_kz], writes=[R_k[tt]])
                            s.op("dve", lambda e, b=b, tsl=tsl: e.tensor_copy(out=kz[1][64:128, tsl], in_=bank(b)[64:128, :]),
                                 reads=[R_b[b], R_kz], writes=[R_k[tt]])
                        else:
                            s.op("act", lambda e, b=b, tsl=tsl: e.activation(out=sg[:, tsl], in_=bank(b), func=AF.Silu),
                                 reads=[R_b[b]], writes=[R_sg[tt]])
                    b = next_bank()

                    def mmv(e, b=b, tt=tt, w=w):
                        for tcl in range(4):
                            t0 = tt * TT + tcl * 128
                            for c in range(8):
                                ins = e.matmul(bank(b, tcl * 128, (tcl + 1) * 128), hT[:, c, t0:t0 + 128], w[:, c, 256:384],
                                               start=(c == 0), stop=(c == 7))
                        return ins
                    s.op("pe", mmv, reads=[R_wbf[wk]] + R_h[tt], writes=[R_b[b]])
                    bv = bank(b).rearrange("p (t f) -> p t f", t=4)
                    s.op("dve", lambda e, bv=bv, tt=tt: e.tensor_copy(out=vaug[0][:, tt * 4:tt * 4 + 4, 0:64], in_=bv[:, :, 0:64]),
                         reads=[R_b[b], R_vinit], writes=[R_v[tt]])
                    s.op("dve", lambda e, bv=bv, tt=tt: e.tensor_copy(out=vaug[1][:, tt * 4:tt * 4 + 4, 64:128], in_=bv[:, :, 64:128]),
                         reads=[R_b[b], R_vinit], writes=[R_v[tt]])
                if hp + 1 < 8:
                    load_w(hp + 1)
                ok = hp % 2
                steps = []
                for a in range(2):
                    tk = load_tab(2 * hp + a)
                    for (m, js) in a_tiles():
                        prs = [js[z:z + 2] for z in range(0, len(js), 2)]
                        for pi_, pr in enumerate(prs):
                            steps.append((a, tk, m, pr, pi_ == 0, pi_ == len(prs) - 1))
                gq = [0]

                def a_qk(idx):
                    a, tk, m, pr, first, last = steps[idx]
                    sb0 = 2 * (idx % 2)
                    lo = 64 * a

                    def qk(e, pr=pr, sb0=sb0, m=m, lo=lo):
                        for z, j in enumerate(pr):
                            ins = e.matmul(bank(sb0 + z), kz[lo // 64][:, j * 128:(j + 1) * 128],
                                           qT[:, m * TT:(m + 1) * TT], start=True, stop=True)
                        return ins
                    s.op("pe", qk, reads=[R_k[jj // 4] for jj in pr] + [R_q[m]], writes=[R_b[sb0], R_b[sb0 + 1]])

                def a_exp(idx):
                    a, tk, m, pr, first, last = steps[idx]
                    sb0 = 2 * (idx % 2)
                    ek = idx % 2
                    s.op("act", lambda e, sb0=sb0, ek=ek: e.activation(
                        out=exs[ek][:, :], in_=ps[:, sb0 * 512:(sb0 + 2) * 512], func=AF.Exp, scale=0.125),
                        reads=[R_b[sb0], R_b[sb0 + 1]], writes=[R_exs[ek]])

                def a_mask(idx):
                    a, tk, m, pr, first, last = steps[idx]
                    ek = idx % 2
                    for z, j in enumerate(pr):
                        e0 = 10 - (2 * j - 8 * m)
                        for (b0, b1, tab) in a_segments(m, j):
                            s.op("dve", lambda e, z=z, e0=e0, b0=b0, b1=b1, tab=tab, ek=ek, tk=tk: e.tensor_tensor(
                                out=pt[ek][:, z * 512 + b0 * 64:z * 512 + b1 * 64],
                                in0=exs[ek][:, z * 512 + b0 * 64:z * 512 + b1 * 64],
                                in1=tbf[tk][:, tab, e0 + b0:e0 + b1, :].rearrange("p e c -> p (e c)"), op=ALU.mult),
                                reads=[R_exs[ek], R_tbf[tk]], writes=[R_pt[ek]])

                pend = []

                def a_pv(idx):
                    a, tk, m, pr, first, last = steps[idx]
                    ek = idx % 2
                    g_ = gq[0]
                    ob = 4 + (g_ % 2)

                    def pv(e, pr=pr, ek=ek, ob=ob, first=first, last=last, a=a):
                        for z, j in enumerate(pr):
                            ins = e.matmul(bank(ob), vaug[a][:, j, :], pt[ek][:, z * 512:(z + 1) * 512],
                                           start=(first and z == 0), stop=(last and z == len(pr) - 1))
                        return ins
                    s.op("pe", pv, reads=[R_pt[ek]] + [R_v[jj // 4] for jj in pr], writes=[R_b[ob]])
                    if last:
                        finalize_a(a, ob, rrow[g_ % 2], R_rrow[g_ % 2])
                        pend.append((idx + 2, (a, ob, 6 + (g_ % 2), sg, R_sg[m], m, rrow[g_ % 2], R_rrow[g_ % 2],
                                               tmp, R_tmp, och[ok], R_och[ok])))
                        gq[0] += 1

                a_qk(0)
                for idx in range(len(steps)):
                    a_exp(idx)
                    if idx + 1 < len(steps):
                        a_qk(idx + 1)
                    a_mask(idx)
                    a_pv(idx)
                    while pend and pend[0][0] <= idx:
                        finalize_b(*pend.pop(0)[1])
                while pend:
                    finalize_b(*pend.pop(0)[1])
                s.dma("sp", o_scr[:, hp, :], och[ok][:, :], reads=[R_och[ok]], writes=[R_oscr[hp]])
            s.barrier()

    def phase_gb(i):
        j_l = i // 2
        with contextlib.ExitStack() as st:
            al = lambda n, sh, dt: st.enter_context(nc.sbuf_tensor("s%d_%s" % (i, n), sh, dt))
            wst = al("wstb", [128, 2, 640], F32)
            R_wst = Res("wstb")
            wbf = [al("wbfb%d" % k, [128, 8, 704], BF16) for k in range(2)]
            R_wbf = [Res("wbfb%d" % k) for k in range(2)]
            qT = [al("qTb%d" % c, [128, S], BF16) for c in range(2)]
            kz = [al("kzb%d" % a, [128, S], BF16) for a in range(2)]
            R_kz = Res("kzinitb")
            s.op("dve", lambda e: e.memset(kz[0][64:128, :], 0.0), writes=[R_kz])
            s.op("dve", lambda e: e.memset(kz[1][0:64, :], 0.0), writes=[R_kz])
            sg = [al("sgb%d" % c, [128, S], BF16) for c in range(2)]
            R_q = [[Res("qb%d_%d" % (c, t)) for t in range(NT)] for c in range(2)]
            R_k = [Res("kb%d" % t) for t in range(NT)]
            R_sg = [[Res("sgb%d_%d" % (c, t)) for t in range(NT)] for c in range(2)]
            vaug = [al("vaugb%d" % a, [128, 32, 128], BF16) for a in range(2)]
            R_v = [Res("vb%d" % t) for t in range(NT)]
            cst = [al("cst%d" % k, [128, TT], F32) for k in range(2)]
            snt = [al("snt%d" % k, [128, TT], F32) for k in range(2)]
            R_cs = [Res("cs%d" % k) for k in range(2)]
            R_sn = [Res("sn%d" % k) for k in range(2)]
            sqb = [al("sqb%d" % k, [128, TT], BF16) for k in range(2)]
            R_sqb = [Res("sqb%d" % k) for k in range(2)]
            ub = [al("ub%d" % k, [128, TT], F32) for k in range(2)]
            R_ub = [Res("ub%d" % k) for k in range(2)]
            sdb = [al("sdb%d" % k, [128, TT], F32) for k in range(2)]
            R_sdb = [Res("sdb%d" % k) for k in range(2)]
            t1 = [al("t1b%d" % k, [128, TT], F32) for k in range(2)]
            R_t1 = [Res("t1b%d" % k) for k in range(2)]
            t2 = [al("t2b%d" % k, [128, TT], F32) for k in range(2)]
            R_t2 = [Res("t2b%d" % k) for k in range(2)]
            pt = [al("ptb%d" % k, [128, 1024], BF16) for k in range(3)]
            R_pt = [Res("ptb%d" % k) for k in range(3)]
            rrow = [al("rrowb%d" % k, [128, TT], F32) for k in range(2)]
            R_rrow = [Res("rrowb%d" % k) for k in range(2)]
            tmp = al("tmpb", [128, TT], F32)
            R_tmp = Res("tmpb")
            och = [al("ochb0", [128, S], BF16)] * 2
            R_och = [Res("ochb0")] * 2
            R_vinit = Res("vinitb")
            s.op("dve", lambda e: e.memset(vaug[0][:, :, 64:128], 1.0), writes=[R_vinit])
            s.op("dve", lambda e: e.memset(vaug[1][:, :, 0:64], 1.0), writes=[R_vinit])

            def load_w(g):
                k = g % 2
                wv = b_in[j_l][g].rearrange("p (c f) -> p c f", c=8)
                for hf in range(4):
                    cs_ = slice(2 * hf, 2 * hf + 2)
                    s.dma("sp", wst[:, :, :], wv[:, cs_, :], writes=[R_wst])
                    for (d0, d1, s0, s1) in ((0, 256, 0, 256), (256, 320, 256, 320), (320, 384, 256, 320), (384, 704, 320, 640)):
                        s.op("pool", lambda e, d0=d0, d1=d1, s0=s0, s1=s1, cs_=cs_: e.tensor_copy(
                            out=wbf[k][:, cs_, d0:d1], in_=wst[:, :, s0:s1]), reads=[R_wst], writes=[R_wbf[k]])

            cscount = [0]
            load_w(0)
            for g in range(4):
                wk = g % 2
                w = wbf[wk]
                for kind in range(3):
                    col = (0, 128, 256)[kind]
                    gcol = (G_BQ + j_l) if kind < 2 else (G_BK + j_l)
                    for tt in range(NT):
                        tsl = slice(tt * TT, (tt + 1) * TT)
                        ck = cscount[0] % 2
                        cscount[0] += 1
                        s.dma("sp", cst[ck][:, :], cos_in[:, tsl], writes=[R_cs[ck]])
                        s.dma("sp", snt[ck][:, :], sin_in[:, tsl], writes=[R_sn[ck]])
                        b = next_bank()

                        def mm(e, b=b, col=col, tsl=tsl, w=w):
                            for c in range(8):
                                ins = e.matmul(bank(b), w[:, c, col:col + 128], hT[:, c, tsl], start=(c == 0), stop=(c == 7))
                            return ins
                        s.op("pe", mm, reads=[R_wbf[wk]] + R_h[tt], writes=[R_b[b]])
                        s.op("act", lambda e, b=b, ck=ck: e.activation(out=sqb[ck][:, :], in_=bank(b), func=AF.Square),
                             reads=[R_b[b]], writes=[R_sqb[ck]])
                        s.op("act", lambda e, b=b, gcol=gcol, ck=ck: e.activation(out=ub[ck][:, :], in_=bank(b), func=AF.Copy,
                                                                                 scale=gv[:, gcol:gcol + 1]),
                             reads=[R_b[b], R_c], writes=[R_ub[ck]])
                        b2 = next_bank()
                        s.op("pe", lambda e, b2=b2, ck=ck: e.matmul(bank(b2), bones[:, :], sqb[ck][:, :], start=True, stop=True),
                             reads=[R_sqb[ck], R_c], writes=[R_b[b2]])
                        b3 = next_bank()
                        s.op("pe", lambda e, b3=b3, ck=ck: e.matmul(bank(b3), perm[:, :], ub[ck][:, :], start=True, stop=True),
                             reads=[R_ub[ck], R_c], writes=[R_b[b3]])
                        s.op("act", lambda e, b2=b2, ck=ck: e.activation(out=sdb[ck][:, :], in_=bank(b2), func=AF.Ln, scale=1.0 / 64,
                                                                          bias=epsc[:, 0:1]),
                             reads=[R_b[b2], R_c], writes=[R_sdb[ck]])
                        s.op("act", lambda e, ck=ck: e.activation(out=sdb[ck][:, :], in_=sdb[ck][:, :], func=AF.Exp, scale=-0.5),
                             reads=[R_sdb[ck]], writes=[R_sdb[ck]])
                        s.op("pool", lambda e, ck=ck: e.tensor_tensor(out=t1[ck][:, :], in0=ub[ck][:, :], in1=cst[ck][:, :], op=ALU.mult),
                             reads=[R_ub[ck], R_cs[ck]], writes=[R_t1[ck]])
                        s.op("dve", lambda e, b3=b3, ck=ck: e.tensor_tensor(out=t2[ck][:, :], in0=bank(b3), in1=snt[ck][:, :], op=ALU.mult),
                             reads=[R_b[b3], R_sn[ck]], writes=[R_t2[ck]])
                        s.op("pool", lambda e, ck=ck: e.tensor_tensor(out=t1[ck][:, :], in0=t1[ck][:, :], in1=t2[ck][:, :], op=ALU.add),
                             reads=[R_t1[ck], R_t2[ck]], writes=[R_t1[ck]])
                        if kind < 2:
                            dst, R_dst = qT[kind][:, tsl], R_q[kind][tt]
                            s.op("dve", lambda e, dst=dst, ck=ck: e.tensor_tensor(out=dst, in0=t1[ck][:, :], in1=sdb[ck][:, :], op=ALU.mult),
                                 reads=[R_t1[ck], R_sdb[ck]], writes=[R_dst])
                        else:
                            for a_ in range(2):
                                s.op("dve", lambda e, a_=a_, tsl=tsl, ck=ck: e.tensor_tensor(
                                    out=kz[a_][64 * a_:64 * a_ + 64, tsl], in0=t1[ck][64 * a_:64 * a_ + 64, :],
                                    in1=sdb[ck][64 * a_:64 * a_ + 64, :], op=ALU.mult),
                                    reads=[R_t1[ck], R_sdb[ck], R_kz], writes=[R_k[tt]])
                for tt in range(NT):
                    tsl = slice(tt * TT, (tt + 1) * TT)
                    for c2 in range(2):
                        col = 448 + 128 * c2
                        b = next_bank()

                        def mm(e, b=b, col=col, tsl=tsl, w=w):
                            for c in range(8):
                                ins = e.matmul(bank(b), w[:, c, col:col + 128], hT[:, c, tsl], start=(c == 0), stop=(c == 7))
                            return ins
                        s.op("pe", mm, reads=[R_wbf[wk]] + R_h[tt], writes=[R_b[b]])
                        s.op("act", lambda e, b=b, tsl=tsl, c2=c2: e.activation(out=sg[c2][:, tsl], in_=bank(b), func=AF.Silu),
                             reads=[R_b[b]], writes=[R_sg[c2][tt]])
                    b = next_bank()

                    def mmv(e, b=b, tt=tt, w=w):
                        for tcl in range(4):
                            t0 = tt * TT + tcl * 128
                            for c in range(8):
                                ins = e.matmul(bank(b, tcl * 64, (tcl + 1) * 64), hT[:, c, t0:t0 + 128], w[:, c, 384:448],
                                               start=(c == 0), stop=(c == 7))
                        return ins
                    s.op("pe", mmv, reads=[R_wbf[wk]] + R_h[tt], writes=[R_b[b]])
                    bv = bank(b, 0, 256).rearrange("p (t f) -> p t f", t=4)
                    s.op("dve", lambda e, bv=bv, tt=tt: e.tensor_copy(out=vaug[0][:, tt * 4:tt * 4 + 4, 0:64], in_=bv),
                         reads=[R_b[b], R_vinit], writes=[R_v[tt]])
                    s.op("dve", lambda e, bv=bv, tt=tt: e.tensor_copy(out=vaug[1][:, tt * 4:tt * 4 + 4, 64:128], in_=bv),
                         reads=[R_b[b], R_vinit], writes=[R_v[tt]])
                if g + 1 < 4:
                    load_w(g + 1)
                steps = [(hi, qt, kp) for hi in range(4) for qt in range(NT) for kp in range(16)]
                pend = []
                gq = [0]

                def b_qk(idx):
                    hi, qt, kp = steps[idx]
                    c2, a = hi // 2, hi % 2
                    lo = 64 * a
                    sb0 = 2 * (idx % 2)

                    def qk(e, kp=kp, sb0=sb0, qt=qt, lo=lo, c2=c2):
                        for z in range(2):
                            j = 2 * kp + z
                            ins = e.matmul(bank(sb0 + z), kz[lo // 64][:, j * 128:(j + 1) * 128],
                                           qT[c2][:, qt * TT:(qt + 1) * TT], start=True, stop=True)
                        return ins
                    s.op("pe", qk, reads=[R_k[kp // 2], R_q[c2][qt]], writes=[R_b[sb0], R_b[sb0 + 1]])

                def b_exp(idx):
                    sb0 = 2 * (idx % 2)
                    ek = idx % 3
                    s.op("act", lambda e, sb0=sb0, ek=ek: e.activation(
                        out=pt[ek][:, :], in_=ps[:, sb0 * 512:(sb0 + 2) * 512], func=AF.Exp, scale=0.125),
                        reads=[R_b[sb0], R_b[sb0 + 1]], writes=[R_pt[ek]])

                def b_pv(idx):
                    hi, qt, kp = steps[idx]
                    c2, a = hi // 2, hi % 2
                    ek = idx % 3
                    g_ = gq[0]
                    ob = 4 + (g_ % 2)

                    def pv(e, kp=kp, ek=ek, ob=ob, a=a):
                        for z in range(2):
                            j = 2 * kp + z
                            ins = e.matmul(bank(ob), vaug[a][:, j, :], pt[ek][:, z * 512:(z + 1) * 512],
                                           start=(kp == 0 and z == 0), stop=(kp == 15 and z == 1))
                        return ins
                    s.op("pe", pv, reads=[R_pt[ek], R_v[kp // 2]], writes=[R_b[ob]])
                    if kp == 15:
                        chunk = 2 * g + c2
                        ok = chunk % 2
                        finalize_a(a, ob, rrow[g_ % 2], R_rrow[g_ % 2])
                        pend.append((idx + 2, (a, ob, 6 + (g_ % 2), sg[c2], R_sg[c2][qt], qt, rrow[g_ % 2], R_rrow[g_ % 2],
                                               tmp, R_tmp, och[ok], R_och[ok]), (chunk, ok) if (a == 1 and qt == NT - 1) else None))
                        gq[0] += 1

                def b_fin(item):
                    finalize_b(*item[1])
                    if item[2] is not None:
                        chunk, ok = item[2]
                        s.dma("sp", o_scr[:, chunk, :], och[ok][:, :], reads=[R_och[ok]], writes=[R_oscr[chunk]])

                b_qk(0)
                for idx in range(len(steps)):
                    b_exp(idx)
                    if idx + 1 < len(steps):
                        b_qk(idx + 1)
                    b_pv(idx)
                    while pend and pend[0][0] <= idx:
                        b_fin(pend.pop(0))
                while pend:
                    b_fin(pend.pop(0))
            s.barrier()

    def phase_o(i, last):
        with contextlib.ExitStack() as st:
            al = lambda n, sh, dt: st.enter_context(nc.sbuf_tensor("s%d_%s" % (i, n), sh, dt))
            wst = al("wsto", [128, 2, 1024], F32)
            R_wst = Res("wsto")
            wo = al("wo", [128, 8, 1024], BF16)
            wg = al("wg", [128, 8, 1024], BF16)
            wp = al("wp", [128, 2, 1024], BF16)
            R_w = Res("w_o")
            xt = [al("xtO%d" % k, [128, 8, TT], F32) for k in range(2)]
            R_xt = [[Res("xtO%d_%d" % (k, c)) for c in range(8)] for k in range(2)]
            ot = [al("otO%d" % k, [128, 8, TT], BF16) for k in range(2)]
            R_ot = [Res("otO%d" % k) for k in range(2)]
            pin = [al("pin%d" % k, [128, 4, 256], F32) for k in range(2)]
            R_pin = [Res("pin%d" % k) for k in range(2)]
            pT = al("pT", [128, 2, TT], BF16)
            R_pT = Res("pT")
            sq = al("sqO", [128, 8, TT], BF16)
            R_sq = Res("sqO")
            sd = al("sdO", [128, TT], F32)
            R_sd = Res("sdO")
            n1 = sq
            R_n1 = R_sq
            gs = [al("gs%d" % k, [128, TT], F32) for k in range(2)]
            R_gs = [Res("gs%d" % k) for k in range(2)]
            if last:
                yo = al("yo", [128, 4, D], F32)
                R_yo = Res("yo")
            for (wdst, wsrc, nch) in ((wo, wo_in[i], 8), (wg, wg_in[i], 8), (wp, wp_in[i], 2)):
                wv = wsrc.rearrange("p (c f) -> p c f", c=nch)
                for c0 in range(0, nch, 2):
                    s.dma("sp", wst[:, :, :], wv[:, c0:c0 + 2, :], writes=[R_wst])
                    s.op("pool", lambda e, wdst=wdst, c0=c0: e.tensor_copy(out=wdst[:, c0:c0 + 2, :], in_=wst[:, :, :]),
                         reads=[R_wst], writes=[R_w])
            pv_ = p_in[i].rearrange("(t b p) f -> t p b f", b=4, p=128)
            ov = out.rearrange("(t b p) d -> t p b d", b=4, p=128)

            def loads(tt):
                k = tt % 2
                tsl = slice(tt * TT, (tt + 1) * TT)
                s.dma("sp", xt[k][:, :, :], x_scr[:, :, tsl], reads=[R_xscr[tt]], writes=R_xt[k], sem_res=R_xt[k][0])
                s.dma("sp", ot[k][:, :, :], o_scr[:, :, tsl], reads=R_oscr, writes=[R_ot[k]])
                s.dma("sp", pin[k][:, :, :], pv_[tt], writes=[R_pin[k]])

            loads(0)
            for tt in range(NT):
                k = tt % 2
                tsl = slice(tt * TT, (tt + 1) * TT)
                if tt + 1 < NT:
                    loads(tt + 1)
                X = xt[k]
                for fc in range(8):
                    b = next_bank()

                    def mm(e, b=b, fc=fc, k=k):
                        for c in range(8):
                            ins = e.matmul(bank(b), wo[:, c, fc * 128:(fc + 1) * 128], ot[k][:, c, :], start=(c == 0), stop=(c == 7))
                        return ins
                    s.op("pe", mm, reads=[R_w, R_ot[k]], writes=[R_b[b]])
                    s.op("dve", lambda e, b=b, fc=fc, X=X: e.tensor_tensor(out=X[:, fc, :], in0=X[:, fc, :], in1=bank(b), op=ALU.add),
                         reads=[R_b[b], R_xt[k][fc]], writes=[R_xt[k][fc]])
                emit_rstd(X, R_xt[k], sq, R_sq, sd, R_sd, 1.0 / D)
                for c in range(8):
                    eng = "dve"
                    s.op(eng, lambda e, c=c, X=X: e.scalar_tensor_tensor(
                        out=n1[:, c, :], in0=X[:, c, :], scalar=gv[:, g_ple(i) + c:g_ple(i) + c + 1], in1=sd[:, :],
                        op0=ALU.mult, op1=ALU.mult), reads=[R_xt[k][c], R_sd, R_c], writes=[R_n1])
                for c2 in range(2):
                    b = next_bank()

                    def tr(e, b=b, c2=c2, k=k):
                        for tb in range(4):
                            ins = e.transpose(out=bank(b, tb * 128, (tb + 1) * 128), in_=pin[k][:, tb, c2 * 128:(c2 + 1) * 128],
                                              identity=ident[:, :])
                        return ins
                    s.op("pe", tr, reads=[R_pin[k], R_c], writes=[R_b[b]])
                    s.op("act", lambda e, b=b, c2=c2: e.activation(out=pT[:, c2, :], in_=bank(b), func=AF.Copy),
                         reads=[R_b[b]], writes=[R_pT])
                for fc in range(8):
                    b = next_bank()

                    def mmg(e, b=b, fc=fc):
                        for c in range(8):
                            ins = e.matmul(bank(b), wg[:, c, fc * 128:(fc + 1) * 128], n1[:, c, :], start=(c == 0), stop=(c == 7))
                        return ins
                    s.op("pe", mmg, reads=[R_w, R_n1], writes=[R_b[b]])
                    gk = fc % 2
                    s.op("act", lambda e, b=b, gk=gk: e.activation(out=gs[gk][:, :], in_=bank(b), func=AF.Sigmoid),
                         reads=[R_b[b]], writes=[R_gs[gk]])
                    b2 = next_bank()

                    def mmp(e, b2=b2, fc=fc):
                        for c in range(2):
                            ins = e.matmul(bank(b2), wp[:, c, fc * 128:(fc + 1) * 128], pT[:, c, :], start=(c == 0), stop=(c == 1))
                        return ins
                    s.op("pe", mmp, reads=[R_w, R_pT], writes=[R_b[b2]])
                    s.op("dve", lambda e, b2=b2, gk=gk: e.tensor_tensor(out=gs[gk][:, :], in0=gs[gk][:, :], in1=bank(b2), op=ALU.mult),
                         reads=[R_gs[gk], R_b[b2]], writes=[R_gs[gk]])
                    s.op("pool", lambda e, fc=fc, X=X, gk=gk: e.tensor_tensor(out=X[:, fc, :], in0=X[:, fc, :], in1=gs[gk][:, :], op=ALU.add),
                         reads=[R_gs[gk], R_xt[k][fc]], writes=[R_xt[k][fc]])
                if dbg == ("x", i):
                    s.dma("sp", dbg_out[:, :, tsl], X[:, :, :], reads=R_xt[k], writes=[R_out[tt]])
                if not last:
                    s.dma("sp", x_scr[:, :, tsl], X[:, :, :], reads=R_xt[k], writes=[R_xscr[tt]])
                    emit_h(X, R_xt[k], sq, R_sq, sd, R_sd, tt, g_norm(i + 1))
                else:
                    emit_rstd(X, R_xt[k], sq, R_sq, sd, R_sd, 1.0 / D)
                    for c in range(8):
                        eng = "dve"
                        s.op(eng, lambda e, c=c, X=X: e.scalar_tensor_tensor(
                            out=X[:, c, :], in0=X[:, c, :], scalar=gv[:, G_FINAL + c:G_FINAL + c + 1], in1=sd[:, :],
                            op0=ALU.mult, op1=ALU.mult), reads=[R_xt[k][c], R_sd, R_c], writes=[R_xt[k][c]])
                    xn = X
                    R_xn = R_xt[k]
                    for tb in range(4):
                        for cq in range(2):
                            b = next_bank()

                            def tr(e, b=b, tb=tb, cq=cq, xn=xn):
                                for cc in range(4):
                                    c = cq * 4 + cc
                                    ins = e.transpose(out=bank(b, cc * 128, (cc + 1) * 128), in_=xn[:, c, tb * 128:(tb + 1) * 128],
                                                      identity=ident[:, :])
                                return ins
                            s.op("pe", tr, reads=R_xn + [R_c], writes=[R_b[b]])
                            if cq == 0:
                                s.op("dve", lambda e, b=b, tb=tb: e.tensor_copy(out=yo[:, tb, 0:512], in_=bank(b)),
                                     reads=[R_b[b]], writes=[R_yo])
                            else:
                                s.op("act", lambda e, b=b, tb=tb: e.activation(out=yo[:, tb, 512:1024], in_=bank(b), func=AF.Copy),
                                     reads=[R_b[b]], writes=[R_yo])
                    s.dma("sp", ov[tt], yo[:, :, :], reads=[R_yo], writes=[R_out[tt]])
            s.barrier()

    for i in range(n_layers):
        if dbg == ("h", i):
            s.dma("sp", dbg_out[:, :, :], hT[:, :, :], reads=[r for t in R_h for r in t], writes=[R_out[0]])
            break
        if i % 2 == 0:
            phase_ga(i)
        else:
            phase_gb(i)
        if dbg == ("o", i):
            s.dma("sp", dbg_out[:, :, :], o_scr[:, :, :], reads=R_oscr, writes=[R_out[0]])
            break
        phase_o(i, last=(i == n_layers - 1))
    s.barrier()
    s.emit()
    return nc


_CACHE = {}
_PADDED = ("cosT", "sinT", "a_in", "b_in", "wo", "wg", "wp", "tb")


def _prep_shared(inp):
    f = lambda a: np.ascontiguousarray(np.asarray(a, dtype=np.float32))
    sh = {}
    cosT, sinT, perm = _rope_tables()
    sh["cosT"], sh["sinT"], sh["perm"] = cosT, sinT, perm
    sh["ident"] = np.eye(128, dtype=np.float32)
    gv = np.zeros((128, 80), np.float32)
    ng, pg = f(inp["norm_g"]), f(inp["ple_norm_g"])
    for i in range(DEPTH):
        gv[:, 16 * i:16 * i + 8] = _vec_pc(ng[i])
        gv[:, 16 * i + 8:16 * i + 16] = _vec_pc(pg[i])
    gv[:, 64:72] = _vec_pc(f(inp["final_norm_g"]))
    bq, bk = f(inp["b_q_norm"]), f(inp["b_k_norm"])
    for j in range(2):
        gv[:, 72 + j] = np.tile(bq[j], 2)
        gv[:, 74 + j] = np.tile(bk[j], 2)
    sh["gv"] = gv
    awi, bwi = f(inp["a_w_in"]), f(inp["b_w_in"])
    for j in range(2):
        W = awi[j]
        groups = []
        for hp in range(8):
            cols = np.concatenate([W[:, k * 1024 + hp * 128:k * 1024 + (hp + 1) * 128] for k in range(4)], axis=1)
            groups.append(_pc(cols))
        sh["a_in%d" % j] = np.ascontiguousarray(np.stack(groups))
        W = bwi[j]
        groups = []
        for g in range(4):
            cols = np.concatenate([W[:, 256 * g:256 * (g + 1)], W[:, 1024 + 64 * g:1024 + 64 * (g + 1)],
                                   W[:, 1280 + 64 * g:1280 + 64 * (g + 1)], W[:, 1536 + 256 * g:1536 + 256 * (g + 1)]], axis=1)
            groups.append(_pc(cols))
        sh["b_in%d" % j] = np.ascontiguousarray(np.stack(groups))
        sh["tb%d" % j] = _a_tables(f(inp["a_rpb"])[j])
    awo, bwo = f(inp["a_w_out"]), f(inp["b_w_out"])
    wgt, wpj = f(inp["ple_w_gate"]), f(inp["ple_w_proj"])
    for i in range(DEPTH):
        sh["wo%d" % i] = _pc(awo[i // 2] if i % 2 == 0 else bwo[i // 2])
        sh["wg%d" % i] = _pc(wgt[i])
        sh["wp%d" % i] = _pc(wpj[i])
    return sh


def _core_inputs(sh, xb, pb, b):
    m = {}
    for k_, v_ in sh.items():
        if k_ in _PADDED or k_[:-1] in _PADDED:
            pad = np.full((1,) + v_.shape[1:], float(b), np.float32)
            m[k_] = np.concatenate([v_, pad], axis=0)
        else:
            m[k_] = v_
    m["x"] = np.ascontiguousarray(xb)
    m["p"] = np.ascontiguousarray(pb)
    return m


def kernel(x, p, norm_g, a_w_in, a_rpb, a_w_out, b_w_in, b_q_norm, b_k_norm, b_w_out,
           ple_norm_g, ple_w_gate, ple_w_proj, final_norm_g):
    inp = dict(norm_g=norm_g, a_w_in=a_w_in, a_rpb=a_rpb, a_w_out=a_w_out, b_w_in=b_w_in, b_q_norm=b_q_norm,
               b_k_norm=b_k_norm, b_w_out=b_w_out, ple_norm_g=ple_norm_g, ple_w_gate=ple_w_gate,
               ple_w_proj=ple_w_proj, final_norm_g=final_norm_g)
    sh = _prep_shared(inp)
    x = np.asarray(x, dtype=np.float32)
    p = np.asarray(p, dtype=np.float32)
    if "nc" not in _CACHE:
        _CACHE["nc"] = build()
    nc = _CACHE["nc"]
    in_maps = [_core_inputs(sh, x[b], p[:, b], b) for b in range(8)]
    res = run_bass_kernel_spmd(nc, in_maps, core_ids=list(range(8)))
    return np.stack([np.asarray(r["out"], dtype=np.float32) for r in res.results], axis=0)
```

```python
import contextlib
import numpy as np
import ml_dtypes
import concourse.bass as bass
import concourse.mybir as mybir
from concourse.bass_utils import run_bass_kernel_spmd

F32 = mybir.dt.float32
BF16 = mybir.dt.bfloat16
AF = mybir.ActivationFunctionType
ALU = mybir.AluOpType

S = 4096
D = 1024
NT = 8
TT = 512
DEPTH = 4
EPS = 1e-6
NE = 22
ENGS = ("pe", "act", "dve", "pool", "sp")


class Res:
    __slots__ = ("name", "w", "r", "sem", "semval")

    def __init__(self, name):
        self.name = name
        self.w = None
        self.r = []
        self.sem = None
        self.semval = 0


class Sched:
    def __init__(self, nc):
        self.nc = nc
        self.ops = {e: [] for e in ENGS}
        self.known = {e: {} for e in ENGS}
        self.dma_sems = []

    def _need(self, eng, tok, waits):
        if tok is None:
            return
        if tok[0] == "e":
            _, f, idx = tok
            if f == eng and eng == "pe":
                return
            key = ("e", f)
            val = idx
        else:
            key = ("s", tok[1])
            val = tok[2]
        kn = self.known[eng]
        if kn.get(key, -1) >= val:
            return
        kn[key] = val
        waits.append(tok)
        if tok[0] == "e":
            self.ops[tok[1]][tok[2]][2] = True

    def _deps(self, eng, reads, writes):
        waits = []
        for r in reads:
            self._need(eng, r.w, waits)
        for w in writes:
            self._need(eng, w.w, waits)
            for t in w.r:
                self._need(eng, t, waits)
        return waits

    def _commit(self, tok, reads, writes):
        for r in reads:
            r.r.append(tok)
        for w in writes:
            w.w = tok
            w.r = []

    def op(self, eng, fn, reads=(), writes=()):
        waits = self._deps(eng, reads, writes)
        tok = ("e", eng, len(self.ops[eng]))
        self.ops[eng].append([fn, waits, False, None])
        self._commit(tok, reads, writes)
        return tok

    def dma(self, eng, out, in_, reads=(), writes=(), sem_res=None):
        waits = self._deps(eng, reads, writes)
        if sem_res is None:
            sem_res = writes[0]
        if sem_res.sem is None:
            sem_res.sem = len(self.dma_sems)
            self.dma_sems.append(sem_res)
        sem_res.semval += 16
        tok = ("s", sem_res.sem, sem_res.semval)

        def fn(e, out=out, in_=in_):
            return e.dma_start(out=out, in_=in_)

        self.ops[eng].append([fn, waits, False, sem_res.sem])
        self._commit(tok, reads, writes)
        return tok

    def barrier(self):
        last = {}
        for f in ENGS:
            for i in range(len(self.ops[f]) - 1, -1, -1):
                o = self.ops[f][i]
                if o[0] is not None and o[3] is None:
                    last[f] = ("e", f, i)
                    break
        for e in ENGS:
            waits = []
            for f in ENGS:
                if f != e and f in last:
                    self._need(e, last[f], waits)
            for r in self.dma_sems:
                self._need(e, ("s", r.sem, r.semval), waits)
            self.ops[e].append([None, waits, False, None])

    def emit(self):
        nc = self.nc
        val = {}
        for e in ENGS:
            c = 0
            for i, o in enumerate(self.ops[e]):
                if o[2]:
                    c += 1
                    val[(e, i)] = c
        with contextlib.ExitStack() as st:
            esem = {e: st.enter_context(nc.semaphore("prog_" + e)) for e in ENGS}
            dsem = [st.enter_context(nc.semaphore("dma_%d" % i)) for i in range(len(self.dma_sems))]
            block = st.enter_context(nc.Block())
            ops = self.ops

            def body(e, h):
                for fn, waits, needed, dinc in ops[e]:
                    for t in waits:
                        if t[0] == "e":
                            h.wait_ge(esem[t[1]], val[(t[1], t[2])])
                        else:
                            h.wait_ge(dsem[t[1]], t[2])
                    if fn is None:
                        continue
                    ins = fn(h)
                    if dinc is not None:
                        ins.then_inc(dsem[dinc], 16)
                    elif needed:
                        ins.then_inc(esem[e], 1)

            @block.tensor
            def _(h):
                body("pe", h)

            @block.scalar
            def _(h):
                body("act", h)

            @block.vector
            def _(h):
                body("dve", h)

            @block.gpsimd
            def _(h):
                body("pool", h)

            @block.sync
            def _(h):
                body("sp", h)


def _rope_tables():
    t = np.arange(S)
    row = (t // 64).astype(np.float32)
    col = (t % 64).astype(np.float32)
    inv = np.power(np.float32(10000.0), -np.arange(16, dtype=np.float32) * np.float32(2.0) / np.float32(32.0)).astype(np.float32)
    cosT = np.zeros((128, S), np.float32)
    sinT = np.zeros((128, S), np.float32)
    perm = np.zeros((128, 128), np.float32)
    for p in range(128):
        d = p % 64
        sec = d // 32
        dd = d % 32
        f = dd % 16
        pos = row if sec == 0 else col
        ang = (pos * inv[f]).astype(np.float32)
        cosT[p] = np.cos(ang)
        sn = np.sin(ang)
        if dd < 16:
            sinT[p] = -sn
            src = p + 16
        else:
            sinT[p] = sn
            src = p - 16
        perm[src, p] = 1.0
    return cosT, sinT, perm


def _a_tables(rpb):
    NEG = np.float32(-30000.0)
    kc = np.arange(64)[:, None]
    c = np.arange(64)[None, :]
    cs = np.clip(c - 8, 0, 48)
    colvalid = (kc >= cs) & (kc < cs + 16)
    dc = np.clip(kc - c + 15, 0, 30)
    out = np.full((16, 2, 64, 2, NE, 64), NEG, np.float32)
    for a in range(2):
        for e in range(NE):
            dr = 17 + a - e
            if 0 <= dr <= 14:
                blk = np.where(colvalid[None], rpb[:, dr][:, dc], NEG)
                out[:, a, :, 1, e, :] = blk
                if 3 <= dr <= 10:
                    out[:, a, :, 0, e, :] = blk
    return np.ascontiguousarray(out.reshape(16, 128, 2 * NE * 64))


def _pc(w):
    K = w.shape[0] // 128
    return np.ascontiguousarray(w.reshape(K, 128, w.shape[1]).transpose(1, 0, 2).reshape(128, K * w.shape[1]))


def _vec_pc(g):
    return np.ascontiguousarray(g.reshape(-1, 128).T)


def a_tiles():
    res = []
    for m in range(8):
        if m == 0:
            js = list(range(0, 6))
        elif m == 7:
            js = list(range(26, 32))
        else:
            js = list(range(4 * m - 2, 4 * m + 6))
        res.append((m, js))
    return res


def a_segments(m, j):
    if m == 0 and j <= 3:
        return [(0, 5, 1), (5, 8, 0)]
    if m == 7 and j >= 28:
        return [(0, 5, 0), (5, 8, 1)]
    return [(0, 8, 0)]


def build(n_layers=DEPTH, dbg=None):
    nc = bass.Bass("TRN2", target_bir_lowering=False)
    s = Sched(nc)

    def din(name, shape, dt=F32, pad=False):
        if pad:
            t = nc.dram_tensor(name, [shape[0] + 1] + list(shape[1:]), dt, kind="ExternalInput").ap()
            return t[0:shape[0]]
        return nc.dram_tensor(name, list(shape), dt, kind="ExternalInput").ap()

    x_in = din("x", [S, D])
    p_in = din("p", [DEPTH, S, 256])
    gv_in = din("gv", [128, 80])
    ident_in = din("ident", [128, 128])
    perm_in = din("perm", [128, 128])
    cos_in = din("cosT", [128, S], pad=True)
    sin_in = din("sinT", [128, S], pad=True)
    a_in = [din("a_in%d" % j, [8, 128, 8 * 512], pad=True) for j in range(2)]
    b_in = [din("b_in%d" % j, [4, 128, 8 * 640], pad=True) for j in range(2)]
    wo_in = [din("wo%d" % i, [128, 8 * 1024], pad=True) for i in range(DEPTH)]
    wg_in = [din("wg%d" % i, [128, 8 * 1024], pad=True) for i in range(DEPTH)]
    wp_in = [din("wp%d" % i, [128, 2 * 1024], pad=True) for i in range(DEPTH)]
    tb_in = [din("tb%d" % j, [16, 128, 2 * NE * 64], pad=True) for j in range(2)]
    out = nc.dram_tensor("out", [S, D], F32, kind="ExternalOutput").ap()
    x_scr = nc.dram_tensor("x_scr", [128, 8, S], F32, kind="Internal").ap()
    o_scr = nc.dram_tensor("o_scr", [128, 8, S], BF16, kind="Internal").ap()
    R_xscr = [Res("xscr%d" % t) for t in range(NT)]
    R_oscr = [Res("oscr%d" % c) for c in range(8)]
    R_out = [Res("out%d" % t) for t in range(NT)]
    dbg_out = None
    if dbg:
        dbg_out = nc.dram_tensor("dbg", [128, 8, S], F32 if dbg[0] == "x" else BF16, kind="ExternalOutput").ap()

    hT = nc.alloc_sbuf_tensor("sb_hT", [128, 8, S], BF16)
    R_h = [[Res("h%d_%d" % (t, c)) for c in range(8)] for t in range(NT)]
    ident = nc.alloc_sbuf_tensor("sb_ident", [128, 128], F32)
    perm = nc.alloc_sbuf_tensor("sb_perm", [128, 128], F32)
    gv = nc.alloc_sbuf_tensor("sb_gv", [128, 80], F32)
    ones_bf = nc.alloc_sbuf_tensor("sb_ones_bf", [128, 128], BF16)
    ones_f = nc.alloc_sbuf_tensor("sb_ones_f", [128, 128], F32)
    bones = nc.alloc_sbuf_tensor("sb_bones", [128, 128], BF16)
    epsc = nc.alloc_sbuf_tensor("sb_epsc", [128, 1], F32)
    R_c = Res("consts")
    ps = nc.alloc_psum_tensor("ps", [128, 8 * 512], F32)
    R_b = [Res("bank%d" % b) for b in range(8)]

    def bank(b, lo=0, hi=512):
        return ps[:, b * 512 + lo:b * 512 + hi]

    s.dma("sp", ident[:, :], ident_in[:, :], writes=[R_c])
    s.dma("sp", perm[:, :], perm_in[:, :], writes=[R_c])
    s.dma("sp", gv[:, :], gv_in[:, :], writes=[R_c])
    s.op("pool", lambda e: e.memset(ones_bf[:, :], 1.0), writes=[R_c])
    s.op("pool", lambda e: e.memset(ones_f[:, :], 1.0), writes=[R_c])
    s.op("pool", lambda e: e.memset(bones[:, :], 0.0), writes=[R_c])
    s.op("pool", lambda e: e.memset(bones[0:64, 0:64], 1.0), writes=[R_c])
    s.op("pool", lambda e: e.memset(bones[64:128, 64:128], 1.0), writes=[R_c])
    s.op("pool", lambda e: e.memset(epsc[:, :], EPS), writes=[R_c])
    s.barrier()

    def g_norm(i):
        return 16 * i

    def g_ple(i):
        return 16 * i + 8

    G_FINAL = 64
    G_BQ = 72
    G_BK = 74

    bank_rr = [0]

    def next_bank():
        b = bank_rr[0]
        bank_rr[0] = (b + 1) % 8
        return b

    def emit_rstd(xt, R_xt, sq, R_sq, sd, R_sd, width_scale):
        s.op("act", lambda e: e.activation(out=sq[:, :, :], in_=xt[:, :, :], func=AF.Square),
             reads=R_xt, writes=[R_sq])
        b = next_bank()

        def mm(e):
            for c in range(8):
                ins = e.matmul(bank(b), ones_bf[:, :], sq[:, c, :], start=(c == 0), stop=(c == 7))
            return ins
        s.op("pe", mm, reads=[R_sq, R_c], writes=[R_b[b]])
        s.op("act", lambda e: e.activation(out=sd[:, :], in_=bank(b), func=AF.Ln, scale=width_scale, bias=epsc[:, 0:1]),
             reads=[R_b[b], R_c], writes=[R_sd])
        s.op("act", lambda e: e.activation(out=sd[:, :], in_=sd[:, :], func=AF.Exp, scale=-0.5), reads=[R_sd], writes=[R_sd])

    def emit_h(xt, R_xt, sq, R_sq, sd, R_sd, tt, gcol):
        emit_rstd(xt, R_xt, sq, R_sq, sd, R_sd, 1.0 / D)
        for c in range(8):
            eng = "dve"
            s.op(eng, lambda e, c=c: e.scalar_tensor_tensor(
                out=hT[:, c, tt * TT:(tt + 1) * TT], in0=xt[:, c, :], scalar=gv[:, gcol + c:gcol + c + 1],
                in1=sd[:, :], op0=ALU.mult, op1=ALU.mult),
                reads=[R_xt[c], R_sd, R_c], writes=[R_h[tt][c]])

    with contextlib.ExitStack() as st:
        xin = [st.enter_context(nc.sbuf_tensor("sT_xin%d" % i, [128, 4, D], F32)) for i in range(2)]
        R_xin = [Res("xin%d" % i) for i in range(2)]
        xt = [st.enter_context(nc.sbuf_tensor("xtT%d" % i, [128, 8, TT], F32)) for i in range(2)]
        R_xt = [[Res("xtT%d_%d" % (i, c)) for c in range(8)] for i in range(2)]
        sq = st.enter_context(nc.sbuf_tensor("sqT", [128, 8, TT], BF16))
        R_sq = Res("sqT")
        sd = st.enter_context(nc.sbuf_tensor("sdT", [128, TT], F32))
        R_sd = Res("sdT")
        xv = x_in.rearrange("(t b p) d -> t p b d", b=4, p=128)
        for tt in range(NT):
            k = tt % 2
            s.dma("sp", xin[k][:, :, :], xv[tt], writes=[R_xin[k]])
            for c in range(8):
                b = next_bank()

                def tr(e, c=c, b=b, k=k):
                    for tb in range(4):
                        ins = e.transpose(out=bank(b, tb * 128, (tb + 1) * 128), in_=xin[k][:, tb, c * 128:(c + 1) * 128],
                                          identity=ident[:, :])
                    return ins
                s.op("pe", tr, reads=[R_xin[k], R_c], writes=[R_b[b]])
                if c % 2 == 0:
                    s.op("dve", lambda e, c=c, b=b, k=k: e.tensor_copy(out=xt[k][:, c, :], in_=bank(b)),
                         reads=[R_b[b]], writes=[R_xt[k][c]])
                else:
                    s.op("act", lambda e, c=c, b=b, k=k: e.activation(out=xt[k][:, c, :], in_=bank(b), func=AF.Copy),
                         reads=[R_b[b]], writes=[R_xt[k][c]])
            s.dma("sp", x_scr[:, :, tt * TT:(tt + 1) * TT], xt[k][:, :, :], reads=R_xt[k], writes=[R_xscr[tt]])
            emit_h(xt[k], R_xt[k], sq, R_sq, sd, R_sd, tt, g_norm(0))
        s.barrier()

    def finalize_a(a, ob, rrow, R_rrow):
        dp = 64 if a == 0 else 0
        rl, rb = rrow
        s.op("act", lambda e: e.activation(out=rl[dp:dp + 1, :], in_=bank(ob)[dp:dp + 1, :], func=AF.Ln),
             reads=[R_b[ob]], writes=[R_rrow])
        s.op("act", lambda e: e.activation(out=rb[dp:dp + 1, :], in_=rl[dp:dp + 1, :], func=AF.Exp, scale=-1.0),
             reads=[R_rrow], writes=[R_rrow])

    def finalize_b(a, ob, obc, sg, R_sg_t, qt, rrow, R_rrow, tmp, R_tmp, ochunk, R_ochunk):
        dp = 64 if a == 0 else 0
        lo = 64 * a
        s.op("pe", lambda e: e.matmul(bank(obc), ones_bf[dp:dp + 1, :], rrow[1][dp:dp + 1, :], start=True, stop=True),
             reads=[R_rrow, R_c], writes=[R_b[obc]])
        s.op("dve", lambda e: e.tensor_tensor(out=tmp[lo:lo + 64, :], in0=bank(ob)[lo:lo + 64, :],
                                              in1=sg[lo:lo + 64, qt * TT:(qt + 1) * TT], op=ALU.mult),
             reads=[R_b[ob], R_sg_t], writes=[R_tmp])
        s.op("dve", lambda e: e.tensor_tensor(out=ochunk[lo:lo + 64, qt * TT:(qt + 1) * TT], in0=tmp[lo:lo + 64, :],
                                              in1=bank(obc)[lo:lo + 64, :], op=ALU.mult),
             reads=[R_tmp, R_b[obc]], writes=[R_ochunk])

    def phase_ga(i):
        j_l = i // 2
        with contextlib.ExitStack() as st:
            al = lambda n, sh, dt: st.enter_context(nc.sbuf_tensor("s%d_%s" % (i, n), sh, dt))
            wst = al("wst", [128, 8, 512], F32)
            R_wst = Res("wst")
            wbf = [al("wbf%d" % k, [128, 8, 512], BF16) for k in range(2)]
            R_wbf = [Res("wbf%d" % k) for k in range(2)]
            qT = al("qT", [128, S], BF16)
            kz = [al("kz%d" % a, [128, S], BF16) for a in range(2)]
            R_kz = Res("kzinit")
            s.op("dve", lambda e: e.memset(kz[0][64:128, :], 0.0), writes=[R_kz])
            s.op("dve", lambda e: e.memset(kz[1][0:64, :], 0.0), writes=[R_kz])
            sg = al("sg", [128, S], BF16)
            R_q = [Res("q%d" % t) for t in range(NT)]
            R_k = [Res("k%d" % t) for t in range(NT)]
            R_sg = [Res("sg%d" % t) for t in range(NT)]
            vaug = [al("vaug%d" % a, [128, 32, 128], BF16) for a in range(2)]
            R_v = [Res("v%d" % t) for t in range(NT)]
            tst = al("tst", [128, 2 * NE * 64], F32)
            R_tst = Res("tst")
            tbf = [al("tbf%d" % k, [128, 2, NE, 64], BF16) for k in range(2)]
            R_tbf = [Res("tbf%d" % k) for k in range(2)]
            exs = [al("exs%d" % k, [128, 1024], BF16) for k in range(2)]
            R_exs = [Res("exs%d" % k) for k in range(2)]
            pt = [al("pt%d" % k, [128, 1024], BF16) for k in range(2)]
            R_pt = [[Res("pt%d_%d" % (k, z)) for z in range(2)] for k in range(2)]
            rrow = [(al("rl%d" % k, [128, TT], F32), al("rb%d" % k, [128, TT], BF16)) for k in range(2)]
            R_rrow = [Res("rrow%d" % k) for k in range(2)]
            tmp = al("tmp", [128, TT], F32)
            R_tmp = Res("tmp")
            och = [al("och%d" % k, [128, S], BF16) for k in range(2)]
            R_och = [Res("och%d" % k) for k in range(2)]
            R_vinit = Res("vinit")
            s.op("dve", lambda e: e.memset(vaug[0][:, :, 64:128], 1.0), writes=[R_vinit])
            s.op("dve", lambda e: e.memset(vaug[1][:, :, 0:64], 1.0), writes=[R_vinit])

            def load_w(hp):
                k = hp % 2
                s.dma("sp", wst[:, :, :], a_in[j_l][hp].rearrange("p (c f) -> p c f", c=8), writes=[R_wst])
                s.op("pool", lambda e: e.tensor_copy(out=wbf[k][:, :, :], in_=wst[:, :, :]),
                     reads=[R_wst], writes=[R_wbf[k]])

            tcount = [0]

            def load_tab(h):
                k = tcount[0] % 2
                tcount[0] += 1
                s.dma("sp", tst[:, :], tb_in[j_l][h], writes=[R_tst])
                s.op("act", lambda e: e.activation(out=tbf[k][:, :, :, :].rearrange("p a e c -> p (a e c)"), in_=tst[:, :], func=AF.Exp),
                     reads=[R_tst], writes=[R_tbf[k]])
                return k

            load_w(0)
            for hp in range(8):
                wk = hp % 2
                w = wbf[wk]
                for tt in range(NT):
                    tsl = slice(tt * TT, (tt + 1) * TT)
                    for kind in range(3):
                        col = (0, 128, 384)[kind]
                        b = next_bank()

                        def mm(e, b=b, col=col, tsl=tsl, w=w):
                            for c in range(8):
                                ins = e.matmul(bank(b), w[:, c, col:col + 128], hT[:, c, tsl], start=(c == 0), stop=(c == 7))
                            return ins
                        s.op("pe", mm, reads=[R_wbf[wk]] + R_h[tt], writes=[R_b[b]])
                        if kind == 0:
                            s.op("dve", lambda e, b=b, tsl=tsl: e.tensor_copy(out=qT[:, tsl], in_=bank(b)),
                                 reads=[R_b[b]], writes=[R_q[tt]])
                        elif kind == 1:
                            s.op("dve", lambda e, b=b, tsl=tsl: e.tensor_copy(out=kz[0][0:64, tsl], in_=bank(b)[0:64, :]),
                                 reads=[R_b[b], R_kz], writes=[R_k[tt]])
                            s.op("dve", lambda e, b=b, tsl=tsl: e.tensor_copy(out=kz[1][64:128, tsl], in_=bank(b)[64:128, :]),
                                 reads=[R_b[b], R_kz], writes=[R_k[tt]])
                        else:
                            s.op("act", lambda e, b=b, tsl=tsl: e.activation(out=sg[:, tsl], in_=bank(b), func=AF.Silu),
                                 reads=[R_b[b]], writes=[R_sg[tt]])
                    b = next_bank()

                    def mmv(e, b=b, tt=tt, w=w):
                        for tcl in range(4):
                            t0 = tt * TT + tcl * 128
                            for c in range(8):
                                ins = e.matmul(bank(b, tcl * 128, (tcl + 1) * 128), hT[:, c, t0:t0 + 128], w[:, c, 256:384],
                                               start=(c == 0), stop=(c == 7))
                        return ins
                    s.op("pe", mmv, reads=[R_wbf[wk]] + R_h[tt], writes=[R_b[b]])
                    bv = bank(b).rearrange("p (t f) -> p t f", t=4)
                    s.op("dve", lambda e, bv=bv, tt=tt: e.tensor_copy(out=vaug[0][:, tt * 4:tt * 4 + 4, 0:64], in_=bv[:, :, 0:64]),
                         reads=[R_b[b], R_vinit], writes=[R_v[tt]])
                    s.op("dve", lambda e, bv=bv, tt=tt: e.tensor_copy(out=vaug[1][:, tt * 4:tt * 4 + 4, 64:128], in_=bv[:, :, 64:128]),
                         reads=[R_b[b], R_vinit], writes=[R_v[tt]])
                if hp + 1 < 8:
                    load_w(hp + 1)
                ok = hp % 2
                steps = []
                for a in range(2):
                    tk = load_tab(2 * hp + a)
                    for (m, js) in a_tiles():
                        prs = [js[z:z + 2] for z in range(0, len(js), 2)]
                        for pi_, pr in enumerate(prs):
                            steps.append((a, tk, m, pr, pi_ == 0, pi_ == len(prs) - 1))
                gq = [0]

                def a_qk(idx):
                    a, tk, m, pr, first, last = steps[idx]
                    sb0 = 2 * (idx % 2)
                    lo = 64 * a

                    def qk(e, pr=pr, sb0=sb0, m=m, lo=lo):
                        for z, j in enumerate(pr):
                            ins = e.matmul(bank(sb0 + z), kz[lo // 64][:, j * 128:(j + 1) * 128],
                                           qT[:, m * TT:(m + 1) * TT], start=True, stop=True)
                        return ins
                    s.op("pe", qk, reads=[R_k[jj // 4] for jj in pr] + [R_q[m]], writes=[R_b[sb0], R_b[sb0 + 1]])

                def a_exp(idx):
                    a, tk, m, pr, first, last = steps[idx]
                    sb0 = 2 * (idx % 2)
                    ek = idx % 2
                    s.op("act", lambda e, sb0=sb0, ek=ek: e.activation(
                        out=exs[ek][:, :], in_=ps[:, sb0 * 512:(sb0 + 2) * 512], func=AF.Exp, scale=0.125),
                        reads=[R_b[sb0], R_b[sb0 + 1]], writes=[R_exs[ek]])

                def a_mask(idx):
                    a, tk, m, pr, first, last = steps[idx]
                    ek = idx % 2
                    for z, j in enumerate(pr):
                        e0 = 10 - (2 * j - 8 * m)
                        for (b0, b1, tab) in a_segments(m, j):
                            s.op("dve", lambda e, z=z, e0=e0, b0=b0, b1=b1, tab=tab, ek=ek, tk=tk: e.tensor_tensor(
                                out=pt[ek][:, z * 512 + b0 * 64:z * 512 + b1 * 64],
                                in0=exs[ek][:, z * 512 + b0 * 64:z * 512 + b1 * 64],
                                in1=tbf[tk][:, tab, e0 + b0:e0 + b1, :].rearrange("p e c -> p (e c)"), op=ALU.mult),
                                reads=[R_exs[ek], R_tbf[tk]], writes=[R_pt[ek][z]])

                pend = []

                def a_pv(idx):
                    a, tk, m, pr, first, last = steps[idx]
                    ek = idx % 2
                    g_ = gq[0]
                    ob = 4 + (g_ % 2)

                    def pv(e, pr=pr, ek=ek, ob=ob, first=first, last=last, a=a):
                        for z, j in enumerate(pr):
                            ins = e.matmul(bank(ob), vaug[a][:, j, :], pt[ek][:, z * 512:(z + 1) * 512],
                                           start=(first and z == 0), stop=(last and z == len(pr) - 1))
                        return ins
                    s.op("pe", pv, reads=R_pt[ek] + [R_v[jj // 4] for jj in pr], writes=[R_b[ob]])
                    if last:
                        finalize_a(a, ob, rrow[g_ % 2], R_rrow[g_ % 2])
                        pend.append((idx + 2, (a, ob, 6 + (g_ % 2), sg, R_sg[m], m, rrow[g_ % 2], R_rrow[g_ % 2],
                                               tmp, R_tmp, och[ok], R_och[ok])))
                        gq[0] += 1

                a_qk(0)
                for idx in range(len(steps)):
                    a_exp(idx)
                    if idx + 1 < len(steps):
                        a_qk(idx + 1)
                    a_mask(idx)
                    a_pv(idx)
                    while pend and pend[0][0] <= idx:
                        finalize_b(*pend.pop(0)[1])
                while pend:
                    finalize_b(*pend.pop(0)[1])
                s.dma("sp", o_scr[:, hp, :], och[ok][:, :], reads=[R_och[ok]], writes=[R_oscr[hp]])
            s.barrier()

    def phase_gb(i):
        j_l = i // 2
        with contextlib.ExitStack() as st:
            al = lambda n, sh, dt: st.enter_context(nc.sbuf_tensor("s%d_%s" % (i, n), sh, dt))
            wst = al("wstb", [128, 2, 640], F32)
            R_wst = Res("wstb")
            wbf = [al("wbfb%d" % k, [128, 8, 704], BF16) for k in range(2)]
            R_wbf = [Res("wbfb%d" % k) for k in range(2)]
            qT = [al("qTb%d" % c, [128, S], BF16) for c in range(2)]
            kz = [al("kzb%d" % a, [128, S], BF16) for a in range(2)]
            R_kz = Res("kzinitb")
            s.op("dve", lambda e: e.memset(kz[0][64:128, :], 0.0), writes=[R_kz])
            s.op("dve", lambda e: e.memset(kz[1][0:64, :], 0.0), writes=[R_kz])
            sg = [al("sgb%d" % c, [128, S], BF16) for c in range(2)]
            R_q = [[Res("qb%d_%d" % (c, t)) for t in range(NT)] for c in range(2)]
            R_k = [Res("kb%d" % t) for t in range(NT)]
            R_sg = [[Res("sgb%d_%d" % (c, t)) for t in range(NT)] for c in range(2)]
            vaug = [al("vaugb%d" % a, [128, 32, 128], BF16) for a in range(2)]
            R_v = [Res("vb%d" % t) for t in range(NT)]
            cst = [al("cst%d" % k, [128, TT], F32) for k in range(2)]
            snt = [al("snt%d" % k, [128, TT], F32) for k in range(2)]
            R_cs = [Res("cs%d" % k) for k in range(2)]
            R_sn = [Res("sn%d" % k) for k in range(2)]
            sqb = [al("sqb%d" % k, [128, TT], BF16) for k in range(2)]
            R_sqb = [Res("sqb%d" % k) for k in range(2)]
            ub = [al("ub%d" % k, [128, TT], F32) for k in range(2)]
            R_ub = [Res("ub%d" % k) for k in range(2)]
            sdb = [al("sdb%d" % k, [128, TT], F32) for k in range(2)]
            R_sdb = [Res("sdb%d" % k) for k in range(2)]
            t1 = [al("t1b%d" % k, [128, TT], F32) for k in range(2)]
            R_t1 = [Res("t1b%d" % k) for k in range(2)]
            t2 = [al("t2b%d" % k, [128, TT], F32) for k in range(2)]
            R_t2 = [Res("t2b%d" % k) for k in range(2)]
            pt = [al("ptb%d" % k, [128, 1024], BF16) for k in range(3)]
            R_pt = [Res("ptb%d" % k) for k in range(3)]
            rrow = [(al("rlb%d" % k, [128, TT], F32), al("rbb%d" % k, [128, TT], BF16)) for k in range(2)]
            R_rrow = [Res("rrowb%d" % k) for k in range(2)]
            tmp = al("tmpb", [128, TT], F32)
            R_tmp = Res("tmpb")
            och = [al("ochb0", [128, S], BF16)] * 2
            R_och = [Res("ochb0")] * 2
            R_vinit = Res("vinitb")
            s.op("dve", lambda e: e.memset(vaug[0][:, :, 64:128], 1.0), writes=[R_vinit])
            s.op("dve", lambda e: e.memset(vaug[1][:, :, 0:64], 1.0), writes=[R_vinit])

            def load_w(g):
                k = g % 2
                wv = b_in[j_l][g].rearrange("p (c f) -> p c f", c=8)
                for hf in range(4):
                    cs_ = slice(2 * hf, 2 * hf + 2)
                    s.dma("sp", wst[:, :, :], wv[:, cs_, :], writes=[R_wst])
                    for (d0, d1, s0, s1) in ((0, 256, 0, 256), (256, 320, 256, 320), (320, 384, 256, 320), (384, 704, 320, 640)):
                        s.op("pool", lambda e, d0=d0, d1=d1, s0=s0, s1=s1, cs_=cs_: e.tensor_copy(
                            out=wbf[k][:, cs_, d0:d1], in_=wst[:, :, s0:s1]), reads=[R_wst], writes=[R_wbf[k]])

            cscount = [0]
            load_w(0)
            for g in range(4):
                wk = g % 2
                w = wbf[wk]
                for kind in range(3):
                    col = (0, 128, 256)[kind]
                    gcol = (G_BQ + j_l) if kind < 2 else (G_BK + j_l)
                    for tt in range(NT):
                        tsl = slice(tt * TT, (tt + 1) * TT)
                        ck = cscount[0] % 2
                        cscount[0] += 1
                        s.dma("sp", cst[ck][:, :], cos_in[:, tsl], writes=[R_cs[ck]])
                        s.dma("sp", snt[ck][:, :], sin_in[:, tsl], writes=[R_sn[ck]])
                        b = next_bank()

                        def mm(e, b=b, col=col, tsl=tsl, w=w):
                            for c in range(8):
                                ins = e.matmul(bank(b), w[:, c, col:col + 128], hT[:, c, tsl], start=(c == 0), stop=(c == 7))
                            return ins
                        s.op("pe", mm, reads=[R_wbf[wk]] + R_h[tt], writes=[R_b[b]])
                        s.op("act", lambda e, b=b, ck=ck: e.activation(out=sqb[ck][:, :], in_=bank(b), func=AF.Square),
                             reads=[R_b[b]], writes=[R_sqb[ck]])
                        s.op("act", lambda e, b=b, gcol=gcol, ck=ck: e.activation(out=ub[ck][:, :], in_=bank(b), func=AF.Copy,
                                                                                 scale=gv[:, gcol:gcol + 1]),
                             reads=[R_b[b], R_c], writes=[R_ub[ck]])
                        b2 = next_bank()
                        s.op("pe", lambda e, b2=b2, ck=ck: e.matmul(bank(b2), bones[:, :], sqb[ck][:, :], start=True, stop=True),
                             reads=[R_sqb[ck], R_c], writes=[R_b[b2]])
                        b3 = next_bank()
                        s.op("pe", lambda e, b3=b3, ck=ck: e.matmul(bank(b3), perm[:, :], ub[ck][:, :], start=True, stop=True),
                             reads=[R_ub[ck], R_c], writes=[R_b[b3]])
                        s.op("act", lambda e, b2=b2, ck=ck: e.activation(out=sdb[ck][:, :], in_=bank(b2), func=AF.Ln, scale=1.0 / 64,
                                                                          bias=epsc[:, 0:1]),
                             reads=[R_b[b2], R_c], writes=[R_sdb[ck]])
                        s.op("act", lambda e, ck=ck: e.activation(out=sdb[ck][:, :], in_=sdb[ck][:, :], func=AF.Exp, scale=-0.5),
                             reads=[R_sdb[ck]], writes=[R_sdb[ck]])
                        s.op("pool", lambda e, ck=ck: e.tensor_tensor(out=t1[ck][:, :], in0=ub[ck][:, :], in1=cst[ck][:, :], op=ALU.mult),
                             reads=[R_ub[ck], R_cs[ck]], writes=[R_t1[ck]])
                        s.op("dve", lambda e, b3=b3, ck=ck: e.tensor_tensor(out=t2[ck][:, :], in0=bank(b3), in1=snt[ck][:, :], op=ALU.mult),
                             reads=[R_b[b3], R_sn[ck]], writes=[R_t2[ck]])
                        s.op("pool", lambda e, ck=ck: e.tensor_tensor(out=t1[ck][:, :], in0=t1[ck][:, :], in1=t2[ck][:, :], op=ALU.add),
                             reads=[R_t1[ck], R_t2[ck]], writes=[R_t1[ck]])
                        if kind < 2:
                            dst, R_dst = qT[kind][:, tsl], R_q[kind][tt]
                            s.op("dve", lambda e, dst=dst, ck=ck: e.tensor_tensor(out=dst, in0=t1[ck][:, :], in1=sdb[ck][:, :], op=ALU.mult),
                                 reads=[R_t1[ck], R_sdb[ck]], writes=[R_dst])
                        else:
                            for a_ in range(2):
                                s.op("dve", lambda e, a_=a_, tsl=tsl, ck=ck: e.tensor_tensor(
                                    out=kz[a_][64 * a_:64 * a_ + 64, tsl], in0=t1[ck][64 * a_:64 * a_ + 64, :],
                                    in1=sdb[ck][64 * a_:64 * a_ + 64, :], op=ALU.mult),
                                    reads=[R_t1[ck], R_sdb[ck], R_kz], writes=[R_k[tt]])
                for tt in range(NT):
                    tsl = slice(tt * TT, (tt + 1) * TT)
                    for c2 in range(2):
                        col = 448 + 128 * c2
                        b = next_bank()

                        def mm(e, b=b, col=col, tsl=tsl, w=w):
                            for c in range(8):
                                ins = e.matmul(bank(b), w[:, c, col:col + 128], hT[:, c, tsl], start=(c == 0), stop=(c == 7))
                            return ins
                        s.op("pe", mm, reads=[R_wbf[wk]] + R_h[tt], writes=[R_b[b]])
                        s.op("act", lambda e, b=b, tsl=tsl, c2=c2: e.activation(out=sg[c2][:, tsl], in_=bank(b), func=AF.Silu),
                             reads=[R_b[b]], writes=[R_sg[c2][tt]])
                    b = next_bank()

                    def mmv(e, b=b, tt=tt, w=w):
                        for tcl in range(4):
                            t0 = tt * TT + tcl * 128
                            for c in range(8):
                                ins = e.matmul(bank(b, tcl * 64, (tcl + 1) * 64), hT[:, c, t0:t0 + 128], w[:, c, 384:448],
                                               start=(c == 0), stop=(c == 7))
                        return ins
                    s.op("pe", mmv, reads=[R_wbf[wk]] + R_h[tt], writes=[R_b[b]])
                    bv = bank(b, 0, 256).rearrange("p (t f) -> p t f", t=4)
                    s.op("dve", lambda e, bv=bv, tt=tt: e.tensor_copy(out=vaug[0][:, tt * 4:tt * 4 + 4, 0:64], in_=bv),
                         reads=[R_b[b], R_vinit], writes=[R_v[tt]])
                    s.op("dve", lambda e, bv=bv, tt=tt: e.tensor_copy(out=vaug[1][:, tt * 4:tt * 4 + 4, 64:128], in_=bv),
                         reads=[R_b[b], R_vinit], writes=[R_v[tt]])
                if g + 1 < 4:
                    load_w(g + 1)
                steps = [(hi, qt, kp) for hi in range(4) for qt in range(NT) for kp in range(16)]
                pend = []
                gq = [0]

                def b_qk(idx):
                    hi, qt, kp = steps[idx]
                    c2, a = hi // 2, hi % 2
                    lo = 64 * a
                    sb0 = 2 * (idx % 2)

                    def qk(e, kp=kp, sb0=sb0, qt=qt, lo=lo, c2=c2):
                        for z in range(2):
                            j = 2 * kp + z
                            ins = e.matmul(bank(sb0 + z), kz[lo // 64][:, j * 128:(j + 1) * 128],
                                           qT[c2][:, qt * TT:(qt + 1) * TT], start=True, stop=True)
                        return ins
                    s.op("pe", qk, reads=[R_k[kp // 2], R_q[c2][qt]], writes=[R_b[sb0], R_b[sb0 + 1]])

                def b_exp(idx):
                    sb0 = 2 * (idx % 2)
                    ek = idx % 3
                    s.op("act", lambda e, sb0=sb0, ek=ek: e.activation(
                        out=pt[ek][:, :], in_=ps[:, sb0 * 512:(sb0 + 2) * 512], func=AF.Exp, scale=0.125),
                        reads=[R_b[sb0], R_b[sb0 + 1]], writes=[R_pt[ek]])

                def b_pv(idx):
                    hi, qt, kp = steps[idx]
                    c2, a = hi // 2, hi % 2
                    ek = idx % 3
                    g_ = gq[0]
                    ob = 4 + (g_ % 2)

                    def pv(e, kp=kp, ek=ek, ob=ob, a=a):
                        for z in range(2):
                            j = 2 * kp + z
                            ins = e.matmul(bank(ob), vaug[a][:, j, :], pt[ek][:, z * 512:(z + 1) * 512],
                                           start=(kp == 0 and z == 0), stop=(kp == 15 and z == 1))
                        return ins
                    s.op("pe", pv, reads=[R_pt[ek], R_v[kp // 2]], writes=[R_b[ob]])
                    if kp == 15:
                        chunk = 2 * g + c2
                        ok = chunk % 2
                        finalize_a(a, ob, rrow[g_ % 2], R_rrow[g_ % 2])
                        pend.append((idx + 2, (a, ob, 6 + (g_ % 2), sg[c2], R_sg[c2][qt], qt, rrow[g_ % 2], R_rrow[g_ % 2],
                                               tmp, R_tmp, och[ok], R_och[ok]), (chunk, ok) if (a == 1 and qt == NT - 1) else None))
                        gq[0] += 1

                def b_fin(item):
                    finalize_b(*item[1])
                    if item[2] is not None:
                        chunk, ok = item[2]
                        s.dma("sp", o_scr[:, chunk, :], och[ok][:, :], reads=[R_och[ok]], writes=[R_oscr[chunk]])

                b_qk(0)
                for idx in range(len(steps)):
                    b_exp(idx)
                    if idx + 1 < len(steps):
                        b_qk(idx + 1)
                    b_pv(idx)
                    while pend and pend[0][0] <= idx:
                        b_fin(pend.pop(0))
                while pend:
                    b_fin(pend.pop(0))
            s.barrier()

    def phase_o(i, last):
        with contextlib.ExitStack() as st:
            al = lambda n, sh, dt: st.enter_context(nc.sbuf_tensor("s%d_%s" % (i, n), sh, dt))
            wst = al("wsto", [128, 2, 1024], F32)
            R_wst = Res("wsto")
            wo = al("wo", [128, 8, 1024], BF16)
            wg = al("wg", [128, 8, 1024], BF16)
            wp = al("wp", [128, 2, 1024], BF16)
            R_w = Res("w_o")
            xt = [al("xtO%d" % k, [128, 8, TT], F32) for k in range(2)]
            R_xt = [[Res("xtO%d_%d" % (k, c)) for c in range(8)] for k in range(2)]
            ot = [al("otO%d" % k, [128, 8, TT], BF16) for k in range(2)]
            R_ot = [Res("otO%d" % k) for k in range(2)]
            pin = [al("pin%d" % k, [128, 4, 256], F32) for k in range(2)]
            R_pin = [Res("pin%d" % k) for k in range(2)]
            pT = al("pT", [128, 2, TT], BF16)
            R_pT = Res("pT")
            sq = al("sqO", [128, 8, TT], BF16)
            R_sq = Res("sqO")
            sd = al("sdO", [128, TT], F32)
            R_sd = Res("sdO")
            n1 = sq
            R_n1 = R_sq
            gs = [al("gs%d" % k, [128, TT], F32) for k in range(2)]
            R_gs = [Res("gs%d" % k) for k in range(2)]
            if last:
                yo = al("yo", [128, 4, D], F32)
                R_yo = Res("yo")
            for (wdst, wsrc, nch) in ((wo, wo_in[i], 8), (wg, wg_in[i], 8), (wp, wp_in[i], 2)):
                wv = wsrc.rearrange("p (c f) -> p c f", c=nch)
                for c0 in range(0, nch, 2):
                    s.dma("sp", wst[:, :, :], wv[:, c0:c0 + 2, :], writes=[R_wst])
                    s.op("pool", lambda e, wdst=wdst, c0=c0: e.tensor_copy(out=wdst[:, c0:c0 + 2, :], in_=wst[:, :, :]),
                         reads=[R_wst], writes=[R_w])
            pv_ = p_in[i].rearrange("(t b p) f -> t p b f", b=4, p=128)
            ov = out.rearrange("(t b p) d -> t p b d", b=4, p=128)

            def loads(tt):
                k = tt % 2
                tsl = slice(tt * TT, (tt + 1) * TT)
                s.dma("sp", xt[k][:, :, :], x_scr[:, :, tsl], reads=[R_xscr[tt]], writes=R_xt[k], sem_res=R_xt[k][0])
                s.dma("sp", ot[k][:, :, :], o_scr[:, :, tsl], reads=R_oscr, writes=[R_ot[k]])
                s.dma("sp", pin[k][:, :, :], pv_[tt], writes=[R_pin[k]])

            loads(0)
            for tt in range(NT):
                k = tt % 2
                tsl = slice(tt * TT, (tt + 1) * TT)
                if tt + 1 < NT:
                    loads(tt + 1)
                X = xt[k]
                for fc in range(8):
                    b = next_bank()

                    def mm(e, b=b, fc=fc, k=k):
                        for c in range(8):
                            ins = e.matmul(bank(b), wo[:, c, fc * 128:(fc + 1) * 128], ot[k][:, c, :], start=(c == 0), stop=(c == 7))
                        return ins
                    s.op("pe", mm, reads=[R_w, R_ot[k]], writes=[R_b[b]])
                    s.op("dve", lambda e, b=b, fc=fc, X=X: e.tensor_tensor(out=X[:, fc, :], in0=X[:, fc, :], in1=bank(b), op=ALU.add),
                         reads=[R_b[b], R_xt[k][fc]], writes=[R_xt[k][fc]])
                emit_rstd(X, R_xt[k], sq, R_sq, sd, R_sd, 1.0 / D)
                for c in range(8):
                    eng = "dve"
                    s.op(eng, lambda e, c=c, X=X: e.scalar_tensor_tensor(
                        out=n1[:, c, :], in0=X[:, c, :], scalar=gv[:, g_ple(i) + c:g_ple(i) + c + 1], in1=sd[:, :],
                        op0=ALU.mult, op1=ALU.mult), reads=[R_xt[k][c], R_sd, R_c], writes=[R_n1])
                for c2 in range(2):
                    b = next_bank()

                    def tr(e, b=b, c2=c2, k=k):
                        for tb in range(4):
                            ins = e.transpose(out=bank(b, tb * 128, (tb + 1) * 128), in_=pin[k][:, tb, c2 * 128:(c2 + 1) * 128],
                                              identity=ident[:, :])
                        return ins
                    s.op("pe", tr, reads=[R_pin[k], R_c], writes=[R_b[b]])
                    s.op("act", lambda e, b=b, c2=c2: e.activation(out=pT[:, c2, :], in_=bank(b), func=AF.Copy),
                         reads=[R_b[b]], writes=[R_pT])
                for fc in range(8):
                    b = next_bank()

                    def mmg(e, b=b, fc=fc):
                        for c in range(8):
                            ins = e.matmul(bank(b), wg[:, c, fc * 128:(fc + 1) * 128], n1[:, c, :], start=(c == 0), stop=(c == 7))
                        return ins
                    s.op("pe", mmg, reads=[R_w, R_n1], writes=[R_b[b]])
                    gk = fc % 2
                    s.op("act", lambda e, b=b, gk=gk: e.activation(out=gs[gk][:, :], in_=bank(b), func=AF.Sigmoid),
                         reads=[R_b[b]], writes=[R_gs[gk]])
                    b2 = next_bank()

                    def mmp(e, b2=b2, fc=fc):
                        for c in range(2):
                            ins = e.matmul(bank(b2), wp[:, c, fc * 128:(fc + 1) * 128], pT[:, c, :], start=(c == 0), stop=(c == 1))
                        return ins
                    s.op("pe", mmp, reads=[R_w, R_pT], writes=[R_b[b2]])
                    s.op("dve", lambda e, b2=b2, gk=gk: e.tensor_tensor(out=gs[gk][:, :], in0=gs[gk][:, :], in1=bank(b2), op=ALU.mult),
                         reads=[R_gs[gk], R_b[b2]], writes=[R_gs[gk]])
                    s.op("pool", lambda e, fc=fc, X=X, gk=gk: e.tensor_tensor(out=X[:, fc, :], in0=X[:, fc, :], in1=gs[gk][:, :], op=ALU.add),
                         reads=[R_gs[gk], R_xt[k][fc]], writes=[R_xt[k][fc]])
                if dbg == ("x", i):
                    s.dma("sp", dbg_out[:, :, tsl], X[:, :, :], reads=R_xt[k], writes=[R_out[tt]])
                if not last:
                    s.dma("sp", x_scr[:, :, tsl], X[:, :, :], reads=R_xt[k], writes=[R_xscr[tt]])
                    emit_h(X, R_xt[k], sq, R_sq, sd, R_sd, tt, g_norm(i + 1))
                else:
                    emit_rstd(X, R_xt[k], sq, R_sq, sd, R_sd, 1.0 / D)
                    for c in range(8):
                        eng = "dve"
                        s.op(eng, lambda e, c=c, X=X: e.scalar_tensor_tensor(
                            out=X[:, c, :], in0=X[:, c, :], scalar=gv[:, G_FINAL + c:G_FINAL + c + 1], in1=sd[:, :],
                            op0=ALU.mult, op1=ALU.mult), reads=[R_xt[k][c], R_sd, R_c], writes=[R_xt[k][c]])
                    xn = X
                    R_xn = R_xt[k]
                    for tb in range(4):
                        for cq in range(2):
                            b = next_bank()

                            def tr(e, b=b, tb=tb, cq=cq, xn=xn):
                                for cc in range(4):
                                    c = cq * 4 + cc
                                    ins = e.transpose(out=bank(b, cc * 128, (cc + 1) * 128), in_=xn[:, c, tb * 128:(tb + 1) * 128],
                                                      identity=ident[:, :])
                                return ins
                            s.op("pe", tr, reads=R_xn + [R_c], writes=[R_b[b]])
                            if cq == 0:
                                s.op("dve", lambda e, b=b, tb=tb: e.tensor_copy(out=yo[:, tb, 0:512], in_=bank(b)),
                                     reads=[R_b[b]], writes=[R_yo])
                            else:
                                s.op("act", lambda e, b=b, tb=tb: e.activation(out=yo[:, tb, 512:1024], in_=bank(b), func=AF.Copy),
                                     reads=[R_b[b]], writes=[R_yo])
                    s.dma("sp", ov[tt], yo[:, :, :], reads=[R_yo], writes=[R_out[tt]])
            s.barrier()

    for i in range(n_layers):
        if dbg == ("h", i):
            s.dma("sp", dbg_out[:, :, :], hT[:, :, :], reads=[r for t in R_h for r in t], writes=[R_out[0]])
            break
        if i % 2 == 0:
            phase_ga(i)
        else:
            phase_gb(i)
        if dbg == ("o", i):
            s.dma("sp", dbg_out[:, :, :], o_scr[:, :, :], reads=R_oscr, writes=[R_out[0]])
            break
        phase_o(i, last=(i == n_layers - 1))
    s.barrier()
    s.emit()
    return nc


_CACHE = {}
_PADDED = ("cosT", "sinT", "a_in", "b_in", "wo", "wg", "wp", "tb")


def _prep_shared(inp):
    f = lambda a: np.ascontiguousarray(np.asarray(a, dtype=np.float32))
    sh = {}
    cosT, sinT, perm = _rope_tables()
    sh["cosT"], sh["sinT"], sh["perm"] = cosT, sinT, perm
    sh["ident"] = np.eye(128, dtype=np.float32)
    gv = np.zeros((128, 80), np.float32)
    ng, pg = f(inp["norm_g"]), f(inp["ple_norm_g"])
    for i in range(DEPTH):
        gv[:, 16 * i:16 * i + 8] = _vec_pc(ng[i])
        gv[:, 16 * i + 8:16 * i + 16] = _vec_pc(pg[i])
    gv[:, 64:72] = _vec_pc(f(inp["final_norm_g"]))
    bq, bk = f(inp["b_q_norm"]), f(inp["b_k_norm"])
    for j in range(2):
        gv[:, 72 + j] = np.tile(bq[j], 2)
        gv[:, 74 + j] = np.tile(bk[j], 2)
    sh["gv"] = gv
    awi, bwi = f(inp["a_w_in"]), f(inp["b_w_in"])
    for j in range(2):
        W = awi[j]
        groups = []
        for hp in range(8):
            cols = np.concatenate([W[:, k * 1024 + hp * 128:k * 1024 + (hp + 1) * 128] for k in range(4)], axis=1)
            groups.append(_pc(cols))
        sh["a_in%d" % j] = np.ascontiguousarray(np.stack(groups))
        W = bwi[j]
        groups = []
        for g in range(4):
            cols = np.concatenate([W[:, 256 * g:256 * (g + 1)], W[:, 1024 + 64 * g:1024 + 64 * (g + 1)],
                                   W[:, 1280 + 64 * g:1280 + 64 * (g + 1)], W[:, 1536 + 256 * g:1536 + 256 * (g + 1)]], axis=1)
            groups.append(_pc(cols))
        sh["b_in%d" % j] = np.ascontiguousarray(np.stack(groups))
        sh["tb%d" % j] = _a_tables(f(inp["a_rpb"])[j])
    awo, bwo = f(inp["a_w_out"]), f(inp["b_w_out"])
    wgt, wpj = f(inp["ple_w_gate"]), f(inp["ple_w_proj"])
    for i in range(DEPTH):
        sh["wo%d" % i] = _pc(awo[i // 2] if i % 2 == 0 else bwo[i // 2])
        sh["wg%d" % i] = _pc(wgt[i])
        sh["wp%d" % i] = _pc(wpj[i])
    return sh


def _core_inputs(sh, xb, pb, b):
    m = {}
    for k_, v_ in sh.items():
        if k_ in _PADDED or k_[:-1] in _PADDED:
            pad = np.full((1,) + v_.shape[1:], float(b), np.float32)
            m[k_] = np.concatenate([v_, pad], axis=0)
        else:
            m[k_] = v_
    m["x"] = np.ascontiguousarray(xb)
    m["p"] = np.ascontiguousarray(pb)
    return m


def kernel(x, p, norm_g, a_w_in, a_rpb, a_w_out, b_w_in, b_q_norm, b_k_norm, b_w_out,
           ple_norm_g, ple_w_gate, ple_w_proj, final_norm_g):
    inp = dict(norm_g=norm_g, a_w_in=a_w_in, a_rpb=a_rpb, a_w_out=a_w_out, b_w_in=b_w_in, b_q_norm=b_q_norm,
               b_k_norm=b_k_norm, b_w_out=b_w_out, ple_norm_g=ple_norm_g, ple_w_gate=ple_w_gate,
               ple_w_proj=ple_w_proj, final_norm_g=final_norm_g)
    sh = _prep_shared(inp)
    x = np.asarray(x, dtype=np.float32)
    p = np.asarray(p, dtype=np.float32)
    if "nc" not in _CACHE:
        _CACHE["nc"] = build()
    nc = _CACHE["nc"]
    in_maps = [_core_inputs(sh, x[b], p[:, b], b) for b in range(8)]
    res = run_bass_kernel_spmd(nc, in_maps, core_ids=list(range(8)))
    return np.stack([np.asarray(r["out"], dtype=np.float32) for r in res.results], axis=0)
```

```python
import contextlib
import numpy as np
import ml_dtypes
import concourse.bass as bass
import concourse.mybir as mybir
from concourse.bass_utils import run_bass_kernel_spmd

F32 = mybir.dt.float32
BF16 = mybir.dt.bfloat16
AF = mybir.ActivationFunctionType
ALU = mybir.AluOpType

S = 4096
D = 1024
NT = 8
TT = 512
DEPTH = 4
EPS = 1e-6
NE = 22
ENGS = ("pe", "act", "dve", "pool", "sp")


class Res:
    __slots__ = ("name", "w", "r", "sem", "semval")

    def __init__(self, name):
        self.name = name
        self.w = None
        self.r = []
        self.sem = None
        self.semval = 0


class Sched:
    def __init__(self, nc):
        self.nc = nc
        self.ops = {e: [] for e in ENGS}
        self.known = {e: {} for e in ENGS}
        self.dma_sems = []

    def _deps(self, eng, reads, writes):
        toks = []
        for r in reads:
            toks.append(r.w)
        for w in writes:
            toks.append(w.w)
            toks.extend(w.r)
        best = {}
        cur = len(self.ops[eng])
        for tok in toks:
            if tok is None:
                continue
            if tok[0] == "e":
                _, f, idx = tok
                if f == eng:
                    if eng == "pe":
                        continue
                    if eng in ("act", "dve") and idx < cur - 1:
                        continue
                key, val = ("e", f), idx
            else:
                key, val = ("s", tok[1]), tok[2]
            if self.known[eng].get(key, -1) >= val:
                continue
            if key not in best or best[key][0] < val:
                best[key] = (val, tok)
        waits = []
        for key, (val, tok) in best.items():
            self.known[eng][key] = val
            waits.append(tok)
            if tok[0] == "e":
                self.ops[tok[1]][tok[2]][2] = True
        return waits

    def _commit(self, tok, reads, writes):
        for r in reads:
            r.r.append(tok)
        for w in writes:
            w.w = tok
            w.r = []

    def op(self, eng, fn, reads=(), writes=()):
        waits = self._deps(eng, reads, writes)
        tok = ("e", eng, len(self.ops[eng]))
        self.ops[eng].append([fn, waits, False, None])
        self._commit(tok, reads, writes)
        return tok

    def dma(self, eng, out, in_, reads=(), writes=(), sem_res=None):
        waits = self._deps(eng, reads, writes)
        if sem_res is None:
            sem_res = writes[0]
        if sem_res.sem is None:
            sem_res.sem = len(self.dma_sems)
            self.dma_sems.append(sem_res)
        sem_res.semval += 16
        tok = ("s", sem_res.sem, sem_res.semval)

        def fn(e, out=out, in_=in_):
            return e.dma_start(out=out, in_=in_)

        self.ops[eng].append([fn, waits, False, sem_res.sem])
        self._commit(tok, reads, writes)
        return tok

    def barrier(self):
        last = {}
        for f in ENGS:
            for i in range(len(self.ops[f]) - 1, -1, -1):
                o = self.ops[f][i]
                if o[0] is not None and o[3] is None:
                    last[f] = ("e", f, i)
                    break
        for e in ENGS:
            waits = []
            cands = [last[f] for f in ENGS if f != e and f in last] + [("s", r.sem, r.semval) for r in self.dma_sems]
            for tok in cands:
                key = ("e", tok[1]) if tok[0] == "e" else ("s", tok[1])
                val = tok[2]
                if self.known[e].get(key, -1) >= val:
                    continue
                self.known[e][key] = val
                waits.append(tok)
                if tok[0] == "e":
                    self.ops[tok[1]][tok[2]][2] = True
            self.ops[e].append([None, waits, False, None])

    def emit(self):
        nc = self.nc
        val = {}
        for e in ENGS:
            c = 0
            for i, o in enumerate(self.ops[e]):
                if o[2]:
                    c += 1
                    val[(e, i)] = c
        with contextlib.ExitStack() as st:
            esem = {e: st.enter_context(nc.semaphore("prog_" + e)) for e in ENGS}
            dsem = [st.enter_context(nc.semaphore("dma_%d" % i)) for i in range(len(self.dma_sems))]
            block = st.enter_context(nc.Block())
            ops = self.ops

            def body(e, h):
                for fn, waits, needed, dinc in ops[e]:
                    for t in waits:
                        if t[0] == "e":
                            h.wait_ge(esem[t[1]], val[(t[1], t[2])])
                        else:
                            h.wait_ge(dsem[t[1]], t[2])
                    if fn is None:
                        continue
                    ins = fn(h)
                    if dinc is not None:
                        ins.then_inc(dsem[dinc], 16)
                    elif needed:
                        ins.then_inc(esem[e], 1)

            @block.tensor
            def _(h):
                body("pe", h)

            @block.scalar
            def _(h):
                body("act", h)

            @block.vector
            def _(h):
                body("dve", h)

            @block.gpsimd
            def _(h):
                body("pool", h)

            @block.sync
            def _(h):
                body("sp", h)


def _rope_tables():
    t = np.arange(S)
    row = (t // 64).astype(np.float32)
    col = (t % 64).astype(np.float32)
    inv = np.power(np.float32(10000.0), -np.arange(16, dtype=np.float32) * np.float32(2.0) / np.float32(32.0)).astype(np.float32)
    cosT = np.zeros((128, S), np.float32)
    sinT = np.zeros((128, S), np.float32)
    perm = np.zeros((128, 128), np.float32)
    for p in range(128):
        d = p % 64
        sec = d // 32
        dd = d % 32
        f = dd % 16
        pos = row if sec == 0 else col
        ang = (pos * inv[f]).astype(np.float32)
        cosT[p] = np.cos(ang)
        sn = np.sin(ang)
        if dd < 16:
            sinT[p] = -sn
            src = p + 16
        else:
            sinT[p] = sn
            src = p - 16
        perm[src, p] = 1.0
    return cosT, sinT, perm


def _a_tables(rpb):
    NEG = np.float32(-30000.0)
    kc = np.arange(64)[:, None]
    c = np.arange(64)[None, :]
    cs = np.clip(c - 8, 0, 48)
    colvalid = (kc >= cs) & (kc < cs + 16)
    dc = np.clip(kc - c + 15, 0, 30)
    out = np.full((16, 2, 64, 2, NE, 64), NEG, np.float32)
    for a in range(2):
        for e in range(NE):
            dr = 17 + a - e
            if 0 <= dr <= 14:
                blk = np.where(colvalid[None], rpb[:, dr][:, dc], NEG)
                out[:, a, :, 1, e, :] = blk
                if 3 <= dr <= 10:
                    out[:, a, :, 0, e, :] = blk
    return np.ascontiguousarray(out.reshape(16, 128, 2 * NE * 64))


def _pc(w):
    K = w.shape[0] // 128
    return np.ascontiguousarray(w.reshape(K, 128, w.shape[1]).transpose(1, 0, 2).reshape(128, K * w.shape[1]))


def _vec_pc(g):
    return np.ascontiguousarray(g.reshape(-1, 128).T)


def a_tiles():
    res = []
    for m in range(8):
        if m == 0:
            js = list(range(0, 6))
        elif m == 7:
            js = list(range(26, 32))
        else:
            js = list(range(4 * m - 2, 4 * m + 6))
        res.append((m, js))
    return res


def a_segments(m, j):
    if m == 0 and j <= 3:
        return [(0, 5, 1), (5, 8, 0)]
    if m == 7 and j >= 28:
        return [(0, 5, 0), (5, 8, 1)]
    return [(0, 8, 0)]


def build(n_layers=DEPTH, dbg=None):
    nc = bass.Bass("TRN2", target_bir_lowering=False)
    s = Sched(nc)

    def din(name, shape, dt=F32, pad=False):
        if pad:
            t = nc.dram_tensor(name, [shape[0] + 1] + list(shape[1:]), dt, kind="ExternalInput").ap()
            return t[0:shape[0]]
        return nc.dram_tensor(name, list(shape), dt, kind="ExternalInput").ap()

    x_in = din("x", [S, D])
    p_in = din("p", [DEPTH, S, 256])
    gv_in = din("gv", [128, 80])
    ident_in = din("ident", [128, 128])
    perm_in = din("perm", [128, 128])
    cos_in = din("cosT", [128, S], pad=True)
    sin_in = din("sinT", [128, S], pad=True)
    a_in = [din("a_in%d" % j, [8, 128, 8 * 512], pad=True) for j in range(2)]
    b_in = [din("b_in%d" % j, [4, 128, 8 * 640], pad=True) for j in range(2)]
    wo_in = [din("wo%d" % i, [128, 8 * 1024], pad=True) for i in range(DEPTH)]
    wg_in = [din("wg%d" % i, [128, 8 * 1024], pad=True) for i in range(DEPTH)]
    wp_in = [din("wp%d" % i, [128, 2 * 1024], pad=True) for i in range(DEPTH)]
    tb_in = [din("tb%d" % j, [16, 128, 2 * NE * 64], pad=True) for j in range(2)]
    out = nc.dram_tensor("out", [S, D], F32, kind="ExternalOutput").ap()
    x_scr = nc.dram_tensor("x_scr", [128, 8, S], F32, kind="Internal").ap()
    o_scr = nc.dram_tensor("o_scr", [128, 8, S], BF16, kind="Internal").ap()
    R_xscr = [Res("xscr%d" % t) for t in range(NT)]
    R_oscr = [Res("oscr%d" % c) for c in range(8)]
    R_out = [Res("out%d" % t) for t in range(NT)]
    dbg_out = None
    if dbg:
        dbg_out = nc.dram_tensor("dbg", [128, 8, S], F32 if dbg[0] == "x" else BF16, kind="ExternalOutput").ap()

    hT = nc.alloc_sbuf_tensor("sb_hT", [128, 8, S], BF16)
    R_h = [[Res("h%d_%d" % (t, c)) for c in range(8)] for t in range(NT)]
    ident = nc.alloc_sbuf_tensor("sb_ident", [128, 128], F32)
    perm = nc.alloc_sbuf_tensor("sb_perm", [128, 128], F32)
    gv = nc.alloc_sbuf_tensor("sb_gv", [128, 80], F32)
    ones_bf = nc.alloc_sbuf_tensor("sb_ones_bf", [128, 128], BF16)
    ones_f = nc.alloc_sbuf_tensor("sb_ones_f", [128, 128], F32)
    bones = nc.alloc_sbuf_tensor("sb_bones", [128, 128], BF16)
    epsc = nc.alloc_sbuf_tensor("sb_epsc", [128, 1], F32)
    R_c = Res("consts")
    ps = nc.alloc_psum_tensor("ps", [128, 8 * 512], F32)
    R_b = [Res("bank%d" % b) for b in range(8)]

    def bank(b, lo=0, hi=512):
        return ps[:, b * 512 + lo:b * 512 + hi]

    s.dma("sp", ident[:, :], ident_in[:, :], writes=[R_c])
    s.dma("sp", perm[:, :], perm_in[:, :], writes=[R_c])
    s.dma("sp", gv[:, :], gv_in[:, :], writes=[R_c])
    s.op("pool", lambda e: e.memset(ones_bf[:, :], 1.0), writes=[R_c])
    s.op("pool", lambda e: e.memset(ones_f[:, :], 1.0), writes=[R_c])
    s.op("pool", lambda e: e.memset(bones[:, :], 0.0), writes=[R_c])
    s.op("pool", lambda e: e.memset(bones[0:64, 0:64], 1.0), writes=[R_c])
    s.op("pool", lambda e: e.memset(bones[64:128, 64:128], 1.0), writes=[R_c])
    s.op("pool", lambda e: e.memset(epsc[:, :], EPS), writes=[R_c])
    s.barrier()

    def g_norm(i):
        return 16 * i

    def g_ple(i):
        return 16 * i + 8

    G_FINAL = 64
    G_BQ = 72
    G_BK = 74

    bank_rr = [0]

    def next_bank():
        b = bank_rr[0]
        bank_rr[0] = (b + 1) % 8
        return b

    def emit_rstd(xt, R_xt, sq, R_sq, sd, R_sd, width_scale):
        s.op("act", lambda e: e.activation(out=sq[:, :, :], in_=xt[:, :, :], func=AF.Square),
             reads=R_xt, writes=[R_sq])
        b = next_bank()

        def mm(e):
            for c in range(8):
                ins = e.matmul(bank(b), ones_bf[:, :], sq[:, c, :], start=(c == 0), stop=(c == 7))
            return ins
        s.op("pe", mm, reads=[R_sq, R_c], writes=[R_b[b]])
        s.op("act", lambda e: e.activation(out=sd[:, :], in_=bank(b), func=AF.Ln, scale=width_scale, bias=epsc[:, 0:1]),
             reads=[R_b[b], R_c], writes=[R_sd])
        s.op("act", lambda e: e.activation(out=sd[:, :], in_=sd[:, :], func=AF.Exp, scale=-0.5), reads=[R_sd], writes=[R_sd])

    def emit_h(xt, R_xt, sq, R_sq, sd, R_sd, tt, gcol):
        emit_rstd(xt, R_xt, sq, R_sq, sd, R_sd, 1.0 / D)
        for c in range(8):
            eng = "dve"
            s.op(eng, lambda e, c=c: e.scalar_tensor_tensor(
                out=hT[:, c, tt * TT:(tt + 1) * TT], in0=xt[:, c, :], scalar=gv[:, gcol + c:gcol + c + 1],
                in1=sd[:, :], op0=ALU.mult, op1=ALU.mult),
                reads=[R_xt[c], R_sd, R_c], writes=[R_h[tt][c]])

    with contextlib.ExitStack() as st:
        xin = [st.enter_context(nc.sbuf_tensor("sT_xin%d" % i, [128, 4, D], F32)) for i in range(2)]
        R_xin = [Res("xin%d" % i) for i in range(2)]
        xt = [st.enter_context(nc.sbuf_tensor("xtT%d" % i, [128, 8, TT], F32)) for i in range(2)]
        R_xt = [[Res("xtT%d_%d" % (i, c)) for c in range(8)] for i in range(2)]
        sq = st.enter_context(nc.sbuf_tensor("sqT", [128, 8, TT], BF16))
        R_sq = Res("sqT")
        sd = st.enter_context(nc.sbuf_tensor("sdT", [128, TT], F32))
        R_sd = Res("sdT")
        xv = x_in.rearrange("(t b p) d -> t p b d", b=4, p=128)
        for tt in range(NT):
            k = tt % 2
            s.dma("sp", xin[k][:, :, :], xv[tt], writes=[R_xin[k]])
            for c in range(8):
                b = next_bank()

                def tr(e, c=c, b=b, k=k):
                    for tb in range(4):
                        ins = e.transpose(out=bank(b, tb * 128, (tb + 1) * 128), in_=xin[k][:, tb, c * 128:(c + 1) * 128],
                                          identity=ident[:, :])
                    return ins
                s.op("pe", tr, reads=[R_xin[k], R_c], writes=[R_b[b]])
                if c % 2 == 0:
                    s.op("dve", lambda e, c=c, b=b, k=k: e.tensor_copy(out=xt[k][:, c, :], in_=bank(b)),
                         reads=[R_b[b]], writes=[R_xt[k][c]])
                else:
                    s.op("act", lambda e, c=c, b=b, k=k: e.activation(out=xt[k][:, c, :], in_=bank(b), func=AF.Copy),
                         reads=[R_b[b]], writes=[R_xt[k][c]])
            s.dma("sp", x_scr[:, :, tt * TT:(tt + 1) * TT], xt[k][:, :, :], reads=R_xt[k], writes=[R_xscr[tt]])
            emit_h(xt[k], R_xt[k], sq, R_sq, sd, R_sd, tt, g_norm(0))
        s.barrier()

    def finalize_a(a, ob, rrow, R_rrow):
        dp = 64 if a == 0 else 0
        rl, rb = rrow
        s.op("act", lambda e: e.activation(out=rl[dp:dp + 1, :], in_=bank(ob)[dp:dp + 1, :], func=AF.Ln),
             reads=[R_b[ob]], writes=[R_rrow])
        s.op("act", lambda e: e.activation(out=rb[dp:dp + 1, :], in_=rl[dp:dp + 1, :], func=AF.Exp, scale=-1.0),
             reads=[R_rrow], writes=[R_rrow])

    def finalize_b(a, ob, obc, sg, R_sg_t, qt, rrow, R_rrow, tmp, R_tmp, ochunk, R_ochunk):
        dp = 64 if a == 0 else 0
        lo = 64 * a
        s.op("pe", lambda e: e.matmul(bank(obc), ones_bf[dp:dp + 1, :], rrow[1][dp:dp + 1, :], start=True, stop=True),
             reads=[R_rrow, R_c], writes=[R_b[obc]])
        s.op("dve", lambda e: e.tensor_tensor(out=tmp[lo:lo + 64, :], in0=bank(ob)[lo:lo + 64, :],
                                              in1=sg[lo:lo + 64, qt * TT:(qt + 1) * TT], op=ALU.mult),
             reads=[R_b[ob], R_sg_t], writes=[R_tmp])
        s.op("dve", lambda e: e.tensor_tensor(out=ochunk[lo:lo + 64, qt * TT:(qt + 1) * TT], in0=tmp[lo:lo + 64, :],
                                              in1=bank(obc)[lo:lo + 64, :], op=ALU.mult),
             reads=[R_tmp, R_b[obc]], writes=[R_ochunk])

    def phase_ga(i):
        j_l = i // 2
        with contextlib.ExitStack() as st:
            al = lambda n, sh, dt: st.enter_context(nc.sbuf_tensor("s%d_%s" % (i, n), sh, dt))
            wst = al("wst", [128, 8, 512], F32)
            R_wst = Res("wst")
            wbf = [al("wbf%d" % k, [128, 8, 512], BF16) for k in range(2)]
            R_wbf = [Res("wbf%d" % k) for k in range(2)]
            qT = al("qT", [128, S], BF16)
            kz = [al("kz%d" % a, [128, S], BF16) for a in range(2)]
            R_kz = Res("kzinit")
            s.op("dve", lambda e: e.memset(kz[0][64:128, :], 0.0), writes=[R_kz])
            s.op("dve", lambda e: e.memset(kz[1][0:64, :], 0.0), writes=[R_kz])
            sg = al("sg", [128, S], BF16)
            R_q = [Res("q%d" % t) for t in range(NT)]
            R_k = [Res("k%d" % t) for t in range(NT)]
            R_sg = [Res("sg%d" % t) for t in range(NT)]
            vaug = [al("vaug%d" % a, [128, 32, 128], BF16) for a in range(2)]
            R_v = [Res("v%d" % t) for t in range(NT)]
            tst = al("tst", [128, 2 * NE * 64], F32)
            R_tst = Res("tst")
            tbf = [al("tbf%d" % k, [128, 2, NE, 64], BF16) for k in range(2)]
            R_tbf = [Res("tbf%d" % k) for k in range(2)]
            exs = [al("exs%d" % k, [128, 1024], BF16) for k in range(2)]
            R_exs = [Res("exs%d" % k) for k in range(2)]
            pt = [al("pt%d" % k, [128, 1024], BF16) for k in range(2)]
            R_pt = [[Res("pt%d_%d" % (k, z)) for z in range(2)] for k in range(2)]
            rrow = [(al("rl%d" % k, [128, TT], F32), al("rb%d" % k, [128, TT], BF16)) for k in range(2)]
            R_rrow = [Res("rrow%d" % k) for k in range(2)]
            tmp = al("tmp", [128, TT], F32)
            R_tmp = Res("tmp")
            och = [al("och%d" % k, [128, S], BF16) for k in range(2)]
            R_och = [Res("och%d" % k) for k in range(2)]
            R_vinit = Res("vinit")
            s.op("dve", lambda e: e.memset(vaug[0][:, :, 64:128], 1.0), writes=[R_vinit])
            s.op("dve", lambda e: e.memset(vaug[1][:, :, 0:64], 1.0), writes=[R_vinit])

            def load_w(hp):
                k = hp % 2
                s.dma("sp", wst[:, :, :], a_in[j_l][hp].rearrange("p (c f) -> p c f", c=8), writes=[R_wst])
                s.op("pool", lambda e: e.tensor_copy(out=wbf[k][:, :, :], in_=wst[:, :, :]),
                     reads=[R_wst], writes=[R_wbf[k]])

            tcount = [0]

            def load_tab(h):
                k = tcount[0] % 2
                tcount[0] += 1
                s.dma("sp", tst[:, :], tb_in[j_l][h], writes=[R_tst])
                s.op("act", lambda e: e.activation(out=tbf[k][:, :, :, :].rearrange("p a e c -> p (a e c)"), in_=tst[:, :], func=AF.Exp),
                     reads=[R_tst], writes=[R_tbf[k]])
                return k

            load_w(0)
            for hp in range(8):
                wk = hp % 2
                w = wbf[wk]
                for tt in range(NT):
                    tsl = slice(tt * TT, (tt + 1) * TT)
                    for kind in range(3):
                        col = (0, 128, 384)[kind]
                        b = next_bank()

                        def mm(e, b=b, col=col, tsl=tsl, w=w):
                            for c in range(8):
                                ins = e.matmul(bank(b), w[:, c, col:col + 128], hT[:, c, tsl], start=(c == 0), stop=(c == 7))
                            return ins
                        s.op("pe", mm, reads=[R_wbf[wk]] + R_h[tt], writes=[R_b[b]])
                        if kind == 0:
                            s.op("dve", lambda e, b=b, tsl=tsl: e.tensor_copy(out=qT[:, tsl], in_=bank(b)),
                                 reads=[R_b[b]], writes=[R_q[tt]])
                        elif kind == 1:
                            s.op("dve", lambda e, b=b, tsl=tsl: e.tensor_copy(out=kz[0][0:64, tsl], in_=bank(b)[0:64, :]),
                                 reads=[R_b[b], R_kz], writes=[R_k[tt]])
                            s.op("dve", lambda e, b=b, tsl=tsl: e.tensor_copy(out=kz[1][64:128, tsl], in_=bank(b)[64:128, :]),
                                 reads=[R_b[b], R_kz], writes=[R_k[tt]])
                        else:
                            s.op("act", lambda e, b=b, tsl=tsl: e.activation(out=sg[:, tsl], in_=bank(b), func=AF.Silu),
                                 reads=[R_b[b]], writes=[R_sg[tt]])
                    b = next_bank()

                    def mmv(e, b=b, tt=tt, w=w):
                        for tcl in range(4):
                            t0 = tt * TT + tcl * 128
                            for c in range(8):
                                ins = e.matmul(bank(b, tcl * 128, (tcl + 1) * 128), hT[:, c, t0:t0 + 128], w[:, c, 256:384],
                                               start=(c == 0), stop=(c == 7))
                        return ins
                    s.op("pe", mmv, reads=[R_wbf[wk]] + R_h[tt], writes=[R_b[b]])
                    bv = bank(b).rearrange("p (t f) -> p t f", t=4)
                    s.op("dve", lambda e, bv=bv, tt=tt: e.tensor_copy(out=vaug[0][:, tt * 4:tt * 4 + 4, 0:64], in_=bv[:, :, 0:64]),
                         reads=[R_b[b], R_vinit], writes=[R_v[tt]])
                    s.op("dve", lambda e, bv=bv, tt=tt: e.tensor_copy(out=vaug[1][:, tt * 4:tt * 4 + 4, 64:128], in_=bv[:, :, 64:128]),
                         reads=[R_b[b], R_vinit], writes=[R_v[tt]])
                if hp + 1 < 8:
                    load_w(hp + 1)
                ok = hp % 2
                steps = []
                for a in range(2):
                    tk = load_tab(2 * hp + a)
                    for (m, js) in a_tiles():
                        prs = [js[z:z + 2] for z in range(0, len(js), 2)]
                        for pi_, pr in enumerate(prs):
                            steps.append((a, tk, m, pr, pi_ == 0, pi_ == len(prs) - 1))
                gq = [0]

                def a_qk(idx):
                    a, tk, m, pr, first, last = steps[idx]
                    sb0 = 2 * (idx % 2)
                    lo = 64 * a

                    def qk(e, pr=pr, sb0=sb0, m=m, lo=lo):
                        for z, j in enumerate(pr):
                            ins = e.matmul(bank(sb0 + z), kz[lo // 64][:, j * 128:(j + 1) * 128],
                                           qT[:, m * TT:(m + 1) * TT], start=True, stop=True)
                        return ins
                    s.op("pe", qk, reads=[R_k[jj // 4] for jj in pr] + [R_q[m]], writes=[R_b[sb0], R_b[sb0 + 1]])

                def a_exp(idx):
                    a, tk, m, pr, first, last = steps[idx]
                    sb0 = 2 * (idx % 2)
                    ek = idx % 2
                    s.op("act", lambda e, sb0=sb0, ek=ek: e.activation(
                        out=exs[ek][:, :], in_=ps[:, sb0 * 512:(sb0 + 2) * 512], func=AF.Exp, scale=0.125),
                        reads=[R_b[sb0], R_b[sb0 + 1]], writes=[R_exs[ek]])

                def a_mask(idx):
                    a, tk, m, pr, first, last = steps[idx]
                    ek = idx % 2
                    for z, j in enumerate(pr):
                        e0 = 10 - (2 * j - 8 * m)
                        for (b0, b1, tab) in a_segments(m, j):
                            s.op("dve", lambda e, z=z, e0=e0, b0=b0, b1=b1, tab=tab, ek=ek, tk=tk: e.tensor_tensor(
                                out=pt[ek][:, z * 512 + b0 * 64:z * 512 + b1 * 64],
                                in0=exs[ek][:, z * 512 + b0 * 64:z * 512 + b1 * 64],
                                in1=tbf[tk][:, tab, e0 + b0:e0 + b1, :].rearrange("p e c -> p (e c)"), op=ALU.mult),
                                reads=[R_exs[ek], R_tbf[tk]], writes=[R_pt[ek][z]])

                pend = []

                def a_pv(idx):
                    a, tk, m, pr, first, last = steps[idx]
                    ek = idx % 2
                    g_ = gq[0]
                    ob = 4 + (g_ % 2)

                    def pv(e, pr=pr, ek=ek, ob=ob, first=first, last=last, a=a):
                        for z, j in enumerate(pr):
                            ins = e.matmul(bank(ob), vaug[a][:, j, :], pt[ek][:, z * 512:(z + 1) * 512],
                                           start=(first and z == 0), stop=(last and z == len(pr) - 1))
                        return ins
                    s.op("pe", pv, reads=R_pt[ek] + [R_v[jj // 4] for jj in pr], writes=[R_b[ob]])
                    if last:
                        finalize_a(a, ob, rrow[g_ % 2], R_rrow[g_ % 2])
                        pend.append((idx + 2, (a, ob, 6 + (g_ % 2), sg, R_sg[m], m, rrow[g_ % 2], R_rrow[g_ % 2],
                                               tmp, R_tmp, och[ok], R_och[ok])))
                        gq[0] += 1

                a_qk(0)
                for idx in range(len(steps)):
                    a_exp(idx)
                    if idx + 1 < len(steps):
                        a_qk(idx + 1)
                    a_mask(idx)
                    a_pv(idx)
                    while pend and pend[0][0] <= idx:
                        finalize_b(*pend.pop(0)[1])
                while pend:
                    finalize_b(*pend.pop(0)[1])
                s.dma("sp", o_scr[:, hp, :], och[ok][:, :], reads=[R_och[ok]], writes=[R_oscr[hp]])
            s.barrier()

    def phase_gb(i):
        j_l = i // 2
        with contextlib.ExitStack() as st:
            al = lambda n, sh, dt: st.enter_context(nc.sbuf_tensor("s%d_%s" % (i, n), sh, dt))
            wst = al("wstb", [128, 2, 640], F32)
            R_wst = Res("wstb")
            wbf = [al("wbfb%d" % k, [128, 8, 704], BF16) for k in range(2)]
            R_wbf = [Res("wbfb%d" % k) for k in range(2)]
            qT = [al("qTb%d" % c, [128, S], BF16) for c in range(2)]
            kz = [al("kzb%d" % a, [128, S], BF16) for a in range(2)]
            R_kz = Res("kzinitb")
            s.op("dve", lambda e: e.memset(kz[0][64:128, :], 0.0), writes=[R_kz])
            s.op("dve", lambda e: e.memset(kz[1][0:64, :], 0.0), writes=[R_kz])
            sg = [al("sgb%d" % c, [128, S], BF16) for c in range(2)]
            R_q = [[Res("qb%d_%d" % (c, t)) for t in range(NT)] for c in range(2)]
            R_k = [Res("kb%d" % t) for t in range(NT)]
            R_sg = [[Res("sgb%d_%d" % (c, t)) for t in range(NT)] for c in range(2)]
            vaug = [al("vaugb%d" % a, [128, 32, 128], BF16) for a in range(2)]
            R_v = [Res("vb%d" % t) for t in range(NT)]
            cst = [al("cst%d" % k, [128, TT], F32) for k in range(2)]
            snt = [al("snt%d" % k, [128, TT], F32) for k in range(2)]
            R_cs = [Res("cs%d" % k) for k in range(2)]
            R_sn = [Res("sn%d" % k) for k in range(2)]
            sqb = [al("sqb%d" % k, [128, TT], BF16) for k in range(2)]
            R_sqb = [Res("sqb%d" % k) for k in range(2)]
            ub = [al("ub%d" % k, [128, TT], F32) for k in range(2)]
            R_ub = [Res("ub%d" % k) for k in range(2)]
            sdb = [al("sdb%d" % k, [128, TT], F32) for k in range(2)]
            R_sdb = [Res("sdb%d" % k) for k in range(2)]
            t1 = [al("t1b%d" % k, [128, TT], F32) for k in range(2)]
            R_t1 = [Res("t1b%d" % k) for k in range(2)]
            t2 = [al("t2b%d" % k, [128, TT], F32) for k in range(2)]
            R_t2 = [Res("t2b%d" % k) for k in range(2)]
            pt = [al("ptb%d" % k, [128, 1024], BF16) for k in range(3)]
            R_pt = [Res("ptb%d" % k) for k in range(3)]
            rrow = [(al("rlb%d" % k, [128, TT], F32), al("rbb%d" % k, [128, TT], BF16)) for k in range(2)]
            R_rrow = [Res("rrowb%d" % k) for k in range(2)]
            tmp = al("tmpb", [128, TT], F32)
            R_tmp = Res("tmpb")
            och = [al("ochb0", [128, S], BF16)] * 2
            R_och = [Res("ochb0")] * 2
            R_vinit = Res("vinitb")
            s.op("dve", lambda e: e.memset(vaug[0][:, :, 64:128], 1.0), writes=[R_vinit])
            s.op("dve", lambda e: e.memset(vaug[1][:, :, 0:64], 1.0), writes=[R_vinit])

            def load_w(g):
                k = g % 2
                wv = b_in[j_l][g].rearrange("p (c f) -> p c f", c=8)
                for hf in range(4):
                    cs_ = slice(2 * hf, 2 * hf + 2)
                    s.dma("sp", wst[:, :, :], wv[:, cs_, :], writes=[R_wst])
                    for (d0, d1, s0, s1) in ((0, 256, 0, 256), (256, 320, 256, 320), (320, 384, 256, 320), (384, 704, 320, 640)):
                        s.op("pool", lambda e, d0=d0, d1=d1, s0=s0, s1=s1, cs_=cs_: e.tensor_copy(
                            out=wbf[k][:, cs_, d0:d1], in_=wst[:, :, s0:s1]), reads=[R_wst], writes=[R_wbf[k]])

            cscount = [0]
            load_w(0)
            for g in range(4):
                wk = g % 2
                w = wbf[wk]
                for kind in range(3):
                    col = (0, 128, 256)[kind]
                    gcol = (G_BQ + j_l) if kind < 2 else (G_BK + j_l)
                    for tt in range(NT):
                        tsl = slice(tt * TT, (tt + 1) * TT)
                        ck = cscount[0] % 2
                        cscount[0] += 1
                        s.dma("sp", cst[ck][:, :], cos_in[:, tsl], writes=[R_cs[ck]])
                        s.dma("sp", snt[ck][:, :], sin_in[:, tsl], writes=[R_sn[ck]])
                        b = next_bank()

                        def mm(e, b=b, col=col, tsl=tsl, w=w):
                            for c in range(8):
                                ins = e.matmul(bank(b), w[:, c, col:col + 128], hT[:, c, tsl], start=(c == 0), stop=(c == 7))
                            return ins
                        s.op("pe", mm, reads=[R_wbf[wk]] + R_h[tt], writes=[R_b[b]])
                        s.op("act", lambda e, b=b, ck=ck: e.activation(out=sqb[ck][:, :], in_=bank(b), func=AF.Square),
                             reads=[R_b[b]], writes=[R_sqb[ck]])
                        s.op("act", lambda e, b=b, gcol=gcol, ck=ck: e.activation(out=ub[ck][:, :], in_=bank(b), func=AF.Copy,
                                                                                 scale=gv[:, gcol:gcol + 1]),
                             reads=[R_b[b], R_c], writes=[R_ub[ck]])
                        b2 = next_bank()
                        s.op("pe", lambda e, b2=b2, ck=ck: e.matmul(bank(b2), bones[:, :], sqb[ck][:, :], start=True, stop=True),
                             reads=[R_sqb[ck], R_c], writes=[R_b[b2]])
                        b3 = next_bank()
                        s.op("pe", lambda e, b3=b3, ck=ck: e.matmul(bank(b3), perm[:, :], ub[ck][:, :], start=True, stop=True),
                             reads=[R_ub[ck], R_c], writes=[R_b[b3]])
                        s.op("act", lambda e, b2=b2, ck=ck: e.activation(out=sdb[ck][:, :], in_=bank(b2), func=AF.Ln, scale=1.0 / 64,
                                                                          bias=epsc[:, 0:1]),
                             reads=[R_b[b2], R_c], writes=[R_sdb[ck]])
                        s.op("act", lambda e, ck=ck: e.activation(out=sdb[ck][:, :], in_=sdb[ck][:, :], func=AF.Exp, scale=-0.5),
                             reads=[R_sdb[ck]], writes=[R_sdb[ck]])
                        s.op("pool", lambda e, ck=ck: e.tensor_tensor(out=t1[ck][:, :], in0=ub[ck][:, :], in1=cst[ck][:, :], op=ALU.mult),
                             reads=[R_ub[ck], R_cs[ck]], writes=[R_t1[ck]])
                        s.op("dve", lambda e, b3=b3, ck=ck: e.tensor_tensor(out=t2[ck][:, :], in0=bank(b3), in1=snt[ck][:, :], op=ALU.mult),
                             reads=[R_b[b3], R_sn[ck]], writes=[R_t2[ck]])
                        s.op("pool", lambda e, ck=ck: e.tensor_tensor(out=t1[ck][:, :], in0=t1[ck][:, :], in1=t2[ck][:, :], op=ALU.add),
                             reads=[R_t1[ck], R_t2[ck]], writes=[R_t1[ck]])
                        if kind < 2:
                            dst, R_dst = qT[kind][:, tsl], R_q[kind][tt]
                            s.op("dve", lambda e, dst=dst, ck=ck: e.tensor_tensor(out=dst, in0=t1[ck][:, :], in1=sdb[ck][:, :], op=ALU.mult),
                                 reads=[R_t1[ck], R_sdb[ck]], writes=[R_dst])
                        else:
                            for a_ in range(2):
                                s.op("dve", lambda e, a_=a_, tsl=tsl, ck=ck: e.tensor_tensor(
                                    out=kz[a_][64 * a_:64 * a_ + 64, tsl], in0=t1[ck][64 * a_:64 * a_ + 64, :],
                                    in1=sdb[ck][64 * a_:64 * a_ + 64, :], op=ALU.mult),
                                    reads=[R_t1[ck], R_sdb[ck], R_kz], writes=[R_k[tt]])
                for tt in range(NT):
                    tsl = slice(tt * TT, (tt + 1) * TT)
                    for c2 in range(2):
                        col = 448 + 128 * c2
                        b = next_bank()

                        def mm(e, b=b, col=col, tsl=tsl, w=w):
                            for c in range(8):
                                ins = e.matmul(bank(b), w[:, c, col:col + 128], hT[:, c, tsl], start=(c == 0), stop=(c == 7))
                            return ins
                        s.op("pe", mm, reads=[R_wbf[wk]] + R_h[tt], writes=[R_b[b]])
                        s.op("act", lambda e, b=b, tsl=tsl, c2=c2: e.activation(out=sg[c2][:, tsl], in_=bank(b), func=AF.Silu),
                             reads=[R_b[b]], writes=[R_sg[c2][tt]])
                    b = next_bank()

                    def mmv(e, b=b, tt=tt, w=w):
                        for tcl in range(4):
                            t0 = tt * TT + tcl * 128
                            for c in range(8):
                                ins = e.matmul(bank(b, tcl * 64, (tcl + 1) * 64), hT[:, c, t0:t0 + 128], w[:, c, 384:448],
                                               start=(c == 0), stop=(c == 7))
                        return ins
                    s.op("pe", mmv, reads=[R_wbf[wk]] + R_h[tt], writes=[R_b[b]])
                    bv = bank(b, 0, 256).rearrange("p (t f) -> p t f", t=4)
                    s.op("dve", lambda e, bv=bv, tt=tt: e.tensor_copy(out=vaug[0][:, tt * 4:tt * 4 + 4, 0:64], in_=bv),
                         reads=[R_b[b], R_vinit], writes=[R_v[tt]])
                    s.op("dve", lambda e, bv=bv, tt=tt: e.tensor_copy(out=vaug[1][:, tt * 4:tt * 4 + 4, 64:128], in_=bv),
                         reads=[R_b[b], R_vinit], writes=[R_v[tt]])
                if g + 1 < 4:
                    load_w(g + 1)
                steps = [(hi, qt, kp) for hi in range(4) for qt in range(NT) for kp in range(16)]
                pend = []
                gq = [0]

                def b_qk(idx):
                    hi, qt, kp = steps[idx]
                    c2, a = hi // 2, hi % 2
                    lo = 64 * a
                    sb0 = 2 * (idx % 2)

                    def qk(e, kp=kp, sb0=sb0, qt=qt, lo=lo, c2=c2):
                        for z in range(2):
                            j = 2 * kp + z
                            ins = e.matmul(bank(sb0 + z), kz[lo // 64][:, j * 128:(j + 1) * 128],
                                           qT[c2][:, qt * TT:(qt + 1) * TT], start=True, stop=True)
                        return ins
                    s.op("pe", qk, reads=[R_k[kp // 2], R_q[c2][qt]], writes=[R_b[sb0], R_b[sb0 + 1]])

                def b_exp(idx):
                    sb0 = 2 * (idx % 2)
                    ek = idx % 3
                    s.op("act", lambda e, sb0=sb0, ek=ek: e.activation(
                        out=pt[ek][:, :], in_=ps[:, sb0 * 512:(sb0 + 2) * 512], func=AF.Exp, scale=0.125),
                        reads=[R_b[sb0], R_b[sb0 + 1]], writes=[R_pt[ek]])

                def b_pv(idx):
                    hi, qt, kp = steps[idx]
                    c2, a = hi // 2, hi % 2
                    ek = idx % 3
                    g_ = gq[0]
                    ob = 4 + (g_ % 2)

                    def pv(e, kp=kp, ek=ek, ob=ob, a=a):
                        for z in range(2):
                            j = 2 * kp + z
                            ins = e.matmul(bank(ob), vaug[a][:, j, :], pt[ek][:, z * 512:(z + 1) * 512],
                                           start=(kp == 0 and z == 0), stop=(kp == 15 and z == 1))
                        return ins
                    s.op("pe", pv, reads=[R_pt[ek], R_v[kp // 2]], writes=[R_b[ob]])
                    if kp == 15:
                        chunk = 2 * g + c2
                        ok = chunk % 2
                        finalize_a(a, ob, rrow[g_ % 2], R_rrow[g_ % 2])
                        pend.append((idx + 2, (a, ob, 6 + (g_ % 2), sg[c2], R_sg[c2][qt], qt, rrow[g_ % 2], R_rrow[g_ % 2],
                                               tmp, R_tmp, och[ok], R_och[ok]), (chunk, ok) if (a == 1 and qt == NT - 1) else None))
                        gq[0] += 1

                def b_fin(item):
                    finalize_b(*item[1])
                    if item[2] is not None:
                        chunk, ok = item[2]
                        s.dma("sp", o_scr[:, chunk, :], och[ok][:, :], reads=[R_och[ok]], writes=[R_oscr[chunk]])

                b_qk(0)
                for idx in range(len(steps)):
                    b_exp(idx)
                    if idx + 1 < len(steps):
                        b_qk(idx + 1)
                    b_pv(idx)
                    while pend and pend[0][0] <= idx:
                        b_fin(pend.pop(0))
                while pend:
                    b_fin(pend.pop(0))
            s.barrier()

    def phase_o(i, last):
        with contextlib.ExitStack() as st:
            al = lambda n, sh, dt: st.enter_context(nc.sbuf_tensor("s%d_%s" % (i, n), sh, dt))
            wst = al("wsto", [128, 2, 1024], F32)
            R_wst = Res("wsto")
            wo = al("wo", [128, 8, 1024], BF16)
            wg = al("wg", [128, 8, 1024], BF16)
            wp = al("wp", [128, 2, 1024], BF16)
            R_w = Res("w_o")
            xt = [al("xtO%d" % k, [128, 8, TT], F32) for k in range(2)]
            R_xt = [[Res("xtO%d_%d" % (k, c)) for c in range(8)] for k in range(2)]
            ot = [al("otO%d" % k, [128, 8, TT], BF16) for k in range(2)]
            R_ot = [Res("otO%d" % k) for k in range(2)]
            pin = [al("pin%d" % k, [128, 4, 256], F32) for k in range(2)]
            R_pin = [Res("pin%d" % k) for k in range(2)]
            pT = al("pT", [128, 2, TT], BF16)
            R_pT = Res("pT")
            sq = al("sqO", [128, 8, TT], BF16)
            R_sq = Res("sqO")
            sd = al("sdO", [128, TT], F32)
            R_sd = Res("sdO")
            n1 = sq
            R_n1 = R_sq
            gs = [al("gs%d" % k, [128, TT], F32) for k in range(2)]
            R_gs = [Res("gs%d" % k) for k in range(2)]
            if last:
                yo = al("yo", [128, 4, D], F32)
                R_yo = Res("yo")
            for (wdst, wsrc, nch) in ((wo, wo_in[i], 8), (wg, wg_in[i], 8), (wp, wp_in[i], 2)):
                wv = wsrc.rearrange("p (c f) -> p c f", c=nch)
                for c0 in range(0, nch, 2):
                    s.dma("sp", wst[:, :, :], wv[:, c0:c0 + 2, :], writes=[R_wst])
                    s.op("pool", lambda e, wdst=wdst, c0=c0: e.tensor_copy(out=wdst[:, c0:c0 + 2, :], in_=wst[:, :, :]),
                         reads=[R_wst], writes=[R_w])
            pv_ = p_in[i].rearrange("(t b p) f -> t p b f", b=4, p=128)
            ov = out.rearrange("(t b p) d -> t p b d", b=4, p=128)

            def loads(tt):
                k = tt % 2
                tsl = slice(tt * TT, (tt + 1) * TT)
                s.dma("sp", xt[k][:, :, :], x_scr[:, :, tsl], reads=[R_xscr[tt]], writes=R_xt[k], sem_res=R_xt[k][0])
                s.dma("sp", ot[k][:, :, :], o_scr[:, :, tsl], reads=R_oscr, writes=[R_ot[k]])
                s.dma("sp", pin[k][:, :, :], pv_[tt], writes=[R_pin[k]])

            loads(0)
            for tt in range(NT):
                k = tt % 2
                tsl = slice(tt * TT, (tt + 1) * TT)
                if tt + 1 < NT:
                    loads(tt + 1)
                X = xt[k]
                for fc in range(8):
                    b = next_bank()

                    def mm(e, b=b, fc=fc, k=k):
                        for c in range(8):
                            ins = e.matmul(bank(b), wo[:, c, fc * 128:(fc + 1) * 128], ot[k][:, c, :], start=(c == 0), stop=(c == 7))
                        return ins
                    s.op("pe", mm, reads=[R_w, R_ot[k]], writes=[R_b[b]])
                    s.op("dve", lambda e, b=b, fc=fc, X=X: e.tensor_tensor(out=X[:, fc, :], in0=X[:, fc, :], in1=bank(b), op=ALU.add),
                         reads=[R_b[b], R_xt[k][fc]], writes=[R_xt[k][fc]])
                emit_rstd(X, R_xt[k], sq, R_sq, sd, R_sd, 1.0 / D)
                for c in range(8):
                    eng = "dve"
                    s.op(eng, lambda e, c=c, X=X: e.scalar_tensor_tensor(
                        out=n1[:, c, :], in0=X[:, c, :], scalar=gv[:, g_ple(i) + c:g_ple(i) + c + 1], in1=sd[:, :],
                        op0=ALU.mult, op1=ALU.mult), reads=[R_xt[k][c], R_sd, R_c], writes=[R_n1])
                for c2 in range(2):
                    b = next_bank()

                    def tr(e, b=b, c2=c2, k=k):
                        for tb in range(4):
                            ins = e.transpose(out=bank(b, tb * 128, (tb + 1) * 128), in_=pin[k][:, tb, c2 * 128:(c2 + 1) * 128],
                                              identity=ident[:, :])
                        return ins
                    s.op("pe", tr, reads=[R_pin[k], R_c], writes=[R_b[b]])
                    s.op("act", lambda e, b=b, c2=c2: e.activation(out=pT[:, c2, :], in_=bank(b), func=AF.Copy),
                         reads=[R_b[b]], writes=[R_pT])
                for fc in range(8):
                    b = next_bank()

                    def mmg(e, b=b, fc=fc):
                        for c in range(8):
                            ins = e.matmul(bank(b), wg[:, c, fc * 128:(fc + 1) * 128], n1[:, c, :], start=(c == 0), stop=(c == 7))
                        return ins
                    s.op("pe", mmg, reads=[R_w, R_n1], writes=[R_b[b]])
                    gk = fc % 2
                    s.op("act", lambda e, b=b, gk=gk: e.activation(out=gs[gk][:, :], in_=bank(b), func=AF.Sigmoid),
                         reads=[R_b[b]], writes=[R_gs[gk]])
                    b2 = next_bank()

                    def mmp(e, b2=b2, fc=fc):
                        for c in range(2):
                            ins = e.matmul(bank(b2), wp[:, c, fc * 128:(fc + 1) * 128], pT[:, c, :], start=(c == 0), stop=(c == 1))
                        return ins
                    s.op("pe", mmp, reads=[R_w, R_pT], writes=[R_b[b2]])
                    s.op("dve", lambda e, b2=b2, gk=gk: e.tensor_tensor(out=gs[gk][:, :], in0=gs[gk][:, :], in1=bank(b2), op=ALU.mult),
                         reads=[R_gs[gk], R_b[b2]], writes=[R_gs[gk]])
                    s.op("pool", lambda e, fc=fc, X=X, gk=gk: e.tensor_tensor(out=X[:, fc, :], in0=X[:, fc, :], in1=gs[gk][:, :], op=ALU.add),
                         reads=[R_gs[gk], R_xt[k][fc]], writes=[R_xt[k][fc]])
                if dbg == ("x", i):
                    s.dma("sp", dbg_out[:, :, tsl], X[:, :, :], reads=R_xt[k], writes=[R_out[tt]])
                if not last:
                    s.dma("sp", x_scr[:, :, tsl], X[:, :, :], reads=R_xt[k], writes=[R_xscr[tt]])
                    emit_h(X, R_xt[k], sq, R_sq, sd, R_sd, tt, g_norm(i + 1))
                else:
                    emit_rstd(X, R_xt[k], sq, R_sq, sd, R_sd, 1.0 / D)
                    for c in range(8):
                        eng = "dve"
                        s.op(eng, lambda e, c=c, X=X: e.scalar_tensor_tensor(
                            out=X[:, c, :], in0=X[:, c, :], scalar=gv[:, G_FINAL + c:G_FINAL + c + 1], in1=sd[:, :],
                            op0=ALU.mult, op1=ALU.mult), reads=[R_xt[k][c], R_sd, R_c], writes=[R_xt[k][c]])
                    xn = X
                    R_xn = R_xt[k]
                    for tb in range(4):
                        for cq in range(2):
                            b = next_bank()

                            def tr(e, b=b, tb=tb, cq=cq, xn=xn):
                                for cc in range(4):
                                    c = cq * 4 + cc
                                    ins = e.transpose(out=bank(b, cc * 128, (cc + 1) * 128), in_=xn[:, c, tb * 128:(tb + 1) * 128],
                                                      identity=ident[:, :])
                                return ins
                            s.op("pe", tr, reads=R_xn + [R_c], writes=[R_b[b]])
                            if cq == 0:
                                s.op("dve", lambda e, b=b, tb=tb: e.tensor_copy(out=yo[:, tb, 0:512], in_=bank(b)),
                                     reads=[R_b[b]], writes=[R_yo])
                            else:
                                s.op("act", lambda e, b=b, tb=tb: e.activation(out=yo[:, tb, 512:1024], in_=bank(b), func=AF.Copy),
                                     reads=[R_b[b]], writes=[R_yo])
                    s.dma("sp", ov[tt], yo[:, :, :], reads=[R_yo], writes=[R_out[tt]])
            s.barrier()

    for i in range(n_layers):
        if dbg == ("h", i):
            s.dma("sp", dbg_out[:, :, :], hT[:, :, :], reads=[r for t in R_h for r in t], writes=[R_out[0]])
            break
        if i % 2 == 0:
            phase_ga(i)
        else:
            phase_gb(i)
        if dbg == ("o", i):
            s.dma("sp", dbg_out[:, :, :], o_scr[:, :, :], reads=R_oscr, writes=[R_out[0]])
            break
        phase_o(i, last=(i == n_layers - 1))
    s.barrier()
    s.emit()
    return nc


_CACHE = {}
_PADDED = ("cosT", "sinT", "a_in", "b_in", "wo", "wg", "wp", "tb")


def _prep_shared(inp):
    f = lambda a: np.ascontiguousarray(np.asarray(a, dtype=np.float32))
    sh = {}
    cosT, sinT, perm = _rope_tables()
    sh["cosT"], sh["sinT"], sh["perm"] = cosT, sinT, perm
    sh["ident"] = np.eye(128, dtype=np.float32)
    gv = np.zeros((128, 80), np.float32)
    ng, pg = f(inp["norm_g"]), f(inp["ple_norm_g"])
    for i in range(DEPTH):
        gv[:, 16 * i:16 * i + 8] = _vec_pc(ng[i])
        gv[:, 16 * i + 8:16 * i + 16] = _vec_pc(pg[i])
    gv[:, 64:72] = _vec_pc(f(inp["final_norm_g"]))
    bq, bk = f(inp["b_q_norm"]), f(inp["b_k_norm"])
    for j in range(2):
        gv[:, 72 + j] = np.tile(bq[j], 2)
        gv[:, 74 + j] = np.tile(bk[j], 2)
    sh["gv"] = gv
    awi, bwi = f(inp["a_w_in"]), f(inp["b_w_in"])
    for j in range(2):
        W = awi[j]
        groups = []
        for hp in range(8):
            cols = np.concatenate([W[:, k * 1024 + hp * 128:k * 1024 + (hp + 1) * 128] for k in range(4)], axis=1)
            groups.append(_pc(cols))
        sh["a_in%d" % j] = np.ascontiguousarray(np.stack(groups))
        W = bwi[j]
        groups = []
        for g in range(4):
            cols = np.concatenate([W[:, 256 * g:256 * (g + 1)], W[:, 1024 + 64 * g:1024 + 64 * (g + 1)],
                                   W[:, 1280 + 64 * g:1280 + 64 * (g + 1)], W[:, 1536 + 256 * g:1536 + 256 * (g + 1)]], axis=1)
            groups.append(_pc(cols))
        sh["b_in%d" % j] = np.ascontiguousarray(np.stack(groups))
        sh["tb%d" % j] = _a_tables(f(inp["a_rpb"])[j])
    awo, bwo = f(inp["a_w_out"]), f(inp["b_w_out"])
    wgt, wpj = f(inp["ple_w_gate"]), f(inp["ple_w_proj"])
    for i in range(DEPTH):
        sh["wo%d" % i] = _pc(awo[i // 2] if i % 2 == 0 else bwo[i // 2])
        sh["wg%d" % i] = _pc(wgt[i])
        sh["wp%d" % i] = _pc(wpj[i])
    return sh


def _core_inputs(sh, xb, pb, b):
    m = {}
    for k_, v_ in sh.items():
        if k_ in _PADDED or k_[:-1] in _PADDED:
            pad = np.full((1,) + v_.shape[1:], float(b), np.float32)
            m[k_] = np.concatenate([v_, pad], axis=0)
        else:
            m[k_] = v_
    m["x"] = np.ascontiguousarray(xb)
    m["p"] = np.ascontiguousarray(pb)
    return m


def kernel(x, p, norm_g, a_w_in, a_rpb, a_w_out, b_w_in, b_q_norm, b_k_norm, b_w_out,
           ple_norm_g, ple_w_gate, ple_w_proj, final_norm_g):
    inp = dict(norm_g=norm_g, a_w_in=a_w_in, a_rpb=a_rpb, a_w_out=a_w_out, b_w_in=b_w_in, b_q_norm=b_q_norm,
               b_k_norm=b_k_norm, b_w_out=b_w_out, ple_norm_g=ple_norm_g, ple_w_gate=ple_w_gate,
               ple_w_proj=ple_w_proj, final_norm_g=final_norm_g)
    sh = _prep_shared(inp)
    x = np.asarray(x, dtype=np.float32)
    p = np.asarray(p, dtype=np.float32)
    if "nc" not in _CACHE:
        _CACHE["nc"] = build()
    nc = _CACHE["nc"]
    in_maps = [_core_inputs(sh, x[b], p[:, b], b) for b in range(8)]
    res = run_bass_kernel_spmd(nc, in_maps, core_ids=list(range(8)))
    return np.stack([np.asarray(r["out"], dtype=np.float32) for r in res.results], axis=0)
```

```python
import contextlib
import numpy as np
import ml_dtypes
import concourse.bass as bass
import concourse.mybir as mybir
from concourse.bass_utils import run_bass_kernel_spmd

F32 = mybir.dt.float32
BF16 = mybir.dt.bfloat16
AF = mybir.ActivationFunctionType
ALU = mybir.AluOpType

S = 4096
D = 1024
NT = 8
TT = 512
DEPTH = 4
EPS = 1e-6
NE = 22
ENGS = ("pe", "act", "dve", "pool", "sp")


class Res:
    __slots__ = ("name", "w", "r", "sem", "semval")

    def __init__(self, name):
        self.name = name
        self.w = None
        self.r = []
        self.sem = None
        self.semval = 0


class Sched:
    def __init__(self, nc):
        self.nc = nc
        self.ops = {e: [] for e in ENGS}
        self.known = {e: {} for e in ENGS}
        self.dma_sems = []

    def _deps(self, eng, reads, writes):
        toks = []
        for r in reads:
            toks.append(r.w)
        for w in writes:
            toks.append(w.w)
            toks.extend(w.r)
        best = {}
        cur = len(self.ops[eng])
        for tok in toks:
            if tok is None:
                continue
            if tok[0] == "e":
                _, f, idx = tok
                if f == eng:
                    if eng == "pe":
                        continue
                    if eng in ("act", "dve") and idx < cur - 1:
                        continue
                key, val = ("e", f), idx
            else:
                key, val = ("s", tok[1]), tok[2]
            if self.known[eng].get(key, -1) >= val:
                continue
            if key not in best or best[key][0] < val:
                best[key] = (val, tok)
        waits = []
        for key, (val, tok) in best.items():
            self.known[eng][key] = val
            waits.append(tok)
            if tok[0] == "e":
                self.ops[tok[1]][tok[2]][2] = True
        return waits

    def _commit(self, tok, reads, writes):
        for r in reads:
            r.r.append(tok)
        for w in writes:
            w.w = tok
            w.r = []

    def op(self, eng, fn, reads=(), writes=()):
        waits = self._deps(eng, reads, writes)
        tok = ("e", eng, len(self.ops[eng]))
        self.ops[eng].append([fn, waits, False, None])
        self._commit(tok, reads, writes)
        return tok

    def dma(self, eng, out, in_, reads=(), writes=(), sem_res=None):
        waits = self._deps(eng, reads, writes)
        if sem_res is None:
            sem_res = writes[0]
        if sem_res.sem is None:
            sem_res.sem = len(self.dma_sems)
            self.dma_sems.append(sem_res)
        sem_res.semval += 16
        tok = ("s", sem_res.sem, sem_res.semval)

        def fn(e, out=out, in_=in_):
            return e.dma_start(out=out, in_=in_)

        self.ops[eng].append([fn, waits, False, sem_res.sem])
        self._commit(tok, reads, writes)
        return tok

    def barrier(self):
        last = {}
        for f in ENGS:
            for i in range(len(self.ops[f]) - 1, -1, -1):
                o = self.ops[f][i]
                if o[0] is not None and o[3] is None:
                    last[f] = ("e", f, i)
                    break
        for e in ENGS:
            waits = []
            cands = [last[f] for f in ENGS if f != e and f in last] + [("s", r.sem, r.semval) for r in self.dma_sems]
            for tok in cands:
                key = ("e", tok[1]) if tok[0] == "e" else ("s", tok[1])
                val = tok[2]
                if self.known[e].get(key, -1) >= val:
                    continue
                self.known[e][key] = val
                waits.append(tok)
                if tok[0] == "e":
                    self.ops[tok[1]][tok[2]][2] = True
            self.ops[e].append([None, waits, False, None])

    def emit(self):
        nc = self.nc
        val = {}
        for e in ENGS:
            c = 0
            for i, o in enumerate(self.ops[e]):
                if o[2]:
                    c += 1
                    val[(e, i)] = c
        with contextlib.ExitStack() as st:
            esem = {e: st.enter_context(nc.semaphore("prog_" + e)) for e in ENGS}
            dsem = [st.enter_context(nc.semaphore("dma_%d" % i)) for i in range(len(self.dma_sems))]
            block = st.enter_context(nc.Block())
            ops = self.ops

            def body(e, h):
                for fn, waits, needed, dinc in ops[e]:
                    for t in waits:
                        if t[0] == "e":
                            h.wait_ge(esem[t[1]], val[(t[1], t[2])])
                        else:
                            h.wait_ge(dsem[t[1]], t[2])
                    if fn is None:
                        continue
                    ins = fn(h)
                    if dinc is not None:
                        ins.then_inc(dsem[dinc], 16)
                    elif needed:
                        ins.then_inc(esem[e], 1)

            @block.tensor
            def _(h):
                body("pe", h)

            @block.scalar
            def _(h):
                body("act", h)

            @block.vector
            def _(h):
                body("dve", h)

            @block.gpsimd
            def _(h):
                body("pool", h)

            @block.sync
            def _(h):
                body("sp", h)


def _rope_tables():
    t = np.arange(S)
    row = (t // 64).astype(np.float32)
    col = (t % 64).astype(np.float32)
    inv = np.power(np.float32(10000.0), -np.arange(16, dtype=np.float32) * np.float32(2.0) / np.float32(32.0)).astype(np.float32)
    cosT = np.zeros((128, S), np.float32)
    sinT = np.zeros((128, S), np.float32)
    perm = np.zeros((128, 128), np.float32)
    for p in range(128):
        d = p % 64
        sec = d // 32
        dd = d % 32
        f = dd % 16
        pos = row if sec == 0 else col
        ang = (pos * inv[f]).astype(np.float32)
        cosT[p] = np.cos(ang)
        sn = np.sin(ang)
        if dd < 16:
            sinT[p] = -sn
            src = p + 16
        else:
            sinT[p] = sn
            src = p - 16
        perm[src, p] = 1.0
    return cosT, sinT, perm


def _a_tables(rpb):
    NEG = np.float32(-30000.0)
    kc = np.arange(64)[:, None]
    c = np.arange(64)[None, :]
    cs = np.clip(c - 8, 0, 48)
    colvalid = (kc >= cs) & (kc < cs + 16)
    dc = np.clip(kc - c + 15, 0, 30)
    out = np.full((16, 2, 64, 2, NE, 64), NEG, np.float32)
    for a in range(2):
        for e in range(NE):
            dr = 17 + a - e
            if 0 <= dr <= 14:
                blk = np.where(colvalid[None], rpb[:, dr][:, dc], NEG)
                out[:, a, :, 1, e, :] = blk
                if 3 <= dr <= 10:
                    out[:, a, :, 0, e, :] = blk
    return np.ascontiguousarray(out.reshape(16, 128, 2 * NE * 64))


def _pc(w):
    K = w.shape[0] // 128
    return np.ascontiguousarray(w.reshape(K, 128, w.shape[1]).transpose(1, 0, 2).reshape(128, K * w.shape[1]))


def _vec_pc(g):
    return np.ascontiguousarray(g.reshape(-1, 128).T)


def a_tiles():
    res = []
    for m in range(8):
        if m == 0:
            js = list(range(0, 6))
        elif m == 7:
            js = list(range(26, 32))
        else:
            js = list(range(4 * m - 2, 4 * m + 6))
        res.append((m, js))
    return res


def a_segments(m, j):
    if m == 0 and j <= 3:
        return [(0, 5, 1), (5, 8, 0)]
    if m == 7 and j >= 28:
        return [(0, 5, 0), (5, 8, 1)]
    return [(0, 8, 0)]


def build(n_layers=DEPTH, dbg=None):
    nc = bass.Bass("TRN2", target_bir_lowering=False)
    s = Sched(nc)

    def din(name, shape, dt=F32, pad=False):
        if pad:
            t = nc.dram_tensor(name, [shape[0] + 1] + list(shape[1:]), dt, kind="ExternalInput").ap()
            return t[0:shape[0]]
        return nc.dram_tensor(name, list(shape), dt, kind="ExternalInput").ap()

    x_in = din("x", [S, D])
    p_in = din("p", [DEPTH, S, 256])
    gv_in = din("gv", [128, 80])
    ident_in = din("ident", [128, 128])
    perm_in = din("perm", [128, 128])
    cos_in = din("cosT", [128, S], pad=True)
    sin_in = din("sinT", [128, S], pad=True)
    a_in = [din("a_in%d" % j, [8, 128, 8 * 512], pad=True) for j in range(2)]
    b_in = [din("b_in%d" % j, [4, 128, 8 * 640], pad=True) for j in range(2)]
    wo_in = [din("wo%d" % i, [128, 8 * 1024], pad=True) for i in range(DEPTH)]
    wg_in = [din("wg%d" % i, [128, 8 * 1024], pad=True) for i in range(DEPTH)]
    wp_in = [din("wp%d" % i, [128, 2 * 1024], pad=True) for i in range(DEPTH)]
    tb_in = [din("tb%d" % j, [16, 128, 2 * NE * 64], pad=True) for j in range(2)]
    out = nc.dram_tensor("out", [S, D], F32, kind="ExternalOutput").ap()
    x_scr = nc.dram_tensor("x_scr", [128, 8, S], F32, kind="Internal").ap()
    o_scr = nc.dram_tensor("o_scr", [128, 8, S], BF16, kind="Internal").ap()
    R_xscr = [Res("xscr%d" % t) for t in range(NT)]
    R_oscr = [Res("oscr%d" % c) for c in range(8)]
    R_out = [Res("out%d" % t) for t in range(NT)]
    dbg_out = None
    if dbg:
        dbg_out = nc.dram_tensor("dbg", [128, 8, S], F32 if dbg[0] == "x" else BF16, kind="ExternalOutput").ap()

    hT = nc.alloc_sbuf_tensor("sb_hT", [128, 8, S], BF16)
    R_h = [[Res("h%d_%d" % (t, c)) for c in range(8)] for t in range(NT)]
    ident = nc.alloc_sbuf_tensor("sb_ident", [128, 128], F32)
    perm = nc.alloc_sbuf_tensor("sb_perm", [128, 128], F32)
    gv = nc.alloc_sbuf_tensor("sb_gv", [128, 80], F32)
    ones_bf = nc.alloc_sbuf_tensor("sb_ones_bf", [128, 128], BF16)
    ones_f = nc.alloc_sbuf_tensor("sb_ones_f", [128, 128], F32)
    bones = nc.alloc_sbuf_tensor("sb_bones", [128, 128], BF16)
    epsc = nc.alloc_sbuf_tensor("sb_epsc", [128, 1], F32)
    R_c = Res("consts")
    ps = nc.alloc_psum_tensor("ps", [128, 8 * 512], F32)
    R_b = [Res("bank%d" % b) for b in range(8)]

    def bank(b, lo=0, hi=512):
        return ps[:, b * 512 + lo:b * 512 + hi]

    s.dma("sp", ident[:, :], ident_in[:, :], writes=[R_c])
    s.dma("sp", perm[:, :], perm_in[:, :], writes=[R_c])
    s.dma("sp", gv[:, :], gv_in[:, :], writes=[R_c])
    s.op("pool", lambda e: e.memset(ones_bf[:, :], 1.0), writes=[R_c])
    s.op("pool", lambda e: e.memset(ones_f[:, :], 1.0), writes=[R_c])
    s.op("pool", lambda e: e.memset(bones[:, :], 0.0), writes=[R_c])
    s.op("pool", lambda e: e.memset(bones[0:64, 0:64], 1.0), writes=[R_c])
    s.op("pool", lambda e: e.memset(bones[64:128, 64:128], 1.0), writes=[R_c])
    s.op("pool", lambda e: e.memset(epsc[:, :], EPS), writes=[R_c])
    s.barrier()

    def g_norm(i):
        return 16 * i

    def g_ple(i):
        return 16 * i + 8

    G_FINAL = 64
    G_BQ = 72
    G_BK = 74

    bank_rr = [0]

    def next_bank():
        b = bank_rr[0]
        bank_rr[0] = (b + 1) % 8
        return b

    def emit_rstd(xt, R_xt, sq, R_sq, sd, R_sd, width_scale):
        s.op("act", lambda e: e.activation(out=sq[:, :, :], in_=xt[:, :, :], func=AF.Square),
             reads=R_xt, writes=[R_sq])
        b = next_bank()

        def mm(e):
            for c in range(8):
                ins = e.matmul(bank(b), ones_bf[:, :], sq[:, c, :], start=(c == 0), stop=(c == 7))
            return ins
        s.op("pe", mm, reads=[R_sq, R_c], writes=[R_b[b]])
        s.op("act", lambda e: e.activation(out=sd[:, :], in_=bank(b), func=AF.Ln, scale=width_scale, bias=epsc[:, 0:1]),
             reads=[R_b[b], R_c], writes=[R_sd])
        s.op("act", lambda e: e.activation(out=sd[:, :], in_=sd[:, :], func=AF.Exp, scale=-0.5), reads=[R_sd], writes=[R_sd])

    def emit_h(xt, R_xt, sq, R_sq, sd, R_sd, tt, gcol):
        emit_rstd(xt, R_xt, sq, R_sq, sd, R_sd, 1.0 / D)
        for c in range(8):
            eng = "dve"
            s.op(eng, lambda e, c=c: e.scalar_tensor_tensor(
                out=hT[:, c, tt * TT:(tt + 1) * TT], in0=xt[:, c, :], scalar=gv[:, gcol + c:gcol + c + 1],
                in1=sd[:, :], op0=ALU.mult, op1=ALU.mult),
                reads=[R_xt[c], R_sd, R_c], writes=[R_h[tt][c]])

    with contextlib.ExitStack() as st:
        xin = [st.enter_context(nc.sbuf_tensor("sT_xin%d" % i, [128, 4, D], F32)) for i in range(2)]
        R_xin = [Res("xin%d" % i) for i in range(2)]
        xt = [st.enter_context(nc.sbuf_tensor("xtT%d" % i, [128, 8, TT], F32)) for i in range(2)]
        R_xt = [[Res("xtT%d_%d" % (i, c)) for c in range(8)] for i in range(2)]
        sq = st.enter_context(nc.sbuf_tensor("sqT", [128, 8, TT], BF16))
        R_sq = Res("sqT")
        sd = st.enter_context(nc.sbuf_tensor("sdT", [128, TT], F32))
        R_sd = Res("sdT")
        xv = x_in.rearrange("(t b p) d -> t p b d", b=4, p=128)
        for tt in range(NT):
            k = tt % 2
            s.dma("sp", xin[k][:, :, :], xv[tt], writes=[R_xin[k]])
            for c in range(8):
                b = next_bank()

                def tr(e, c=c, b=b, k=k):
                    for tb in range(4):
                        ins = e.transpose(out=bank(b, tb * 128, (tb + 1) * 128), in_=xin[k][:, tb, c * 128:(c + 1) * 128],
                                          identity=ident[:, :])
                    return ins
                s.op("pe", tr, reads=[R_xin[k], R_c], writes=[R_b[b]])
                if c % 2 == 0:
                    s.op("dve", lambda e, c=c, b=b, k=k: e.tensor_copy(out=xt[k][:, c, :], in_=bank(b)),
                         reads=[R_b[b]], writes=[R_xt[k][c]])
                else:
                    s.op("act", lambda e, c=c, b=b, k=k: e.activation(out=xt[k][:, c, :], in_=bank(b), func=AF.Copy),
                         reads=[R_b[b]], writes=[R_xt[k][c]])
            s.dma("sp", x_scr[:, :, tt * TT:(tt + 1) * TT], xt[k][:, :, :], reads=R_xt[k], writes=[R_xscr[tt]])
            emit_h(xt[k], R_xt[k], sq, R_sq, sd, R_sd, tt, g_norm(0))
        s.barrier()

    def finalize_a(a, ob, rrow, R_rrow):
        dp = 64 if a == 0 else 0
        rl, rb = rrow
        s.op("act", lambda e: e.activation(out=rl[dp:dp + 1, :], in_=bank(ob)[dp:dp + 1, :], func=AF.Ln),
             reads=[R_b[ob]], writes=[R_rrow])
        s.op("act", lambda e: e.activation(out=rb[dp:dp + 1, :], in_=rl[dp:dp + 1, :], func=AF.Exp, scale=-1.0),
             reads=[R_rrow], writes=[R_rrow])

    def finalize_b(a, ob, obc, sg, R_sg_t, qt, rrow, R_rrow, tmp, R_tmp, ochunk, R_ochunk):
        dp = 64 if a == 0 else 0
        lo = 64 * a
        s.op("pe", lambda e: e.matmul(bank(obc), ones_bf[dp:dp + 1, :], rrow[1][dp:dp + 1, :], start=True, stop=True),
             reads=[R_rrow, R_c], writes=[R_b[obc]])
        s.op("dve", lambda e: e.tensor_tensor(out=tmp[lo:lo + 64, :], in0=bank(ob)[lo:lo + 64, :],
                                              in1=sg[lo:lo + 64, qt * TT:(qt + 1) * TT], op=ALU.mult),
             reads=[R_b[ob], R_sg_t], writes=[R_tmp])
        s.op("dve", lambda e: e.tensor_tensor(out=ochunk[lo:lo + 64, qt * TT:(qt + 1) * TT], in0=tmp[lo:lo + 64, :],
                                              in1=bank(obc)[lo:lo + 64, :], op=ALU.mult),
             reads=[R_tmp, R_b[obc]], writes=[R_ochunk])

    def phase_ga(i):
        j_l = i // 2
        with contextlib.ExitStack() as st:
            al = lambda n, sh, dt: st.enter_context(nc.sbuf_tensor("s%d_%s" % (i, n), sh, dt))
            wst = al("wst", [128, 8, 512], F32)
            R_wst = Res("wst")
            wbf = [al("wbf%d" % k, [128, 8, 512], BF16) for k in range(2)]
            R_wbf = [Res("wbf%d" % k) for k in range(2)]
            qT = al("qT", [128, S], BF16)
            kz = [al("kz%d" % a, [128, S], BF16) for a in range(2)]
            R_kz = Res("kzinit")
            s.op("dve", lambda e: e.memset(kz[0][64:128, :], 0.0), writes=[R_kz])
            s.op("dve", lambda e: e.memset(kz[1][0:64, :], 0.0), writes=[R_kz])
            sg = al("sg", [128, S], BF16)
            R_q = [Res("q%d" % t) for t in range(NT)]
            R_k = [Res("k%d" % t) for t in range(NT)]
            R_sg = [Res("sg%d" % t) for t in range(NT)]
            vaug = [al("vaug%d" % a, [128, 32, 128], BF16) for a in range(2)]
            R_v = [Res("v%d" % t) for t in range(NT)]
            tst = al("tst", [128, 2 * NE * 64], F32)
            R_tst = Res("tst")
            tbf = [al("tbf%d" % k, [128, 2, NE, 64], BF16) for k in range(2)]
            R_tbf = [Res("tbf%d" % k) for k in range(2)]
            exs = [al("exs%d" % k, [128, 1024], BF16) for k in range(2)]
            R_exs = [Res("exs%d" % k) for k in range(2)]
            pt = [al("pt%d" % k, [128, 1024], BF16) for k in range(2)]
            R_pt = [[Res("pt%d_%d" % (k, z)) for z in range(2)] for k in range(2)]
            rrow = [(al("rl%d" % k, [128, TT], F32), al("rb%d" % k, [128, TT], BF16)) for k in range(2)]
            R_rrow = [Res("rrow%d" % k) for k in range(2)]
            tmp = al("tmp", [128, TT], F32)
            R_tmp = Res("tmp")
            och = [al("och%d" % k, [128, S], BF16) for k in range(2)]
            R_och = [Res("och%d" % k) for k in range(2)]
            R_vinit = Res("vinit")
            s.op("dve", lambda e: e.memset(vaug[0][:, :, 64:128], 1.0), writes=[R_vinit])
            s.op("dve", lambda e: e.memset(vaug[1][:, :, 0:64], 1.0), writes=[R_vinit])

            def load_w(hp):
                k = hp % 2
                s.dma("sp", wst[:, :, :], a_in[j_l][hp].rearrange("p (c f) -> p c f", c=8), writes=[R_wst])
                s.op("pool", lambda e: e.tensor_copy(out=wbf[k][:, :, :], in_=wst[:, :, :]),
                     reads=[R_wst], writes=[R_wbf[k]])

            tcount = [0]

            def load_tab(h):
                k = tcount[0] % 2
                tcount[0] += 1
                s.dma("sp", tst[:, :], tb_in[j_l][h], writes=[R_tst])
                s.op("act", lambda e: e.activation(out=tbf[k][:, :, :, :].rearrange("p a e c -> p (a e c)"), in_=tst[:, :], func=AF.Exp),
                     reads=[R_tst], writes=[R_tbf[k]])
                return k

            load_w(0)
            for hp in range(8):
                wk = hp % 2
                w = wbf[wk]
                for tt in range(NT):
                    tsl = slice(tt * TT, (tt + 1) * TT)
                    for kind in range(3):
                        col = (0, 128, 384)[kind]
                        b = next_bank()

                        def mm(e, b=b, col=col, tsl=tsl, w=w):
                            for c in range(8):
                                ins = e.matmul(bank(b), w[:, c, col:col + 128], hT[:, c, tsl], start=(c == 0), stop=(c == 7))
                            return ins
                        s.op("pe", mm, reads=[R_wbf[wk]] + R_h[tt], writes=[R_b[b]])
                        if kind == 0:
                            s.op("dve", lambda e, b=b, tsl=tsl: e.tensor_copy(out=qT[:, tsl], in_=bank(b)),
                                 reads=[R_b[b]], writes=[R_q[tt]])
                        elif kind == 1:
                            s.op("dve", lambda e, b=b, tsl=tsl: e.tensor_copy(out=kz[0][0:64, tsl], in_=bank(b)[0:64, :]),
                                 reads=[R_b[b], R_kz], writes=[R_k[tt]])
                            s.op("dve", lambda e, b=b, tsl=tsl: e.tensor_copy(out=kz[1][64:128, tsl], in_=bank(b)[64:128, :]),
                                 reads=[R_b[b], R_kz], writes=[R_k[tt]])
                        else:
                            s.op("act", lambda e, b=b, tsl=tsl: e.activation(out=sg[:, tsl], in_=bank(b), func=AF.Silu),
                                 reads=[R_b[b]], writes=[R_sg[tt]])
                    b = next_bank()

                    def mmv(e, b=b, tt=tt, w=w):
                        for tcl in range(4):
                            t0 = tt * TT + tcl * 128
                            for c in range(8):
                                ins = e.matmul(bank(b, tcl * 128, (tcl + 1) * 128), hT[:, c, t0:t0 + 128], w[:, c, 256:384],
                                               start=(c == 0), stop=(c == 7))
                        return ins
                    s.op("pe", mmv, reads=[R_wbf[wk]] + R_h[tt], writes=[R_b[b]])
                    bv = bank(b).rearrange("p (t f) -> p t f", t=4)
                    s.op("dve", lambda e, bv=bv, tt=tt: e.tensor_copy(out=vaug[0][:, tt * 4:tt * 4 + 4, 0:64], in_=bv[:, :, 0:64]),
                         reads=[R_b[b], R_vinit], writes=[R_v[tt]])
                    s.op("dve", lambda e, bv=bv, tt=tt: e.tensor_copy(out=vaug[1][:, tt * 4:tt * 4 + 4, 64:128], in_=bv[:, :, 64:128]),
                         reads=[R_b[b], R_vinit], writes=[R_v[tt]])
                if hp + 1 < 8:
                    load_w(hp + 1)
                ok = hp % 2
                steps = []
                for a in range(2):
                    tk = load_tab(2 * hp + a)
                    for (m, js) in a_tiles():
                        prs = [js[z:z + 2] for z in range(0, len(js), 2)]
                        for pi_, pr in enumerate(prs):
                            steps.append((a, tk, m, pr, pi_ == 0, pi_ == len(prs) - 1))
                gq = [0]

                def a_qk(idx):
                    a, tk, m, pr, first, last = steps[idx]
                    sb0 = 2 * (idx % 2)
                    lo = 64 * a

                    def qk(e, pr=pr, sb0=sb0, m=m, lo=lo):
                        for z, j in enumerate(pr):
                            ins = e.matmul(bank(sb0 + z), kz[lo // 64][:, j * 128:(j + 1) * 128],
                                           qT[:, m * TT:(m + 1) * TT], start=True, stop=True)
                        return ins
                    s.op("pe", qk, reads=[R_k[jj // 4] for jj in pr] + [R_q[m]], writes=[R_b[sb0], R_b[sb0 + 1]])

                def a_exp(idx):
                    a, tk, m, pr, first, last = steps[idx]
                    sb0 = 2 * (idx % 2)
                    ek = idx % 2
                    s.op("act", lambda e, sb0=sb0, ek=ek: e.activation(
                        out=exs[ek][:, :], in_=ps[:, sb0 * 512:(sb0 + 2) * 512], func=AF.Exp, scale=0.125),
                        reads=[R_b[sb0], R_b[sb0 + 1]], writes=[R_exs[ek]])

                def a_mask(idx):
                    a, tk, m, pr, first, last = steps[idx]
                    ek = idx % 2
                    for z, j in enumerate(pr):
                        e0 = 10 - (2 * j - 8 * m)
                        for (b0, b1, tab) in a_segments(m, j):
                            s.op("dve", lambda e, z=z, e0=e0, b0=b0, b1=b1, tab=tab, ek=ek, tk=tk: e.tensor_tensor(
                                out=pt[ek][:, z * 512 + b0 * 64:z * 512 + b1 * 64],
                                in0=exs[ek][:, z * 512 + b0 * 64:z * 512 + b1 * 64],
                                in1=tbf[tk][:, tab, e0 + b0:e0 + b1, :].rearrange("p e c -> p (e c)"), op=ALU.mult),
                                reads=[R_exs[ek], R_tbf[tk]], writes=[R_pt[ek][z]])

                pend = []

                def a_pv(idx):
                    a, tk, m, pr, first, last = steps[idx]
                    ek = idx % 2
                    g_ = gq[0]
                    ob = 4 + (g_ % 2)

                    def pv(e, pr=pr, ek=ek, ob=ob, first=first, last=last, a=a):
                        for z, j in enumerate(pr):
                            ins = e.matmul(bank(ob), vaug[a][:, j, :], pt[ek][:, z * 512:(z + 1) * 512],
                                           start=(first and z == 0), stop=(last and z == len(pr) - 1))
                        return ins
                    s.op("pe", pv, reads=R_pt[ek] + [R_v[jj // 4] for jj in pr], writes=[R_b[ob]])
                    if last:
                        finalize_a(a, ob, rrow[g_ % 2], R_rrow[g_ % 2])
                        pend.append((idx + 2, (a, ob, 6 + (g_ % 2), sg, R_sg[m], m, rrow[g_ % 2], R_rrow[g_ % 2],
                                               tmp, R_tmp, och[ok], R_och[ok])))
                        gq[0] += 1

                a_qk(0)
                for idx in range(len(steps)):
                    a_exp(idx)
                    if idx + 1 < len(steps):
                        a_qk(idx + 1)
                    a_mask(idx)
                    a_pv(idx)
                    while pend and pend[0][0] <= idx:
                        finalize_b(*pend.pop(0)[1])
                while pend:
                    finalize_b(*pend.pop(0)[1])
                s.dma("sp", o_scr[:, hp, :], och[ok][:, :], reads=[R_och[ok]], writes=[R_oscr[hp]])
            s.barrier()

    def phase_gb(i):
        j_l = i // 2
        with contextlib.ExitStack() as st:
            al = lambda n, sh, dt: st.enter_context(nc.sbuf_tensor("s%d_%s" % (i, n), sh, dt))
            wst = al("wstb", [128, 2, 640], F32)
            R_wst = Res("wstb")
            wbf = [al("wbfb%d" % k, [128, 8, 704], BF16) for k in range(2)]
            R_wbf = [Res("wbfb%d" % k) for k in range(2)]
            qT = [al("qTb%d" % c, [128, S], BF16) for c in range(2)]
            kz = [al("kzb%d" % a, [128, S], BF16) for a in range(2)]
            R_kz = Res("kzinitb")
            s.op("dve", lambda e: e.memset(kz[0][64:128, :], 0.0), writes=[R_kz])
            s.op("dve", lambda e: e.memset(kz[1][0:64, :], 0.0), writes=[R_kz])
            sg = [al("sgb%d" % c, [128, S], BF16) for c in range(2)]
            R_q = [[Res("qb%d_%d" % (c, t)) for t in range(NT)] for c in range(2)]
            R_k = [Res("kb%d" % t) for t in range(NT)]
            R_sg = [[Res("sgb%d_%d" % (c, t)) for t in range(NT)] for c in range(2)]
            vaug = [al("vaugb%d" % a, [128, 32, 128], BF16) for a in range(2)]
            R_v = [Res("vb%d" % t) for t in range(NT)]
            cst = [al("cst%d" % k, [128, TT], F32) for k in range(2)]
            snt = [al("snt%d" % k, [128, TT], F32) for k in range(2)]
            R_cs = [Res("cs%d" % k) for k in range(2)]
            R_sn = [Res("sn%d" % k) for k in range(2)]
            sqb = [al("sqb%d" % k, [128, TT], BF16) for k in range(2)]
            R_sqb = [Res("sqb%d" % k) for k in range(2)]
            ub = [al("ub%d" % k, [128, TT], F32) for k in range(2)]
            R_ub = [Res("ub%d" % k) for k in range(2)]
            sdb = [al("sdb%d" % k, [128, TT], F32) for k in range(2)]
            R_sdb = [Res("sdb%d" % k) for k in range(2)]
            t1 = [al("t1b%d" % k, [128, TT], F32) for k in range(2)]
            R_t1 = [Res("t1b%d" % k) for k in range(2)]
            t2 = [al("t2b%d" % k, [128, TT], F32) for k in range(2)]
            R_t2 = [Res("t2b%d" % k) for k in range(2)]
            pt = [al("ptb%d" % k, [128, 1024], BF16) for k in range(3)]
            R_pt = [Res("ptb%d" % k) for k in range(3)]
            rrow = [(al("rlb%d" % k, [128, TT], F32), al("rbb%d" % k, [128, TT], BF16)) for k in range(2)]
            R_rrow = [Res("rrowb%d" % k) for k in range(2)]
            tmp = al("tmpb", [128, TT], F32)
            R_tmp = Res("tmpb")
            och = [al("ochb0", [128, S], BF16)] * 2
            R_och = [Res("ochb0")] * 2
            R_vinit = Res("vinitb")
            s.op("dve", lambda e: e.memset(vaug[0][:, :, 64:128], 1.0), writes=[R_vinit])
            s.op("dve", lambda e: e.memset(vaug[1][:, :, 0:64], 1.0), writes=[R_vinit])

            def load_w(g):
                k = g % 2
                wv = b_in[j_l][g].rearrange("p (c f) -> p c f", c=8)
                for hf in range(4):
                    cs_ = slice(2 * hf, 2 * hf + 2)
                    s.dma("sp", wst[:, :, :], wv[:, cs_, :], writes=[R_wst])
                    for (d0, d1, s0, s1) in ((0, 256, 0, 256), (256, 320, 256, 320), (320, 384, 256, 320), (384, 704, 320, 640)):
                        s.op("pool", lambda e, d0=d0, d1=d1, s0=s0, s1=s1, cs_=cs_: e.tensor_copy(
                            out=wbf[k][:, cs_, d0:d1], in_=wst[:, :, s0:s1]), reads=[R_wst], writes=[R_wbf[k]])

            cscount = [0]
            load_w(0)
            for g in range(4):
                wk = g % 2
                w = wbf[wk]
                for kind in range(3):
                    col = (0, 128, 256)[kind]
                    gcol = (G_BQ + j_l) if kind < 2 else (G_BK + j_l)
                    for tt in range(NT):
                        tsl = slice(tt * TT, (tt + 1) * TT)
                        ck = cscount[0] % 2
                        cscount[0] += 1
                        s.dma("sp", cst[ck][:, :], cos_in[:, tsl], writes=[R_cs[ck]])
                        s.dma("sp", snt[ck][:, :], sin_in[:, tsl], writes=[R_sn[ck]])
                        b = next_bank()

                        def mm(e, b=b, col=col, tsl=tsl, w=w):
                            for c in range(8):
                                ins = e.matmul(bank(b), w[:, c, col:col + 128], hT[:, c, tsl], start=(c == 0), stop=(c == 7))
                            return ins
                        s.op("pe", mm, reads=[R_wbf[wk]] + R_h[tt], writes=[R_b[b]])
                        s.op("act", lambda e, b=b, ck=ck: e.activation(out=sqb[ck][:, :], in_=bank(b), func=AF.Square),
                             reads=[R_b[b]], writes=[R_sqb[ck]])
                        s.op("act", lambda e, b=b, gcol=gcol, ck=ck: e.activation(out=ub[ck][:, :], in_=bank(b), func=AF.Copy,
                                                                                 scale=gv[:, gcol:gcol + 1]),
                             reads=[R_b[b], R_c], writes=[R_ub[ck]])
                        b2 = next_bank()
                        s.op("pe", lambda e, b2=b2, ck=ck: e.matmul(bank(b2), bones[:, :], sqb[ck][:, :], start=True, stop=True),
                             reads=[R_sqb[ck], R_c], writes=[R_b[b2]])
                        b3 = next_bank()
                        s.op("pe", lambda e, b3=b3, ck=ck: e.matmul(bank(b3), perm[:, :], ub[ck][:, :], start=True, stop=True),
                             reads=[R_ub[ck], R_c], writes=[R_b[b3]])
                        s.op("act", lambda e, b2=b2, ck=ck: e.activation(out=sdb[ck][:, :], in_=bank(b2), func=AF.Ln, scale=1.0 / 64,
                                                                          bias=epsc[:, 0:1]),
                             reads=[R_b[b2], R_c], writes=[R_sdb[ck]])
                        s.op("act", lambda e, ck=ck: e.activation(out=sdb[ck][:, :], in_=sdb[ck][:, :], func=AF.Exp, scale=-0.5),
                             reads=[R_sdb[ck]], writes=[R_sdb[ck]])
                        s.op("pool", lambda e, ck=ck: e.tensor_tensor(out=t1[ck][:, :], in0=ub[ck][:, :], in1=cst[ck][:, :], op=ALU.mult),
                             reads=[R_ub[ck], R_cs[ck]], writes=[R_t1[ck]])
                        s.op("dve", lambda e, b3=b3, ck=ck: e.tensor_tensor(out=t2[ck][:, :], in0=bank(b3), in1=snt[ck][:, :], op=ALU.mult),
                             reads=[R_b[b3], R_sn[ck]], writes=[R_t2[ck]])
                        s.op("pool", lambda e, ck=ck: e.tensor_tensor(out=t1[ck][:, :], in0=t1[ck][:, :], in1=t2[ck][:, :], op=ALU.add),
                             reads=[R_t1[ck], R_t2[ck]], writes=[R_t1[ck]])
                        if kind < 2:
                            dst, R_dst = qT[kind][:, tsl], R_q[kind][tt]
                            s.op("dve", lambda e, dst=dst, ck=ck: e.tensor_tensor(out=dst, in0=t1[ck][:, :], in1=sdb[ck][:, :], op=ALU.mult),
                                 reads=[R_t1[ck], R_sdb[ck]], writes=[R_dst])
                        else:
                            for a_ in range(2):
                                s.op("dve", lambda e, a_=a_, tsl=tsl, ck=ck: e.tensor_tensor(
                                    out=kz[a_][64 * a_:64 * a_ + 64, tsl], in0=t1[ck][64 * a_:64 * a_ + 64, :],
                                    in1=sdb[ck][64 * a_:64 * a_ + 64, :], op=ALU.mult),
                                    reads=[R_t1[ck], R_sdb[ck], R_kz], writes=[R_k[tt]])
                for tt in range(NT):
                    tsl = slice(tt * TT, (tt + 1) * TT)
                    for c2 in range(2):
                        col = 448 + 128 * c2
                        b = next_bank()

                        def mm(e, b=b, col=col, tsl=tsl, w=w):
                            for c in range(8):
                                ins = e.matmul(bank(b), w[:, c, col:col + 128], hT[:, c, tsl], start=(c == 0), stop=(c == 7))
                            return ins
                        s.op("pe", mm, reads=[R_wbf[wk]] + R_h[tt], writes=[R_b[b]])
                        s.op("act", lambda e, b=b, tsl=tsl, c2=c2: e.activation(out=sg[c2][:, tsl], in_=bank(b), func=AF.Silu),
                             reads=[R_b[b]], writes=[R_sg[c2][tt]])
                    b = next_bank()

                    def mmv(e, b=b, tt=tt, w=w):
                        for tcl in range(4):
                            t0 = tt * TT + tcl * 128
                            for c in range(8):
                                ins = e.matmul(bank(b, tcl * 64, (tcl + 1) * 64), hT[:, c, t0:t0 + 128], w[:, c, 384:448],
                                               start=(c == 0), stop=(c == 7))
                        return ins
                    s.op("pe", mmv, reads=[R_wbf[wk]] + R_h[tt], writes=[R_b[b]])
                    bv = bank(b, 0, 256).rearrange("p (t f) -> p t f", t=4)
                    s.op("dve", lambda e, bv=bv, tt=tt: e.tensor_copy(out=vaug[0][:, tt * 4:tt * 4 + 4, 0:64], in_=bv),
                         reads=[R_b[b], R_vinit], writes=[R_v[tt]])
                    s.op("dve", lambda e, bv=bv, tt=tt: e.tensor_copy(out=vaug[1][:, tt * 4:tt * 4 + 4, 64:128], in_=bv),
                         reads=[R_b[b], R_vinit], writes=[R_v[tt]])
                if g + 1 < 4:
                    load_w(g + 1)
                steps = [(hi, qt, kp) for hi in range(4) for qt in range(NT) for kp in range(16)]
                pend = []
                gq = [0]

                def b_qk(idx):
                    hi, qt, kp = steps[idx]
                    c2, a = hi // 2, hi % 2
                    lo = 64 * a
                    sb0 = 2 * (idx % 2)

                    def qk(e, kp=kp, sb0=sb0, qt=qt, lo=lo, c2=c2):
                        for z in range(2):
                            j = 2 * kp + z
                            ins = e.matmul(bank(sb0 + z), kz[lo // 64][:, j * 128:(j + 1) * 128],
                                           qT[c2][:, qt * TT:(qt + 1) * TT], start=True, stop=True)
                        return ins
                    s.op("pe", qk, reads=[R_k[kp // 2], R_q[c2][qt]], writes=[R_b[sb0], R_b[sb0 + 1]])

                def b_exp(idx):
                    sb0 = 2 * (idx % 2)
                    ek = idx % 3
                    s.op("act", lambda e, sb0=sb0, ek=ek: e.activation(
                        out=pt[ek][:, :], in_=ps[:, sb0 * 512:(sb0 + 2) * 512], func=AF.Exp, scale=0.125),
                        reads=[R_b[sb0], R_b[sb0 + 1]], writes=[R_pt[ek]])

                def b_pv(idx):
                    hi, qt, kp = steps[idx]
                    c2, a = hi // 2, hi % 2
                    ek = idx % 3
                    g_ = gq[0]
                    ob = 4 + (g_ % 2)

                    def pv(e, kp=kp, ek=ek, ob=ob, a=a):
                        for z in range(2):
                            j = 2 * kp + z
                            ins = e.matmul(bank(ob), vaug[a][:, j, :], pt[ek][:, z * 512:(z + 1) * 512],
                                           start=(kp == 0 and z == 0), stop=(kp == 15 and z == 1))
                        return ins
                    s.op("pe", pv, reads=[R_pt[ek], R_v[kp // 2]], writes=[R_b[ob]])
                    if kp == 15:
                        chunk = 2 * g + c2
                        ok = chunk % 2
                        finalize_a(a, ob, rrow[g_ % 2], R_rrow[g_ % 2])
                        pend.append((idx + 2, (a, ob, 6 + (g_ % 2), sg[c2], R_sg[c2][qt], qt, rrow[g_ % 2], R_rrow[g_ % 2],
                                               tmp, R_tmp, och[ok], R_och[ok]), (chunk, ok) if (a == 1 and qt == NT - 1) else None))
                        gq[0] += 1

                def b_fin(item):
                    finalize_b(*item[1])
                    if item[2] is not None:
                        chunk, ok = item[2]
                        s.dma("sp", o_scr[:, chunk, :], och[ok][:, :], reads=[R_och[ok]], writes=[R_oscr[chunk]])

                b_qk(0)
                for idx in range(len(steps)):
                    b_exp(idx)
                    if idx + 1 < len(steps):
                        b_qk(idx + 1)
                    b_pv(idx)
                    while pend and pend[0][0] <= idx:
                        b_fin(pend.pop(0))
                while pend:
                    b_fin(pend.pop(0))
            s.barrier()

    def phase_o(i, last):
        with contextlib.ExitStack() as st:
            al = lambda n, sh, dt: st.enter_context(nc.sbuf_tensor("s%d_%s" % (i, n), sh, dt))
            wst = [al("wsto%d" % k, [128, 2, 1024], F32) for k in range(2)]
            R_wst = [Res("wsto%d" % k) for k in range(2)]
            wo = al("wo", [128, 8, 1024], BF16)
            wg = al("wg", [128, 8, 1024], BF16)
            wp = al("wp", [128, 2, 1024], BF16)
            R_wo, R_wg, R_wp = Res("w_wo"), Res("w_wg"), Res("w_wp")
            xt = [al("xtO%d" % k, [128, 8, TT], F32) for k in range(2)]
            R_xt = [[Res("xtO%d_%d" % (k, c)) for c in range(8)] for k in range(2)]
            ot = [al("otO%d" % k, [128, 8, TT], BF16) for k in range(2)]
            R_ot = [Res("otO%d" % k) for k in range(2)]
            pin = [al("pin%d" % k, [128, 4, 256], F32) for k in range(2)]
            R_pin = [Res("pin%d" % k) for k in range(2)]
            pT = al("pT", [128, 2, TT], BF16)
            R_pT = Res("pT")
            sq = al("sqO", [128, 8, TT], BF16)
            R_sq = Res("sqO")
            sd = al("sdO", [128, TT], F32)
            R_sd = Res("sdO")
            n1 = sq
            R_n1 = R_sq
            gs = [al("gs%d" % k, [128, TT], F32) for k in range(2)]
            R_gs = [Res("gs%d" % k) for k in range(2)]
            if last:
                yo = al("yo", [128, 4, D], F32)
                R_yo = Res("yo")
            wcnt = 0
            for (wdst, wsrc, nch, R_w) in ((wo, wo_in[i], 8, R_wo), (wg, wg_in[i], 8, R_wg), (wp, wp_in[i], 2, R_wp)):
                wv = wsrc.rearrange("p (c f) -> p c f", c=nch)
                for c0 in range(0, nch, 2):
                    wk_ = wcnt % 2
                    wcnt += 1
                    s.dma("sp", wst[wk_][:, :, :], wv[:, c0:c0 + 2, :], writes=[R_wst[wk_]])
                    s.op("pool", lambda e, wdst=wdst, c0=c0, wk_=wk_: e.tensor_copy(out=wdst[:, c0:c0 + 2, :], in_=wst[wk_][:, :, :]),
                         reads=[R_wst[wk_]], writes=[R_w])
            pv_ = p_in[i].rearrange("(t b p) f -> t p b f", b=4, p=128)
            ov = out.rearrange("(t b p) d -> t p b d", b=4, p=128)

            def loads(tt):
                k = tt % 2
                tsl = slice(tt * TT, (tt + 1) * TT)
                s.dma("sp", xt[k][:, :, :], x_scr[:, :, tsl], reads=[R_xscr[tt]], writes=R_xt[k], sem_res=R_xt[k][0])
                s.dma("sp", ot[k][:, :, :], o_scr[:, :, tsl], reads=R_oscr, writes=[R_ot[k]])
                s.dma("sp", pin[k][:, :, :], pv_[tt], writes=[R_pin[k]])

            loads(0)
            for tt in range(NT):
                k = tt % 2
                tsl = slice(tt * TT, (tt + 1) * TT)
                if tt + 1 < NT:
                    loads(tt + 1)
                X = xt[k]
                for fc in range(8):
                    b = next_bank()

                    def mm(e, b=b, fc=fc, k=k):
                        for c in range(8):
                            ins = e.matmul(bank(b), wo[:, c, fc * 128:(fc + 1) * 128], ot[k][:, c, :], start=(c == 0), stop=(c == 7))
                        return ins
                    s.op("pe", mm, reads=[R_wo, R_ot[k]], writes=[R_b[b]])
                    s.op("dve", lambda e, b=b, fc=fc, X=X: e.tensor_tensor(out=X[:, fc, :], in0=X[:, fc, :], in1=bank(b), op=ALU.add),
                         reads=[R_b[b], R_xt[k][fc]], writes=[R_xt[k][fc]])
                emit_rstd(X, R_xt[k], sq, R_sq, sd, R_sd, 1.0 / D)
                for c in range(8):
                    eng = "dve"
                    s.op(eng, lambda e, c=c, X=X: e.scalar_tensor_tensor(
                        out=n1[:, c, :], in0=X[:, c, :], scalar=gv[:, g_ple(i) + c:g_ple(i) + c + 1], in1=sd[:, :],
                        op0=ALU.mult, op1=ALU.mult), reads=[R_xt[k][c], R_sd, R_c], writes=[R_n1])
                for c2 in range(2):
                    b = next_bank()

                    def tr(e, b=b, c2=c2, k=k):
                        for tb in range(4):
                            ins = e.transpose(out=bank(b, tb * 128, (tb + 1) * 128), in_=pin[k][:, tb, c2 * 128:(c2 + 1) * 128],
                                              identity=ident[:, :])
                        return ins
                    s.op("pe", tr, reads=[R_pin[k], R_c], writes=[R_b[b]])
                    s.op("act", lambda e, b=b, c2=c2: e.activation(out=pT[:, c2, :], in_=bank(b), func=AF.Copy),
                         reads=[R_b[b]], writes=[R_pT])
                for fc in range(8):
                    b = next_bank()

                    def mmg(e, b=b, fc=fc):
                        for c in range(8):
                            ins = e.matmul(bank(b), wg[:, c, fc * 128:(fc + 1) * 128], n1[:, c, :], start=(c == 0), stop=(c == 7))
                        return ins
                    s.op("pe", mmg, reads=[R_wg, R_n1], writes=[R_b[b]])
                    gk = fc % 2
                    s.op("act", lambda e, b=b, gk=gk: e.activation(out=gs[gk][:, :], in_=bank(b), func=AF.Sigmoid),
                         reads=[R_b[b]], writes=[R_gs[gk]])
                    b2 = next_bank()

                    def mmp(e, b2=b2, fc=fc):
                        for c in range(2):
                            ins = e.matmul(bank(b2), wp[:, c, fc * 128:(fc + 1) * 128], pT[:, c, :], start=(c == 0), stop=(c == 1))
                        return ins
                    s.op("pe", mmp, reads=[R_wp, R_pT], writes=[R_b[b2]])
                    s.op("dve", lambda e, b2=b2, gk=gk: e.tensor_tensor(out=gs[gk][:, :], in0=gs[gk][:, :], in1=bank(b2), op=ALU.mult),
                         reads=[R_gs[gk], R_b[b2]], writes=[R_gs[gk]])
                    s.op("pool", lambda e, fc=fc, X=X, gk=gk: e.tensor_tensor(out=X[:, fc, :], in0=X[:, fc, :], in1=gs[gk][:, :], op=ALU.add),
                         reads=[R_gs[gk], R_xt[k][fc]], writes=[R_xt[k][fc]])
                if dbg == ("x", i):
                    s.dma("sp", dbg_out[:, :, tsl], X[:, :, :], reads=R_xt[k], writes=[R_out[tt]])
                if not last:
                    s.dma("sp", x_scr[:, :, tsl], X[:, :, :], reads=R_xt[k], writes=[R_xscr[tt]])
                    emit_h(X, R_xt[k], sq, R_sq, sd, R_sd, tt, g_norm(i + 1))
                else:
                    emit_rstd(X, R_xt[k], sq, R_sq, sd, R_sd, 1.0 / D)
                    for c in range(8):
                        eng = "dve"
                        s.op(eng, lambda e, c=c, X=X: e.scalar_tensor_tensor(
                            out=X[:, c, :], in0=X[:, c, :], scalar=gv[:, G_FINAL + c:G_FINAL + c + 1], in1=sd[:, :],
                            op0=ALU.mult, op1=ALU.mult), reads=[R_xt[k][c], R_sd, R_c], writes=[R_xt[k][c]])
                    xn = X
                    R_xn = R_xt[k]
                    for tb in range(4):
                        for cq in range(2):
                            b = next_bank()

                            def tr(e, b=b, tb=tb, cq=cq, xn=xn):
                                for cc in range(4):
                                    c = cq * 4 + cc
                                    ins = e.transpose(out=bank(b, cc * 128, (cc + 1) * 128), in_=xn[:, c, tb * 128:(tb + 1) * 128],
                                                      identity=ident[:, :])
                                return ins
                            s.op("pe", tr, reads=R_xn + [R_c], writes=[R_b[b]])
                            if cq == 0:
                                s.op("dve", lambda e, b=b, tb=tb: e.tensor_copy(out=yo[:, tb, 0:512], in_=bank(b)),
                                     reads=[R_b[b]], writes=[R_yo])
                            else:
                                s.op("act", lambda e, b=b, tb=tb: e.activation(out=yo[:, tb, 512:1024], in_=bank(b), func=AF.Copy),
                                     reads=[R_b[b]], writes=[R_yo])
                    s.dma("sp", ov[tt], yo[:, :, :], reads=[R_yo], writes=[R_out[tt]])
            s.barrier()

    for i in range(n_layers):
        if dbg == ("h", i):
            s.dma("sp", dbg_out[:, :, :], hT[:, :, :], reads=[r for t in R_h for r in t], writes=[R_out[0]])
            break
        if i % 2 == 0:
            phase_ga(i)
        else:
            phase_gb(i)
        if dbg == ("o", i):
            s.dma("sp", dbg_out[:, :, :], o_scr[:, :, :], reads=R_oscr, writes=[R_out[0]])
            break
        phase_o(i, last=(i == n_layers - 1))
    s.barrier()
    s.emit()
    return nc


_CACHE = {}
_PADDED = ("cosT", "sinT", "a_in", "b_in", "wo", "wg", "wp", "tb")


def _prep_shared(inp):
    f = lambda a: np.ascontiguousarray(np.asarray(a, dtype=np.float32))
    sh = {}
    cosT, sinT, perm = _rope_tables()
    sh["cosT"], sh["sinT"], sh["perm"] = cosT, sinT, perm
    sh["ident"] = np.eye(128, dtype=np.float32)
    gv = np.zeros((128, 80), np.float32)
    ng, pg = f(inp["norm_g"]), f(inp["ple_norm_g"])
    for i in range(DEPTH):
        gv[:, 16 * i:16 * i + 8] = _vec_pc(ng[i])
        gv[:, 16 * i + 8:16 * i + 16] = _vec_pc(pg[i])
    gv[:, 64:72] = _vec_pc(f(inp["final_norm_g"]))
    bq, bk = f(inp["b_q_norm"]), f(inp["b_k_norm"])
    for j in range(2):
        gv[:, 72 + j] = np.tile(bq[j], 2)
        gv[:, 74 + j] = np.tile(bk[j], 2)
    sh["gv"] = gv
    awi, bwi = f(inp["a_w_in"]), f(inp["b_w_in"])
    for j in range(2):
        W = awi[j]
        groups = []
        for hp in range(8):
            cols = np.concatenate([W[:, k * 1024 + hp * 128:k * 1024 + (hp + 1) * 128] for k in range(4)], axis=1)
            groups.append(_pc(cols))
        sh["a_in%d" % j] = np.ascontiguousarray(np.stack(groups))
        W = bwi[j]
        groups = []
        for g in range(4):
            cols = np.concatenate([W[:, 256 * g:256 * (g + 1)], W[:, 1024 + 64 * g:1024 + 64 * (g + 1)],
                                   W[:, 1280 + 64 * g:1280 + 64 * (g + 1)], W[:, 1536 + 256 * g:1536 + 256 * (g + 1)]], axis=1)
            groups.append(_pc(cols))
        sh["b_in%d" % j] = np.ascontiguousarray(np.stack(groups))
        sh["tb%d" % j] = _a_tables(f(inp["a_rpb"])[j])
    awo, bwo = f(inp["a_w_out"]), f(inp["b_w_out"])
    wgt, wpj = f(inp["ple_w_gate"]), f(inp["ple_w_proj"])
    for i in range(DEPTH):
        sh["wo%d" % i] = _pc(awo[i // 2] if i % 2 == 0 else bwo[i // 2])
        sh["wg%d" % i] = _pc(wgt[i])
        sh["wp%d" % i] = _pc(wpj[i])
    return sh


def _core_inputs(sh, xb, pb, b):
    m = {}
    for k_, v_ in sh.items():
        if k_ in _PADDED or k_[:-1] in _PADDED:
            pad = np.full((1,) + v_.shape[1:], float(b), np.float32)
            m[k_] = np.concatenate([v_, pad], axis=0)
        else:
            m[k_] = v_
    m["x"] = np.ascontiguousarray(xb)
    m["p"] = np.ascontiguousarray(pb)
    return m


def kernel(x, p, norm_g, a_w_in, a_rpb, a_w_out, b_w_in, b_q_norm, b_k_norm, b_w_out,
           ple_norm_g, ple_w_gate, ple_w_proj, final_norm_g):
    inp = dict(norm_g=norm_g, a_w_in=a_w_in, a_rpb=a_rpb, a_w_out=a_w_out, b_w_in=b_w_in, b_q_norm=b_q_norm,
               b_k_norm=b_k_norm, b_w_out=b_w_out, ple_norm_g=ple_norm_g, ple_w_gate=ple_w_gate,
               ple_w_proj=ple_w_proj, final_norm_g=final_norm_g)
    sh = _prep_shared(inp)
    x = np.asarray(x, dtype=np.float32)
    p = np.asarray(p, dtype=np.float32)
    if "nc" not in _CACHE:
        _CACHE["nc"] = build()
    nc = _CACHE["nc"]
    in_maps = [_core_inputs(sh, x[b], p[:, b], b) for b in range(8)]
    res = run_bass_kernel_spmd(nc, in_maps, core_ids=list(range(8)))
    return np.stack([np.asarray(r["out"], dtype=np.float32) for r in res.results], axis=0)
```
